# Optimizing a Trainium2 kernel written in Bass

```python
import math
import jax, jax.numpy as jnp
from jax import lax
import numpy as np

D_MODEL = 1024
BATCH = 8
SEQ = 4096
DEPTH = 4

CHUNK = 64
N_MIXERS = 2
S5_GROUP = 16
S5_STATE = 64
S5_GROUPS = D_MODEL // S5_GROUP
FOX_HEAD_DIM = 64
FOX_HEADS = D_MODEL // FOX_HEAD_DIM
Q_BLOCK = 2 * CHUNK
MLP_HIDDEN = 4 * D_MODEL
RMS_EPS = 1e-6
N_S5_LAYERS = (DEPTH + N_MIXERS - 1) // N_MIXERS
N_FOX_LAYERS = DEPTH // N_MIXERS

kernel_name = "s5_fox_interleaved_sandwich_trunk"


def rms_norm(x, gain):
    xf = x.astype(jnp.float32)
    inv = lax.rsqrt(jnp.mean(xf * xf, axis=-1, keepdims=True) + RMS_EPS)
    return (xf * inv * gain.astype(jnp.float32)).astype(x.dtype)


def s5_mixer(u, a_re, a_im, log_dt, b_re, b_im, c_re, c_im, d_skip, w_glu):
    bsz, seq, dm = u.shape
    uf = u.astype(jnp.float32).reshape(bsz, seq, S5_GROUPS, S5_GROUP)
    a = lax.complex(a_re.astype(jnp.float32), a_im.astype(jnp.float32))
    dt = jnp.exp(log_dt.astype(jnp.float32))[:, None]
    a_bar = jnp.exp(a * dt)
    b_c = lax.complex(b_re.astype(jnp.float32), b_im.astype(jnp.float32))
    b_bar = ((a_bar - 1.0) / a)[..., None] * b_c
    bu = jnp.einsum('blgh,gph->blgp', uf.astype(jnp.complex64), b_bar)
    decay = jnp.broadcast_to(a_bar, (1, seq, S5_GROUPS, S5_STATE))

    def combine(e1, e2):
        a1, b1 = e1
        a2, b2 = e2
        return a1 * a2, a2 * b1 + b2

    _, states = lax.associative_scan(combine, (decay, bu), axis=1)
    c_c = lax.complex(c_re.astype(jnp.float32), c_im.astype(jnp.float32))
    y = jnp.real(jnp.einsum('blgp,ghp->blgh', states, c_c))
    y = y + d_skip.astype(jnp.float32).reshape(S5_GROUPS, S5_GROUP) * uf
    g = jax.nn.gelu(y.reshape(bsz, seq, dm))
    vg = g @ w_glu.astype(jnp.float32)
    val, gate = jnp.split(vg, 2, axis=-1)
    return (val * jax.nn.sigmoid(gate)).astype(u.dtype)


def fox_mixer(u, w_in, b_f, w_out):
    bsz, seq, dm = u.shape
    proj = u @ w_in
    q, k, v, g, f_logit = jnp.split(proj, [dm, 2 * dm, 3 * dm, 4 * dm], axis=-1)

    def heads(t):
        return t.reshape(bsz, seq, FOX_HEADS, FOX_HEAD_DIM).transpose(0, 2, 1, 3).astype(jnp.float32)

    q, k, v = heads(q), heads(k), heads(v)
    log_f = jax.nn.log_sigmoid((f_logit + b_f).astype(jnp.float32))
    cum = jnp.cumsum(log_f, axis=1).transpose(0, 2, 1)
    scale = FOX_HEAD_DIM ** -0.5
    outs = []
    for blk in range(seq // Q_BLOCK):
        s0, s1 = blk * Q_BLOCK, (blk + 1) * Q_BLOCK
        logits = jnp.einsum('bhqd,bhkd->bhqk', q[:, :, s0:s1], k[:, :, :s1]) * scale
        logits = logits + cum[:, :, s0:s1, None] - cum[:, :, None, :s1]
        causal = jnp.arange(s0, s1)[:, None] >= jnp.arange(s1)[None, :]
        probs = jax.nn.softmax(jnp.where(causal, logits, -jnp.inf), axis=-1)
        outs.append(jnp.einsum('bhqk,bhkd->bhqd', probs, v[:, :, :s1]))
    o = jnp.concatenate(outs, axis=2).transpose(0, 2, 1, 3).reshape(bsz, seq, dm)
    o = o * jax.nn.sigmoid(g.astype(jnp.float32))
    return o.astype(u.dtype) @ w_out


def sqrelu_mlp(u, w1, w2):
    h = jax.nn.relu(u @ w1)
    return (h * h) @ w2


def setup_inputs(seed: int = 0) -> dict:
    key = jax.random.key(seed)
    ks = jax.random.split(key, 16)
    f32 = jnp.float32
    ns, nf = N_S5_LAYERS, N_FOX_LAYERS
    g, p, h = S5_GROUPS, S5_STATE, S5_GROUP
    n_idx = jnp.arange(S5_STATE, dtype=f32)
    x = jax.random.normal(ks[0], (BATCH, SEQ, D_MODEL), f32)
    norm_gains = 1.0 + 0.05 * jax.random.normal(ks[1], (DEPTH, 4, D_MODEL), f32)
    s5_a_re = -0.5 * jnp.exp(0.02 * jax.random.normal(ks[2], (ns, g, p), f32))
    s5_a_im = jnp.pi * n_idx + 0.01 * jax.random.normal(ks[3], (ns, g, p), f32)
    s5_log_dt = jax.random.uniform(ks[4], (ns, g), f32, minval=math.log(1e-3), maxval=math.log(1e-1))
    s5_b_re = jax.random.normal(ks[5], (ns, g, p, h), f32) * (2 * h) ** -0.5
    s5_b_im = jax.random.normal(ks[6], (ns, g, p, h), f32) * (2 * h) ** -0.5
    s5_c_re = jax.random.normal(ks[7], (ns, g, h, p), f32) * (2 * p) ** -0.5
    s5_c_im = jax.random.normal(ks[8], (ns, g, h, p), f32) * (2 * p) ** -0.5
    s5_d = jax.random.normal(ks[9], (ns, D_MODEL), f32)
    s5_w_glu = jax.random.normal(ks[10], (ns, D_MODEL, 2 * D_MODEL), f32) * D_MODEL ** -0.5
    fox_w_in = jax.random.normal(ks[11], (nf, D_MODEL, 4 * D_MODEL + FOX_HEADS), f32) * D_MODEL ** -0.5
    fox_b_f = jax.random.uniform(ks[12], (nf, FOX_HEADS), f32, minval=1.0, maxval=6.0)
    fox_w_out = jax.random.normal(ks[13], (nf, D_MODEL, D_MODEL), f32) * D_MODEL ** -0.5
    mlp_w1 = jax.random.normal(ks[14], (DEPTH, D_MODEL, MLP_HIDDEN), f32) * D_MODEL ** -0.5
    mlp_w2 = jax.random.normal(ks[15], (DEPTH, MLP_HIDDEN, D_MODEL), f32) * MLP_HIDDEN ** -0.5
    return {"x": x, "norm_gains": norm_gains,
            "s5_a_re": s5_a_re, "s5_a_im": s5_a_im, "s5_log_dt": s5_log_dt,
            "s5_b_re": s5_b_re, "s5_b_im": s5_b_im, "s5_c_re": s5_c_re, "s5_c_im": s5_c_im,
            "s5_d": s5_d, "s5_w_glu": s5_w_glu,
            "fox_w_in": fox_w_in, "fox_b_f": fox_b_f, "fox_w_out": fox_w_out,
            "mlp_w1": mlp_w1, "mlp_w2": mlp_w2}


def reference(x, norm_gains, s5_a_re, s5_a_im, s5_log_dt, s5_b_re, s5_b_im, s5_c_re, s5_c_im,
              s5_d, s5_w_glu, fox_w_in, fox_b_f, fox_w_out, mlp_w1, mlp_w2):
    assert x.shape[1] % Q_BLOCK == 0 and Q_BLOCK % CHUNK == 0
    h = x
    for i in range(DEPTH):
        gains = norm_gains[i]
        u = rms_norm(h, gains[0])
        j = i // N_MIXERS
        if i % N_MIXERS == 0:
            mix = s5_mixer(u, s5_a_re[j], s5_a_im[j], s5_log_dt[j], s5_b_re[j], s5_b_im[j],
                           s5_c_re[j], s5_c_im[j], s5_d[j], s5_w_glu[j])
        else:
            mix = fox_mixer(u, fox_w_in[j], fox_b_f[j], fox_w_out[j])
        h = h + rms_norm(mix, gains[1])
        h = h + rms_norm(sqrelu_mlp(rms_norm(h, gains[2]), mlp_w1[i], mlp_w2[i]), gains[3])
    return h
```

```python
import numpy as np
from contextlib import ExitStack
import concourse.bass as bass
import concourse.mybir as mybir
from concourse.bass_utils import run_bass_kernel_spmd

F32 = mybir.dt.float32
BF16 = mybir.dt.bfloat16
AF = mybir.ActivationFunctionType
ALU = mybir.AluOpType

D = 1024
L = 4096
DEPTH = 4
HID = 4096
EPS = 1e-6
NCORES = 8


class Tok:
    __slots__ = ("name", "w", "r", "ds")

    def __init__(self, name, ds=None):
        self.name = name
        self.w = None
        self.r = {}
        self.ds = ds


class DSem:
    def __init__(self, sem):
        self.sem = sem
        self.cnt = 0


class Eng:
    def __init__(self, name, obj, sem):
        self.name = name
        self.obj = obj
        self.sem = sem
        self.cnt = 0
        self.seen = {}


class K:
    def __init__(self, nc):
        self.nc = nc
        self.engs = {}
        for name, obj in (("pe", nc.tensor), ("act", nc.scalar), ("dve", nc.vector),
                          ("pool", nc.gpsimd), ("sp", nc.sync)):
            self.engs[name] = Eng(name, obj, nc.alloc_semaphore("s_" + name))
        self.dsems = {}
        self.uid = 0

    def tok(self, name, dma=False):
        t = Tok(name)
        if dma:
            if name not in self.dsems:
                self.dsems[name] = DSem(self.nc.alloc_semaphore("d_" + name))
            t.ds = self.dsems[name]
        return t

    def _wait(self, e, ev):
        sem, val = ev
        if sem is e.sem and e.name == "pe":
            return
        key = id(sem)
        if e.seen.get(key, 0) >= val:
            return
        e.obj.wait_ge(sem, val)
        e.seen[key] = val

    def _deps(self, e, reads, writes):
        for t in reads:
            if t.w is not None:
                self._wait(e, t.w)
        for t in writes:
            if t.w is not None:
                self._wait(e, t.w)
            for ev in t.r.values():
                self._wait(e, ev)

    def _reg(self, ev, reads, writes):
        for t in reads:
            t.r[id(ev[0])] = ev
        for t in writes:
            t.w = ev
            t.r = {}

    def op(self, en, fn, reads=(), writes=()):
        e = self.engs[en]
        self._deps(e, reads, writes)
        inst = fn(e.obj)
        e.cnt += 1
        inst.then_inc(e.sem, 1)
        self._reg((e.sem, e.cnt), reads, writes)
        return inst

    def mm_group(self, out_ap, pairs, reads=(), writes=(), **kw):
        e = self.engs["pe"]
        self._deps(e, reads, writes)
        n = len(pairs)
        inst = None
        for i, (lhsT, rhs) in enumerate(pairs):
            inst = e.obj.matmul(out_ap, lhsT, rhs, start=(i == 0), stop=(i == n - 1), **kw)
        e.cnt += 1
        inst.then_inc(e.sem, 1)
        self._reg((e.sem, e.cnt), reads, writes)

    def dma(self, qn, out, in_, reads=(), writes=(), st=None, **kw):
        e = self.engs[qn]
        self._deps(e, reads, writes)
        inst = e.obj.dma_start(out=out, in_=in_, **kw)
        ds = st.ds
        ds.cnt += 16
        inst.then_inc(ds.sem, 16)
        self._reg((ds.sem, ds.cnt), reads, writes)

    def barrier(self):
        comp = [self.engs[n] for n in ("pe", "act", "dve", "pool")]
        for e in self.engs.values():
            for e2 in comp:
                if e2 is not e and e2.cnt > 0:
                    self._wait(e, (e2.sem, e2.cnt))
            for ds in self.dsems.values():
                if ds.cnt > 0:
                    self._wait(e, (ds.sem, ds.cnt))


class Stream:
    def __init__(self, kk, bufs, toks, loads):
        self.kk, self.bufs, self.toks, self.loads = kk, bufs, toks, loads
        self.nxt = 0

    def get(self, i):
        nb = len(self.bufs)
        while self.nxt < len(self.loads) and self.nxt <= i + nb - 1:
            j = self.nxt
            self.loads[j](self.bufs[j % nb], self.toks[j % nb])
            self.nxt += 1
        return self.bufs[i % nb], self.toks[i % nb]


class Prog:
    def __init__(self, phases):
        self.phases = phases
        nc = self.nc = bass.Bass("TRN2", target_bir_lowering=False)
        self.kk = K(nc)
        dt = nc.dram_tensor
        self.x = dt("x", [D, L], F32, kind="ExternalInput").ap()
        self.out = dt("out", [D, L], F32, kind="ExternalOutput").ap()
        self.gcol_d = dt("gcol", [128, DEPTH * 4 * 8], F32, kind="ExternalInput").ap()
        self.w1_d = dt("w1", [DEPTH, D * HID // 2048, 2048], F32, kind="ExternalInput").ap()
        self.w2_d = dt("w2", [DEPTH, D * HID // 2048, 2048], F32, kind="ExternalInput").ap()
        self.w1_b = dt("w1b", [DEPTH, D * HID // 2048, 2048], BF16, kind="Internal").ap()
        self.w2_b = dt("w2b", [DEPTH, 8, 128, 4096], BF16, kind="Internal").ap()
        NS = 2
        self.s5_at_re = dt("s5_at_re", [NS, 128, 32], F32, kind="ExternalInput").ap()
        self.s5_at_im = dt("s5_at_im", [NS, 128, 32], F32, kind="ExternalInput").ap()
        self.s5_ldt = dt("s5_ldt", [NS, 128, 32], F32, kind="ExternalInput").ap()
        self.s5_b_re = dt("s5_bre", [NS, 128, 512], F32, kind="ExternalInput").ap()
        self.s5_b_im = dt("s5_bim", [NS, 128, 512], F32, kind="ExternalInput").ap()
        self.s5_c_re = dt("s5_cre", [NS, 128, 512], F32, kind="ExternalInput").ap()
        self.s5_c_im = dt("s5_cim", [NS, 128, 512], F32, kind="ExternalInput").ap()
        self.s5_dcol = dt("s5_dcol", [NS, 128, 8], F32, kind="ExternalInput").ap()
        self.wglu_d = dt("wglu", [NS, 1024, 2048], F32, kind="ExternalInput").ap()
        self.wglu_b = dt("wglub", [NS, 1024, 2048], BF16, kind="Internal").ap()
        NF = 2
        self.win_d = dt("win", [NF, 16, 1024, 256], F32, kind="ExternalInput").ap()
        self.win_b = dt("winb", [NF, 16, 1024, 256], BF16, kind="Internal").ap()
        self.wf_d = dt("wf", [NF, 1024, 16], F32, kind="ExternalInput").ap()
        self.wf_b = dt("wfb", [NF, 1024, 16], BF16, kind="Internal").ap()
        self.bf_d = dt("bf", [NF, 16, 1], F32, kind="ExternalInput").ap()
        self.wout_d = dt("wout", [NF, 1024, 1024], F32, kind="ExternalInput").ap()
        self.wout_b = dt("woutb", [NF, 1024, 1024], BF16, kind="Internal").ap()
        self.tri_d = dt("tri", [128, 128], F32, kind="ExternalInput").ap()
        self.cumq = dt("cumq", [16, 3, L], BF16, kind="Internal").ap()
        self.o_d = dt("o_d", [D, L], BF16, kind="Internal").ap()
        self.ps = [nc.alloc_psum_tensor(f"ps{i}", [128, 512], F32) for i in range(8)]
        self.pst = [self.kk.tok(f"ps{i}") for i in range(8)]
        self.ones = nc.alloc_sbuf_tensor("ones", [128, 128], BF16)
        self.gcol = nc.alloc_sbuf_tensor("gcol_sb", [128, DEPTH * 4 * 8], F32)
        self.epsc = nc.alloc_sbuf_tensor("epsc", [128, 1], F32)
        self.hpic = nc.alloc_sbuf_tensor("hpic", [128, 1], F32)
        self.onec = nc.alloc_sbuf_tensor("onec", [128, 1], F32)
        self.ident = nc.alloc_sbuf_tensor("ident_sb", [128, 128], F32)
        self.ident_d = dt("ident", [128, 128], F32, kind="ExternalInput").ap()
        self.cast_tok = {}

    def build(self):
        kk, nc = self.kk, self.nc
        t_const = kk.tok("const", dma=True)
        kk.op("dve", lambda e: e.memset(self.ones[:], 1.0), writes=[t_const])
        kk.op("dve", lambda e: e.memset(self.epsc[:], EPS), writes=[t_const])
        kk.op("dve", lambda e: e.memset(self.hpic[:], float(np.pi / 2)), writes=[t_const])
        kk.op("dve", lambda e: e.memset(self.onec[:], 1.0), writes=[t_const])
        kk.dma("sp", self.gcol[:], self.gcol_d, writes=[t_const], st=t_const)
        kk.dma("sp", self.ident[:], self.ident_d, writes=[t_const], st=t_const)
        self.t_const = t_const
        layers = sorted({l for (_, l) in self.phases})
        for l in layers:
            w2cast = self.w2_b.rearrange("l c p (a m) -> l (c p a) m", a=2)
            for nm, src, dst in (("w1", self.w1_d, self.w1_b), ("w2", self.w2_d, w2cast)):
                if any(p == "mlp" and pl == l for p, pl in self.phases):
                    t = kk.tok(f"cast_{nm}{l}", dma=True)
                    kk.dma("pool", dst[l], src[l], writes=[t], st=t)
                    self.cast_tok[(nm, l)] = t
            if any(p == "s5" and pl == l for p, pl in self.phases):
                t = kk.tok(f"cast_glu{l}", dma=True)
                kk.dma("pool", self.wglu_b[l // 2], self.wglu_d[l // 2], writes=[t], st=t)
                self.cast_tok[("glu", l)] = t
            if any(p == "fox" and pl == l for p, pl in self.phases):
                jf = l // 2
                for nm, src, dst in (("win", self.win_d[jf].rearrange("h (r a) n -> (h r) (a n)", a=8),
                                      self.win_b[jf].rearrange("h (r a) n -> (h r) (a n)", a=8)),
                                     ("wf", self.wf_d[jf].rearrange("(r a) n -> r (a n)", a=128),
                                      self.wf_b[jf].rearrange("(r a) n -> r (a n)", a=128)),
                                     ("wout", self.wout_d[jf].rearrange("(r a) n -> r (a n)", a=2),
                                      self.wout_b[jf].rearrange("(r a) n -> r (a n)", a=2))):
                    t = kk.tok(f"cast_{nm}{l}", dma=True)
                    kk.dma("pool", dst, src, writes=[t], st=t)
                    self.cast_tok[(nm, l)] = t
        kk.barrier()
        cur = self.x
        for (p, l) in self.phases:
            if p == "mlp":
                self.mlp_phase(l, cur, self.out)
            elif p == "s5":
                self.s5_phase(l, cur, self.out)
            elif p == "fox":
                self.fox_phase(l, cur, self.out)
            cur = self.out
            kk.barrier()
        return nc

    def rstd_from_ss(self, ss_ps, ss_tok, tmp, tmp_tok, rstd, rstd_tok):
        kk = self.kk
        kk.op("act", lambda e: e.activation(out=tmp, in_=ss_ps, func=AF.Sqrt, bias=self.epsc[:], scale=1.0 / D),
              reads=[ss_tok, self.t_const], writes=[tmp_tok])
        kk.op("dve", lambda e: e.reciprocal(out=rstd, in_=tmp), reads=[tmp_tok], writes=[rstd_tok])

    def s5_phase(self, l, hsrc, hdst):
        kk, nc = self.kk, self.nc
        j5 = l // 2
        TT = 512
        NT = L // TT
        g0 = (l * 4 + 0) * 8
        g1 = (l * 4 + 1) * 8
        ps, pst = self.ps, self.pst
        tc = self.t_const
        with ExitStack() as es0:
            ufm = es0.enter_context(nc.sbuf_tensor(f"s_u_{l}", [128, 8, L], BF16))
            ut = [[kk.tok(f"s_u{k}_{n}") for n in range(NT)] for k in range(8)]
            with ExitStack() as es:
                def A(name, shape, dtp):
                    return es.enter_context(nc.sbuf_tensor(f"{name}_{l}", shape, dtp))
                hb = [A(f"s1_h{i}", [128, 8, TT], F32) for i in range(2)]
                hbt = [kk.tok(f"s1_h{i}", dma=True) for i in range(2)]
                sq = [A(f"s1_sq{i}", [128, 8, TT], BF16) for i in range(2)]
                sqt = [kk.tok(f"s1_sq{i}") for i in range(2)]
                tmp = A("s1_tmp", [128, TT], F32)
                tmpt = kk.tok("s1_tmp")
                rstd = [A(f"s1_rstd{i}", [128, TT], F32) for i in range(2)]
                rstdt = [kk.tok(f"s1_rstd{i}") for i in range(2)]
                for i in range(NT):
                    b = i % 2
                    kk.dma("sp", hb[b][:], hsrc[:, i * TT:(i + 1) * TT].rearrange("(k p) t -> p k t", p=128),
                           writes=[hbt[b]], st=hbt[b])
                    kk.op("act", lambda e: e.activation(out=sq[b][:], in_=hb[b][:], func=AF.Square),
                          reads=[hbt[b]], writes=[sqt[b]])
                    kk.mm_group(ps[6 + b][:], [(self.ones[:], sq[b][:, k, :]) for k in range(8)],
                                reads=[sqt[b], tc], writes=[pst[6 + b]])
                    self.rstd_from_ss(ps[6 + b][:], pst[6 + b], tmp[:], tmpt, rstd[b][:], rstdt[b])
                    for k in range(8):
                        kk.op("dve", lambda e, k=k: e.scalar_tensor_tensor(
                            out=ufm[:, k, i * TT:(i + 1) * TT], in0=hb[b][:, k, :],
                            scalar=self.gcol[:, g0 + k:g0 + k + 1], in1=rstd[b][:], op0=ALU.mult, op1=ALU.mult),
                            reads=[hbt[b], rstdt[b], tc], writes=[ut[k][i]])
            kk.barrier()
            import os
            S5STOP = os.environ.get("S5STOP", "")
            if S5STOP == "p1":
                return
            with ExitStack() as es:
                def A(name, shape, dtp):
                    return es.enter_context(nc.sbuf_tensor(f"{name}_{l}", shape, dtp))
                NSL = 48
                sc = A("s2_sc", [128, NSL, 32], F32)
                SC = {}

                def S(name):
                    if name not in SC:
                        assert len(SC) < NSL
                        SC[name] = (len(SC), kk.tok("sc_" + name, dma=name in ("atre", "atim", "ldt")))
                    i_, t_ = SC[name]
                    return sc[:, i_, :], t_

                def tt(o, a, b, op):
                    (oa, ot), (aa, at), (ba, bt) = S(o), S(a), S(b)
                    kk.op("dve", lambda e: e.tensor_tensor(out=oa, in0=aa, in1=ba, op=op), reads=[at, bt], writes=[ot])

                def tsc(o, a, s1, op0, s2=None, op1=None):
                    (oa, ot), (aa, at) = S(o), S(a)
                    if s2 is None:
                        kk.op("dve", lambda e: e.tensor_scalar(out=oa, in0=aa, scalar1=s1, scalar2=None, op0=op0),
                              reads=[at], writes=[ot])
                    else:
                        kk.op("dve", lambda e: e.tensor_scalar(out=oa, in0=aa, scalar1=s1, scalar2=s2, op0=op0, op1=op1),
                              reads=[at], writes=[ot])

                def act(o, a, func, **kw):
                    (oa, ot), (aa, at) = S(o), S(a)
                    kk.op("act", lambda e: e.activation(out=oa, in_=aa, func=func, **kw), reads=[at, tc], writes=[ot])

                for nm, src in (("atre", self.s5_at_re), ("atim", self.s5_at_im), ("ldt", self.s5_ldt)):
                    oa, ot = S(nm)
                    kk.dma("sp", oa, src[j5], writes=[ot], st=ot)
                act("dt", "ldt", AF.Exp)
                tt("lr", "atre", "dt", ALU.mult)
                tt("th", "atim", "dt", ALU.mult)
                act("rho", "lr", AF.Exp)
                act("s_0", "th", AF.Sin, scale=1.0 / 16)
                act("c_0", "th", AF.Sin, scale=1.0 / 16, bias=self.hpic[:])

                def csq(ci, si, co, so):
                    tt("cc", ci, ci, ALU.mult)
                    tt("ss", si, si, ALU.mult)
                    tt("cs", ci, si, ALU.mult)
                    tt(co, "cc", "ss", ALU.subtract)
                    tsc(so, "cs", 2.0, ALU.mult)
                csq("c_0", "s_0", "c_1", "s_1")
                csq("c_1", "s_1", "c_0", "s_0")
                csq("c_0", "s_0", "c_1", "s_1")
                csq("c_1", "s_1", "ec0", "es0")
                for k in range(1, 10):
                    csq(f"ec{k - 1}", f"es{k - 1}", f"ec{k}", f"es{k}")
                tt("abr", "rho", "ec0", ALU.mult)
                tt("abi", "rho", "es0", ALU.mult)
                tsc("am1", "abr", -1.0, ALU.add)
                tt("den", "atre", "atre", ALU.mult)
                tt("d2", "atim", "atim", ALU.mult)
                tt("den", "den", "d2", ALU.add)
                (oa, ot), (aa, at) = S("rden"), S("den")
                kk.op("dve", lambda e: e.reciprocal(out=oa, in_=aa), reads=[at], writes=[ot])
                tt("nr", "am1", "atre", ALU.mult)
                tt("t", "abi", "atim", ALU.mult)
                tt("nr", "nr", "t", ALU.add)
                tt("ni", "abi", "atre", ALU.mult)
                tt("t", "am1", "atim", ALU.mult)
                tt("ni", "ni", "t", ALU.subtract)
                tt("kr", "nr", "rden", ALU.mult)
                tt("ki", "ni", "rden", ALU.mult)

                def col(name, q):
                    i_, t_ = SC[name]
                    return sc[:, i_, q:q + 1], t_
                if S5STOP == "sc":
                    return

                bre = A("s2_bre", [128, 32, 16], F32)
                bim = A("s2_bim", [128, 32, 16], F32)
                cre = A("s2_cre", [128, 32, 16], F32)
                cim = A("s2_cim", [128, 32, 16], F32)
                dcol = A("s2_dcol", [128, 8], F32)
                pt = kk.tok("s2_par", dma=True)
                for dst, src in ((bre, self.s5_b_re), (bim, self.s5_b_im), (cre, self.s5_c_re), (cim, self.s5_c_im)):
                    kk.dma("sp", dst[:].rearrange("p q h -> p (q h)"), src[j5], writes=[pt], st=pt)
                kk.dma("sp", dcol[:], self.s5_dcol[j5], writes=[pt], st=pt)
                wtmp = [[A(f"s2_wt{c}{m}", [128, 128], F32) for m in range(4)] for c in range(2)]
                wtmpt = [[kk.tok(f"s2_wt{c}{m}") for m in range(4)] for c in range(2)]
                WB = [[A(f"s2_wb{c}{m}", [128, 128], BF16) for m in range(4)] for c in range(2)]
                WBt = [[kk.tok(f"s2_wb{c}{m}") for m in range(4)] for c in range(2)]
                CW = [[A(f"s2_cw{c}{m}", [128, 128], BF16) for m in range(4)] for c in range(2)]
                CWt = [[kk.tok(f"s2_cw{c}{m}") for m in range(4)] for c in range(2)]
                t16 = [A(f"s2_t16{i}", [128, 16], F32) for i in range(2)]
                t16t = [kk.tok(f"s2_t16{i}") for i in range(2)]
                tabc = [A(f"s2_tc{m}", [128, 512], F32) for m in range(4)]
                tabs = [A(f"s2_ts{m}", [128, 512], F32) for m in range(4)]
                tabt = [kk.tok(f"s2_tab{m}") for m in range(4)]
                tw = [A(f"s2_tw{i}", [128, 256], F32) for i in range(2)]
                twt = [kk.tok(f"s2_tw{i}") for i in range(2)]
                rbc = [A(f"s2_rb{m}", [128, 512], F32) for m in range(4)]
                rbt = [kk.tok(f"s2_rb{m}") for m in range(4)]
                onesf = A("s2_onesf", [128, 512], F32)
                ini = A("s2_ini", [128, 4, 2], F32)
                init = [kk.tok(f"s2_ini{m}") for m in range(4)]
                itmp = A("s2_itmp", [128, 4, 2], F32)
                T = [[A(f"s2_T{pb}{i}", [128, 512], F32) for i in range(8)] for pb in range(2)]
                Tt = [[kk.tok(f"s2_T{pb}{i}") for i in range(8)] for pb in range(2)]
                U = [A(f"s2_U{i}", [128, 512], F32) for i in range(4)]
                Ut = [kk.tok(f"s2_U{i}") for i in range(4)]
                xb = [[A(f"s2_x{pb}{c}", [128, 512], BF16) for c in range(2)] for pb in range(2)]
                xbt = [[kk.tok(f"s2_x{pb}{c}") for c in range(2)] for pb in range(2)]
                yv = [A(f"s2_yv{i}", [128, 512], F32) for i in range(2)]
                yvt = [kk.tok(f"s2_yv{i}") for i in range(2)]
                for c in range(2):
                    for m in range(4):
                        kk.op("pool", lambda e, c=c, m=m: e.memset(wtmp[c][m][:], 0.0), writes=[wtmpt[c][m]])
                        kk.op("pool", lambda e, c=c, m=m: e.memset(CW[c][m][:], 0.0), writes=[CWt[c][m]])
                for m in range(4):
                    kk.op("pool", lambda e, m=m: e.memset(tabc[m][:, 0:1], 1.0), writes=[tabt[m]])
                    kk.op("pool", lambda e, m=m: e.memset(tabs[m][:, 0:1], 0.0), writes=[tabt[m]])
                kk.op("pool", lambda e: e.memset(onesf[:], 1.0), writes=[tc])

                pcount = 0
                for k in range(8):
                    for m in range(4):
                        q = 4 * k + m
                        (kr, krt), (ki, kit) = col("kr", q), col("ki", q)
                        kk.op("dve", lambda e: e.tensor_scalar(out=t16[0][:], in0=bim[:, q, :], scalar1=ki, scalar2=None,
                                                               op0=ALU.mult), reads=[pt, kit], writes=[t16t[0]])
                        kk.op("dve", lambda e: e.tensor_scalar(out=t16[1][:], in0=bim[:, q, :], scalar1=kr, scalar2=None,
                                                               op0=ALU.mult), reads=[pt, krt], writes=[t16t[1]])
                        for i2 in range(2):
                            r0, r1 = 64 * i2, 64 * i2 + 64
                            c0 = 32 * m + 16 * i2
                            kk.op("dve", lambda e: e.scalar_tensor_tensor(
                                out=wtmp[0][m][r0:r1, c0:c0 + 16], in0=bre[r0:r1, q, :], scalar=kr[r0:r1, :],
                                in1=t16[0][r0:r1, :], op0=ALU.mult, op1=ALU.subtract),
                                reads=[pt, krt, t16t[0]], writes=[wtmpt[0][m]])
                            kk.op("dve", lambda e: e.scalar_tensor_tensor(
                                out=wtmp[1][m][r0:r1, c0:c0 + 16], in0=bre[r0:r1, q, :], scalar=ki[r0:r1, :],
                                in1=t16[1][r0:r1, :], op0=ALU.mult, op1=ALU.add),
                                reads=[pt, kit, t16t[1]], writes=[wtmpt[1][m]])
                            kk.op("pool", lambda e: e.tensor_copy(out=CW[0][m][r0:r1, c0:c0 + 16], in_=cre[r0:r1, q, :]),
                                  reads=[pt], writes=[CWt[0][m]])
                            kk.op("pool", lambda e: e.tensor_scalar(out=CW[1][m][r0:r1, c0:c0 + 16], in0=cim[r0:r1, q, :],
                                                                    scalar1=-1.0, scalar2=None, op0=ALU.mult),
                                  reads=[pt], writes=[CWt[1][m]])
                        if S5STOP == "wg1":
                            return
                        for c in range(2):
                            kk.op("pe", lambda e: e.transpose(out=ps[6][:, 0:128], in_=wtmp[c][m][:], identity=self.ident[:]),
                                  reads=[wtmpt[c][m], tc], writes=[pst[6]])
                            kk.op("act", lambda e: e.activation(out=WB[c][m][:], in_=ps[6][:, 0:128], func=AF.Copy),
                                  reads=[pst[6]], writes=[WBt[c][m]])
                        if S5STOP == "wg2":
                            return
                        for lv in range(9):
                            w = 1 << lv
                            (ec, ect), (esn, est) = col(f"ec{lv}", q), col(f"es{lv}", q)
                            kk.op("dve", lambda e: e.tensor_scalar(out=tw[0][:, 0:w], in0=tabs[m][:, 0:w], scalar1=esn,
                                                                   scalar2=None, op0=ALU.mult),
                                  reads=[tabt[m], est], writes=[twt[0]])
                            kk.op("dve", lambda e: e.tensor_scalar(out=tw[1][:, 0:w], in0=tabs[m][:, 0:w], scalar1=ec,
                                                                   scalar2=None, op0=ALU.mult),
                                  reads=[tabt[m], ect], writes=[twt[1]])
                            kk.op("dve", lambda e: e.scalar_tensor_tensor(
                                out=tabs[m][:, w:2 * w], in0=tabc[m][:, 0:w], scalar=esn, in1=tw[1][:, 0:w],
                                op0=ALU.mult, op1=ALU.add), reads=[twt[1], est], writes=[tabt[m]])
                            kk.op("dve", lambda e: e.scalar_tensor_tensor(
                                out=tabc[m][:, w:2 * w], in0=tabc[m][:, 0:w], scalar=ec, in1=tw[0][:, 0:w],
                                op0=ALU.mult, op1=ALU.subtract), reads=[twt[0], ect], writes=[tabt[m]])
                        if S5STOP == "wg3":
                            return
                        (rh, rht) = col("rho", q)
                        kk.op("dve", lambda e: e.tensor_scalar(out=rbc[m][:], in0=onesf[:], scalar1=rh, scalar2=None,
                                                               op0=ALU.mult), reads=[rht, tc], writes=[rbt[m]])
                        if S5STOP == "wg4":
                            return
                        kk.op("dve", lambda e: e.memset(ini[:, m, :], 0.0), writes=[init[m]])
                        if S5STOP == "m0":
                            return
                        if S5STOP in ("m1", "m2", "m3") and m == int(S5STOP[1]):
                            return
                    if S5STOP == "wgen":
                        return
                    items = [(n, m) for n in range(NT) for m in range(4)]

                    def stA(it, n, m):
                        cs_ = slice(n * TT, (n + 1) * TT)
                        pb = it % 2
                        pa, pbk = (0, 1) if pb == 0 else (2, 3)
                        Tb, Ttb = T[pb], Tt[pb]
                        kk.mm_group(ps[pa][:], [(WB[0][m][:], ufm[:, k, cs_])], reads=[WBt[0][m], ut[k][n]],
                                    writes=[pst[pa]])
                        kk.mm_group(ps[pbk][:], [(WB[1][m][:], ufm[:, k, cs_])], reads=[WBt[1][m], ut[k][n]],
                                    writes=[pst[pbk]])
                        for (ti, pp, tab) in ((0, pa, tabc), (1, pbk, tabs), (2, pbk, tabc), (3, pa, tabs)):
                            kk.op("dve", lambda e, ti=ti, pp=pp, tab=tab: e.tensor_tensor(
                                out=Tb[ti][:], in0=ps[pp][:], in1=tab[m][:], op=ALU.mult),
                                reads=[pst[pp], tabt[m]], writes=[Ttb[ti]])

                    def stB(it, n, m):
                        Tb, Ttb = T[it % 2], Tt[it % 2]
                        kk.op("dve", lambda e: e.tensor_tensor(out=Tb[4][:], in0=Tb[0][:], in1=Tb[1][:], op=ALU.add),
                              reads=[Ttb[0], Ttb[1]], writes=[Ttb[4]])
                        kk.op("dve", lambda e: e.tensor_tensor(out=Tb[5][:], in0=Tb[2][:], in1=Tb[3][:], op=ALU.subtract),
                              reads=[Ttb[2], Ttb[3]], writes=[Ttb[5]])

                    def stC(it, n, m):
                        q = 4 * k + m
                        Tb, Ttb = T[it % 2], Tt[it % 2]
                        kk.op("dve", lambda e: e.tensor_tensor_scan(out=Tb[6][:], data0=rbc[m][:], data1=Tb[4][:],
                                                                    initial=ini[:, m, 0:1], op0=ALU.mult, op1=ALU.add),
                              reads=[rbt[m], Ttb[4], init[m]], writes=[Ttb[6]])
                        kk.op("dve", lambda e: e.tensor_tensor_scan(out=Tb[7][:], data0=rbc[m][:], data1=Tb[5][:],
                                                                    initial=ini[:, m, 1:2], op0=ALU.mult, op1=ALU.add),
                              reads=[rbt[m], Ttb[5], init[m]], writes=[Ttb[7]])
                        (e9c, e9ct), (e9s, e9st) = col("ec9", q), col("es9", q)
                        kk.op("dve", lambda e: e.tensor_scalar(out=itmp[:, m, 0:1], in0=Tb[7][:, TT - 1:TT], scalar1=e9s,
                                                               scalar2=None, op0=ALU.mult),
                              reads=[Ttb[7], e9st], writes=[init[m]])
                        kk.op("dve", lambda e: e.tensor_scalar(out=itmp[:, m, 1:2], in0=Tb[7][:, TT - 1:TT], scalar1=e9c,
                                                               scalar2=None, op0=ALU.mult),
                              reads=[Ttb[7], e9ct], writes=[init[m]])
                        kk.op("dve", lambda e: e.scalar_tensor_tensor(
                            out=ini[:, m, 0:1], in0=Tb[6][:, TT - 1:TT], scalar=e9c, in1=itmp[:, m, 0:1],
                            op0=ALU.mult, op1=ALU.subtract), reads=[Ttb[6], e9ct], writes=[init[m]])
                        kk.op("dve", lambda e: e.scalar_tensor_tensor(
                            out=ini[:, m, 1:2], in0=Tb[6][:, TT - 1:TT], scalar=e9s, in1=itmp[:, m, 1:2],
                            op0=ALU.mult, op1=ALU.add), reads=[Ttb[6], e9st], writes=[init[m]])

                    def stD(it, n, m):
                        pb = it % 2
                        Tb, Ttb = T[pb], Tt[pb]
                        for (ti, zi_, tab) in ((0, 6, tabc), (1, 7, tabs), (2, 6, tabs), (3, 7, tabc)):
                            kk.op("pool", lambda e, ti=ti, zi_=zi_, tab=tab: e.tensor_tensor(
                                out=U[ti][:], in0=Tb[zi_][:], in1=tab[m][:], op=ALU.mult),
                                reads=[Ttb[zi_], tabt[m]], writes=[Ut[ti]])
                        kk.op("pool", lambda e: e.tensor_tensor(out=xb[pb][0][:], in0=U[0][:], in1=U[1][:], op=ALU.subtract),
                              reads=[Ut[0], Ut[1]], writes=[xbt[pb][0]])
                        kk.op("pool", lambda e: e.tensor_tensor(out=xb[pb][1][:], in0=U[2][:], in1=U[3][:], op=ALU.add),
                              reads=[Ut[2], Ut[3]], writes=[xbt[pb][1]])

                    def stE(it, n, m):
                        cs_ = slice(n * TT, (n + 1) * TT)
                        pb = it % 2
                        py = 4 + (n % 2)
                        e_pe = kk.engs["pe"]
                        kk._deps(e_pe, [CWt[0][m], CWt[1][m], xbt[pb][0], xbt[pb][1]], [pst[py]] if m == 0 else [])
                        e_pe.obj.matmul(ps[py][:], CW[0][m][:], xb[pb][0][:], start=(m == 0), stop=False)
                        inst = e_pe.obj.matmul(ps[py][:], CW[1][m][:], xb[pb][1][:], start=False, stop=(m == 3))
                        e_pe.cnt += 1
                        inst.then_inc(e_pe.sem, 1)
                        kk._reg((e_pe.sem, e_pe.cnt), [CWt[0][m], CWt[1][m], xbt[pb][0], xbt[pb][1]], [pst[py]])
                        if m == 3:
                            yb_ = n % 2
                            kk.op("dve", lambda e: e.scalar_tensor_tensor(
                                out=yv[yb_][:], in0=ufm[:, k, cs_], scalar=dcol[:, k:k + 1], in1=ps[py][:],
                                op0=ALU.mult, op1=ALU.add), reads=[ut[k][n], pt, pst[py]], writes=[yvt[yb_]])
                            kk.op("act", lambda e: e.activation(out=ufm[:, k, cs_], in_=yv[yb_][:],
                                                                func=AF.Gelu_apprx_tanh),
                                  reads=[yvt[yb_]], writes=[ut[k][n]])

                    NI = len(items)
                    for s_ in range(NI + 1):
                        if s_ < NI:
                            stA(s_, *items[s_])
                            stB(s_, *items[s_])
                        if s_ >= 1:
                            stC(s_ - 1, *items[s_ - 1])
                            stD(s_ - 1, *items[s_ - 1])
                            stE(s_ - 1, *items[s_ - 1])
                    if S5STOP == "k0":
                        return
            kk.barrier()
            if S5STOP == "p2":
                return
            with ExitStack() as es:
                def A(name, shape, dtp):
                    return es.enter_context(nc.sbuf_tensor(f"{name}_{l}", shape, dtp))
                wg = A("s3_wg", [128, 8, 2048], BF16)
                wgt = kk.tok("s3_wg", dma=True)
                cg = self.cast_tok[("glu", l)]
                for k in range(8):
                    kk.dma("sp", wg[:, k, :], self.wglu_b[j5][k * 128:(k + 1) * 128, :], reads=[cg], writes=[wgt], st=wgt)
                hb = [A(f"s3_h{i}", [128, 8, TT], F32) for i in range(2)]
                hbt = [kk.tok(f"s3_h{i}", dma=True) for i in range(2)]
                hst = [kk.tok(f"s3_hs{i}", dma=True) for i in range(2)]
                yb = A("s3_y", [128, 8, TT], F32)
                ybt = [kk.tok(f"s3_y{c}") for c in range(8)]
                ysq = A("s3_ysq", [128, 8, TT], BF16)
                ysqt = [kk.tok(f"s3_ysq{c}") for c in range(8)]
                sg = [A(f"s3_sg{i}", [128, TT], F32) for i in range(2)]
                sgt = [kk.tok(f"s3_sg{i}") for i in range(2)]
                tmp = A("s3_tmp", [128, TT], F32)
                tmpt = kk.tok("s3_tmp")
                rstd = A("s3_rstd", [128, TT], F32)
                rstdt = kk.tok("s3_rstd")
                for i in range(NT):
                    b = i % 2
                    cs_ = slice(i * TT, (i + 1) * TT)
                    kk.dma("sp", hb[b][:], hsrc[:, cs_].rearrange("(k p) t -> p k t", p=128), writes=[hbt[b]], st=hbt[b])
                    for c in range(8):
                        pv, pg = (0, 1) if c % 2 == 0 else (2, 3)
                        kk.mm_group(ps[pv][:], [(wg[:, k, c * 128:(c + 1) * 128], ufm[:, k, cs_]) for k in range(8)],
                                    reads=[wgt] + [ut[k][i] for k in range(8)], writes=[pst[pv]])
                        kk.mm_group(ps[pg][:], [(wg[:, k, 1024 + c * 128:1024 + (c + 1) * 128], ufm[:, k, cs_])
                                                for k in range(8)],
                                    reads=[wgt] + [ut[k][i] for k in range(8)], writes=[pst[pg]])
                        s_ = c % 2
                        kk.op("act", lambda e: e.activation(out=sg[s_][:], in_=ps[pg][:], func=AF.Sigmoid),
                              reads=[pst[pg]], writes=[sgt[s_]])
                        kk.op("dve", lambda e: e.tensor_tensor(out=yb[:, c, :], in0=ps[pv][:], in1=sg[s_][:], op=ALU.mult),
                              reads=[pst[pv], sgt[s_]], writes=[ybt[c]])
                        kk.op("act", lambda e: e.activation(out=ysq[:, c, :], in_=yb[:, c, :], func=AF.Square),
                              reads=[ybt[c]], writes=[ysqt[c]])
                    kk.mm_group(ps[6][:], [(self.ones[:], ysq[:, c, :]) for c in range(8)],
                                reads=ysqt + [tc], writes=[pst[6]])
                    self.rstd_from_ss(ps[6][:], pst[6], tmp[:], tmpt, rstd[:], rstdt)
                    for c in range(8):
                        kk.op("dve", lambda e, c=c: e.scalar_tensor_tensor(
                            out=yb[:, c, :], in0=yb[:, c, :], scalar=self.gcol[:, g1 + c:g1 + c + 1],
                            in1=rstd[:], op0=ALU.mult, op1=ALU.mult),
                            reads=[rstdt, tc], writes=[ybt[c]])
                    kk.op("pool", lambda e: e.tensor_tensor(out=hb[b][:], in0=hb[b][:], in1=yb[:], op=ALU.add),
                          reads=ybt, writes=[hbt[b]])
                    kk.dma("sp", hdst[:, cs_].rearrange("(k p) t -> p k t", p=128), hb[b][:],
                           reads=[hbt[b]], writes=[hst[b]], st=hst[b])

    def norm_full(self, l, hsrc, gidx, ufm, ut, pfx):
        kk, nc = self.kk, self.nc
        TT = 512
        NT = L // TT
        ps, pst, tc = self.ps, self.pst, self.t_const
        with ExitStack() as es:
            def A(name, shape, dtp):
                return es.enter_context(nc.sbuf_tensor(f"{pfx}{name}_{l}", shape, dtp))
            hb = [A(f"h{i}", [128, 8, TT], F32) for i in range(2)]
            hbt = [kk.tok(f"{pfx}h{i}", dma=True) for i in range(2)]
            sq = [A(f"sq{i}", [128, 8, TT], BF16) for i in range(2)]
            sqt = [kk.tok(f"{pfx}sq{i}") for i in range(2)]
            tmp = A("tmp", [128, TT], F32)
            tmpt = kk.tok(f"{pfx}tmp")
            rstd = [A(f"rstd{i}", [128, TT], F32) for i in range(2)]
            rstdt = [kk.tok(f"{pfx}rstd{i}") for i in range(2)]
            for i in range(NT):
                b = i % 2
                kk.dma("sp", hb[b][:], hsrc[:, i * TT:(i + 1) * TT].rearrange("(k p) t -> p k t", p=128),
                       writes=[hbt[b]], st=hbt[b])
                kk.op("act", lambda e: e.activation(out=sq[b][:], in_=hb[b][:], func=AF.Square),
                      reads=[hbt[b]], writes=[sqt[b]])
                kk.mm_group(ps[6 + b][:], [(self.ones[:], sq[b][:, k, :]) for k in range(8)],
                            reads=[sqt[b], tc], writes=[pst[6 + b]])
                self.rstd_from_ss(ps[6 + b][:], pst[6 + b], tmp[:], tmpt, rstd[b][:], rstdt[b])
                for k in range(8):
                    kk.op("dve", lambda e, k=k: e.scalar_tensor_tensor(
                        out=ufm[:, k, i * TT:(i + 1) * TT], in0=hb[b][:, k, :],
                        scalar=self.gcol[:, gidx + k:gidx + k + 1], in1=rstd[b][:], op0=ALU.mult, op1=ALU.mult),
                        reads=[hbt[b], rstdt[b], tc], writes=[ut[k][i]])
        kk.barrier()

    def fox_phase(self, l, hsrc, hdst):
        kk, nc = self.kk, self.nc
        jf = l // 2
        TT = 512
        NT = L // TT
        g0 = (l * 4 + 0) * 8
        g1 = (l * 4 + 1) * 8
        ps, pst, tc = self.ps, self.pst, self.t_const
        o_d = self.o_d
        odt = kk.tok("o_d")
        with ExitStack() as es0:
            ufm = es0.enter_context(nc.sbuf_tensor(f"f_u_{l}", [128, 8, L], BF16))
            ut = [[kk.tok(f"f_u{k}_{n}") for n in range(NT)] for k in range(8)]
            uall = [ut[k][n] for k in range(8) for n in range(NT)]
            self.norm_full(l, hsrc, g0, ufm, ut, "f1_")
            negcT = es0.enter_context(nc.sbuf_tensor(f"f_negcT_{l}", [128, 32, 16], F32))
            negct = kk.tok("f_negcT")
            cq_t = kk.tok("f_cq", dma=True)
            with ExitStack() as es:
                def A(name, shape, dtp):
                    return es.enter_context(nc.sbuf_tensor(f"{name}_{l}", shape, dtp))
                wf = A("f2_wf", [128, 8, 16], BF16)
                wft = kk.tok("f2_wf", dma=True)
                kk.dma("sp", wf[:], self.wf_b[jf].rearrange("(k p) n -> p k n", p=128), reads=[self.cast_tok[("wf", l)]],
                       writes=[wft], st=wft)
                bf_ = A("f2_bf", [16, 1], F32)
                nbf = A("f2_nbf", [16, 1], F32)
                bft = kk.tok("f2_bf", dma=True)
                kk.dma("sp", bf_[:], self.bf_d[jf], writes=[bft], st=bft)
                kk.op("dve", lambda e: e.tensor_scalar(out=nbf[:], in0=bf_[:], scalar1=-1.0, scalar2=None, op0=ALU.mult),
                      reads=[bft], writes=[bft])
                cum = A("f2_cum", [16, L], F32)
                cumt = kk.tok("f2_cum")
                ex = [A(f"f2_ex{i}", [16, TT], F32) for i in range(2)]
                ext = [kk.tok(f"f2_ex{i}") for i in range(2)]
                one16 = A("f2_one16", [16, TT], F32)
                kk.op("pool", lambda e: e.memset(one16[:], 1.0), writes=[tc])
                zero1 = A("f2_zero", [16, 1], F32)
                kk.op("pool", lambda e: e.memset(zero1[:], 0.0), writes=[tc])
                for n in range(NT):
                    cs_ = slice(n * TT, (n + 1) * TT)
                    b = n % 2
                    kk.mm_group(ps[b][0:16, :], [(wf[:, k, :], ufm[:, k, cs_]) for k in range(8)],
                                reads=[wft] + [ut[k][n] for k in range(8)], writes=[pst[b]])
                    kk.op("act", lambda e: e.activation(out=ex[b][:], in_=ps[b][0:16, :], func=AF.Exp, scale=-1.0,
                                                        bias=nbf[:]), reads=[pst[b], bft], writes=[ext[b]])
                    kk.op("act", lambda e: e.activation(out=ex[b][:], in_=ex[b][:], func=AF.Ln, bias=self.onec[0:16, :]),
                          reads=[ext[b], tc], writes=[ext[b]])
                    kk.op("dve", lambda e: e.tensor_scalar(out=ex[b][:], in0=ex[b][:], scalar1=-1.0, scalar2=None,
                                                           op0=ALU.mult), reads=[ext[b]], writes=[ext[b]])
                    kk.op("dve", lambda e: e.tensor_tensor_scan(
                        out=cum[:, cs_], data0=one16[:], data1=ex[b][:],
                        initial=(zero1[:] if n == 0 else cum[:, n * TT - 1:n * TT]), op0=ALU.mult, op1=ALU.add),
                        reads=[ext[b], tc, cumt], writes=[cumt])
                for tb in range(32):
                    kk.op("pe", lambda e, tb=tb: e.transpose(out=ps[6][:, tb * 16:(tb + 1) * 16],
                                                             in_=cum[:, tb * 128:(tb + 1) * 128],
                                                             identity=self.ident[0:16, 0:16]),
                          reads=[cumt, tc], writes=[pst[6]])
                kk.op("dve", lambda e: e.tensor_scalar(out=negcT[:].rearrange("p a b -> p (a b)"), in0=ps[6][:],
                                                       scalar1=-1.0, scalar2=None, op0=ALU.mult),
                      reads=[pst[6]], writes=[negct])
                c8 = A("f2_c8", [16, L], F32)
                c8t = kk.tok("f2_c8")
                cp = [A(f"f2_cp{i}", [16, L], BF16) for i in range(3)]
                cpt = [kk.tok(f"f2_cp{i}", dma=True) for i in range(3)]
                kk.op("dve", lambda e: e.tensor_scalar(out=c8[:], in0=cum[:], scalar1=8.0, scalar2=None, op0=ALU.mult),
                      reads=[cumt], writes=[c8t])
                for i in range(3):
                    kk.op("dve", lambda e, i=i: e.tensor_copy(out=cp[i][:], in_=c8[:]), reads=[c8t], writes=[cpt[i]])
                    if i < 2:
                        kk.op("dve", lambda e, i=i: e.tensor_tensor(out=c8[:], in0=c8[:], in1=cp[i][:], op=ALU.subtract),
                              reads=[cpt[i]], writes=[c8t])
                    kk.dma("sp", self.cumq[:, i, :], cp[i][:], reads=[cpt[i]], writes=[cq_t], st=cpt[i])
            kk.barrier()
            with ExitStack() as es:
                def A(name, shape, dtp):
                    return es.enter_context(nc.sbuf_tensor(f"{name}_{l}", shape, dtp))
                qa = [A(f"f3_q{i}", [128, L], BF16) for i in range(2)]
                ka = [A(f"f3_k{i}", [128, L], BF16) for i in range(2)]
                qat = [kk.tok(f"f3_q{i}", dma=True) for i in range(2)]
                kat = [kk.tok(f"f3_k{i}") for i in range(2)]
                va = [A(f"f3_v{i}", [128, 32, 65], BF16) for i in range(2)]
                vat = [kk.tok(f"f3_v{i}") for i in range(2)]
                wh = [A(f"f3_w{i}", [128, 8, 256], BF16) for i in range(2)]
                wht = [kk.tok(f"f3_w{i}", dma=True) for i in range(2)]
                P = [A(f"f3_P{i}", [128, TT], BF16) for i in range(4)]
                Pt = [kk.tok(f"f3_P{i}") for i in range(4)]
                eg = A("f3_eg", [64, TT], F32)
                egt = kk.tok("f3_eg")
                rs = A("f3_rs", [128, TT], F32)
                rst = kk.tok("f3_rs")
                den = A("f3_den", [64, TT], F32)
                dent = kk.tok("f3_den")
                osb = [A(f"f3_o{i}", [64, TT], BF16) for i in range(2)]
                osbt = [kk.tok(f"f3_o{i}", dma=True) for i in range(2)]
                tri = A("f3_tri", [128, 128], BF16)
                trif = A("f3_trif", [128, 128], F32)
                trit = kk.tok("f3_tri", dma=True)
                onesf = A("f3_onesf", [128, 64], F32)
                kk.op("pool", lambda e: e.memset(onesf[:], 1.0), writes=[tc])
                kk.dma("sp", trif[:], self.tri_d, writes=[trit], st=trit)
                kk.op("dve", lambda e: e.tensor_copy(out=tri[:], in_=trif[:]), reads=[trit], writes=[trit])
                for i in range(2):
                    kk.op("pool", lambda e, i=i: e.memset(ka[i][64:96, :], 1.0), writes=[kat[i]])
                    kk.op("pool", lambda e, i=i: e.memset(va[i][:, :, 64:65], 1.0), writes=[vat[i]])
                cw = self.cast_tok[("win", l)]
                pcount = 0
                for h in range(16):
                    hb_ = h % 2
                    q_, k_, v_, w_ = qa[hb_], ka[hb_], va[hb_], wh[hb_]
                    kk.dma("sp", w_[:], self.win_b[jf, h].rearrange("(k p) n -> p k n", p=128), reads=[cw],
                           writes=[wht[hb_]], st=wht[hb_])
                    kk.dma("sp", q_[64:67, :], self.cumq[h], reads=[cq_t], writes=[qat[hb_]], st=qat[hb_])
                    for n in range(NT):
                        cs_ = slice(n * TT, (n + 1) * TT)
                        un = [ut[k][n] for k in range(8)]
                        kk.mm_group(ps[3][0:64, :], [(w_[:, k, 0:64], ufm[:, k, cs_]) for k in range(8)],
                                    reads=[wht[hb_]] + un, writes=[pst[3]])
                        kk.op("act", lambda e: e.activation(out=q_[0:64, cs_], in_=ps[3][0:64, :], func=AF.Copy),
                              reads=[pst[3]], writes=[qat[hb_]])
                        kk.mm_group(ps[7][0:64, :], [(w_[:, k, 64:128], ufm[:, k, cs_]) for k in range(8)],
                                    reads=[wht[hb_]] + un, writes=[pst[7]])
                        kk.op("dve", lambda e: e.tensor_copy(out=k_[0:64, cs_], in_=ps[7][0:64, :]),
                              reads=[pst[7]], writes=[kat[hb_]])
                    for tg in range(4):
                        for t8 in range(8):
                            tb = tg * 8 + t8
                            e_pe = kk.engs["pe"]
                            rd = [wht[hb_]] + [ut[k][tb // 4] for k in range(8)]
                            kk._deps(e_pe, rd, [pst[7]] if t8 == 0 else [])
                            inst = None
                            for k in range(8):
                                inst = e_pe.obj.matmul(ps[7][:, t8 * 64:(t8 + 1) * 64], ufm[:, k, tb * 128:(tb + 1) * 128],
                                                       w_[:, k, 128:192], start=(k == 0), stop=(k == 7))
                            e_pe.cnt += 1
                            inst.then_inc(e_pe.sem, 1)
                            kk._reg((e_pe.sem, e_pe.cnt), rd, [pst[7]])
                        kk.op("dve", lambda e: e.tensor_copy(out=v_[:, tg * 8:(tg + 1) * 8, 0:64],
                                                             in_=ps[7][:].rearrange("p (a b) -> p a b", b=64)),
                              reads=[pst[7]], writes=[vat[hb_]])
                    for qc in range(NT):
                        qs = slice(qc * TT, (qc + 1) * TT)
                        po = 4 + (qc % 2)
                        un = [ut[k][qc] for k in range(8)]
                        kk.mm_group(ps[3][0:64, :], [(w_[:, k, 192:256], ufm[:, k, qs]) for k in range(8)],
                                    reads=[wht[hb_]] + un, writes=[pst[3]])
                        kk.op("act", lambda e: e.activation(out=eg[:], in_=ps[3][0:64, :], func=AF.Exp, scale=-1.0),
                              reads=[pst[3]], writes=[egt])
                        nkt = 4 * qc + 4
                        slots = {}

                        def issue_S(kt):
                            nonlocal pcount
                            j = kt - 4 * qc
                            c0 = max(0, j) * 128
                            pb = pcount % 3
                            pp = pcount % 4
                            pcount += 1
                            slots[kt] = (c0, pp)
                            kk.mm_group(ps[pb][:, c0:TT],
                                        [(k_[0:67, kt * 128:(kt + 1) * 128], q_[0:67, qc * TT + c0:(qc + 1) * TT])],
                                        reads=[kat[hb_], qat[hb_]], writes=[pst[pb]])
                            kk.op("act", lambda e: e.activation(out=P[pp][:, c0:TT], in_=ps[pb][:, c0:TT], func=AF.Exp,
                                                                scale=0.125, bias=negcT[:, kt, h:h + 1]),
                                  reads=[pst[pb], negct], writes=[Pt[pp]])
                            if j >= 0:
                                kk.op("pool", lambda e: e.tensor_tensor(out=P[pp][:, c0:c0 + 128], in0=P[pp][:, c0:c0 + 128],
                                                                        in1=tri[:], op=ALU.mult),
                                      reads=[trit], writes=[Pt[pp]])
                        issue_S(0)
                        if nkt > 1:
                            issue_S(1)
                        for kt in range(nkt):
                            if kt + 2 < nkt:
                                issue_S(kt + 2)
                            c0, pp = slots[kt]
                            e_pe = kk.engs["pe"]
                            kk._deps(e_pe, [vat[hb_], Pt[pp]], [pst[po]] if kt == 0 else [])
                            inst = e_pe.obj.matmul(ps[po][0:65, c0:TT], v_[:, kt, 0:65], P[pp][:, c0:TT],
                                                   start=(kt == 0), stop=(kt == nkt - 1))
                            e_pe.cnt += 1
                            inst.then_inc(e_pe.sem, 1)
                            kk._reg((e_pe.sem, e_pe.cnt), [vat[hb_], Pt[pp]], [pst[po]])
                        ob = qc % 2
                        kk.op("dve", lambda e: e.tensor_copy(out=rs[64:65, :], in_=ps[po][64:65, :]),
                              reads=[pst[po]], writes=[rst])
                        kk.mm_group(ps[6][0:64, :], [(onesf[64:65, 0:64], rs[64:65, :])], reads=[rst, tc], writes=[pst[6]])
                        kk.op("dve", lambda e: e.scalar_tensor_tensor(out=den[:], in0=eg[:], scalar=1.0, in1=ps[6][0:64, :],
                                                                      op0=ALU.add, op1=ALU.mult),
                              reads=[egt, pst[6]], writes=[dent])
                        kk.op("dve", lambda e: e.reciprocal(out=den[:], in_=den[:]), reads=[dent], writes=[dent])
                        kk.op("dve", lambda e: e.tensor_tensor(out=osb[ob][:], in0=ps[po][0:64, :], in1=den[:], op=ALU.mult),
                              reads=[pst[po], dent], writes=[osbt[ob]])
                        kk.dma("sp", o_d[h * 64:(h + 1) * 64, qs], osb[ob][:], reads=[osbt[ob]], writes=[odt], st=osbt[ob])
        kk.barrier()
        with ExitStack() as es:
            def A(name, shape, dtp):
                return es.enter_context(nc.sbuf_tensor(f"{name}_{l}", shape, dtp))
            wo = A("f4_wo", [128, 8, 1024], BF16)
            wot = kk.tok("f4_wo", dma=True)
            kk.dma("sp", wo[:], self.wout_b[jf].rearrange("(k p) n -> p k n", p=128), reads=[self.cast_tok[("wout", l)]],
                   writes=[wot], st=wot)
            ob = [A(f"f4_o{i}", [128, 8, TT], BF16) for i in range(2)]
            obt = [kk.tok(f"f4_o{i}", dma=True) for i in range(2)]
            hb = [A(f"f4_h{i}", [128, 8, TT], F32) for i in range(2)]
            hbt = [kk.tok(f"f4_h{i}", dma=True) for i in range(2)]
            hst = [kk.tok(f"f4_hs{i}", dma=True) for i in range(2)]
            yb = A("f4_y", [128, 8, TT], F32)
            ybt = [kk.tok(f"f4_y{c}") for c in range(8)]
            ysq = A("f4_ysq", [128, 8, TT], BF16)
            ysqt = [kk.tok(f"f4_ysq{c}") for c in range(8)]
            tmp = A("f4_tmp", [128, TT], F32)
            tmpt = kk.tok("f4_tmp")
            rstd = A("f4_rstd", [128, TT], F32)
            rstdt = kk.tok("f4_rstd")
            for i in range(NT):
                b = i % 2
                cs_ = slice(i * TT, (i + 1) * TT)
                kk.dma("sp", hb[b][:], hsrc[:, cs_].rearrange("(k p) t -> p k t", p=128), writes=[hbt[b]], st=hbt[b])
                kk.dma("sp", ob[b][:], o_d[:, cs_].rearrange("(k p) t -> p k t", p=128), reads=[odt], writes=[obt[b]],
                       st=obt[b])
                for c in range(8):
                    p = c % 2
                    kk.mm_group(ps[p][:], [(wo[:, k, c * 128:(c + 1) * 128], ob[b][:, k, :]) for k in range(8)],
                                reads=[wot, obt[b]], writes=[pst[p]])
                    kk.op("dve", lambda e: e.tensor_copy(out=yb[:, c, :], in_=ps[p][:]), reads=[pst[p]], writes=[ybt[c]])
                    kk.op("act", lambda e: e.activation(out=ysq[:, c, :], in_=yb[:, c, :], func=AF.Square),
                          reads=[ybt[c]], writes=[ysqt[c]])
                kk.mm_group(ps[6][:], [(self.ones[:], ysq[:, c, :]) for c in range(8)], reads=ysqt + [tc], writes=[pst[6]])
                self.rstd_from_ss(ps[6][:], pst[6], tmp[:], tmpt, rstd[:], rstdt)
                for c in range(8):
                    kk.op("dve", lambda e, c=c: e.scalar_tensor_tensor(
                        out=yb[:, c, :], in0=yb[:, c, :], scalar=self.gcol[:, g1 + c:g1 + c + 1],
                        in1=rstd[:], op0=ALU.mult, op1=ALU.mult), reads=[rstdt, tc], writes=[ybt[c]])
                kk.op("pool", lambda e: e.tensor_tensor(out=hb[b][:], in0=hb[b][:], in1=yb[:], op=ALU.add),
                      reads=ybt, writes=[hbt[b]])
                kk.dma("sp", hdst[:, cs_].rearrange("(k p) t -> p k t", p=128), hb[b][:],
                       reads=[hbt[b]], writes=[hst[b]], st=hst[b])

    def mlp_phase(self, l, hsrc, hdst):
        kk, nc = self.kk, self.nc
        TT = 512
        NT = L // TT
        g2 = (l * 4 + 2) * 8
        g3 = (l * 4 + 3) * 8
        with ExitStack() as es:
            def A(name, shape, dtp):
                return es.enter_context(nc.sbuf_tensor(f"{name}_{l}", shape, dtp))
            hb = [A(f"m_h{i}", [128, 8, TT], F32) for i in range(2)]
            hbt = [kk.tok(f"m_h{i}", dma=True) for i in range(2)]
            hst = [kk.tok(f"m_hs{i}", dma=True) for i in range(2)]
            sq = A("m_sq", [128, 8, TT], BF16)
            sqt = kk.tok("m_sq")
            ub = [A(f"m_u{i}", [128, 8, TT], BF16) for i in range(2)]
            ubt = [kk.tok(f"m_u{i}") for i in range(2)]
            hid = A("m_hid", [128, 32, TT], BF16)
            hidt = [kk.tok(f"m_hid{j}") for j in range(32)]
            rr = [A(f"m_r{i}", [128, TT], F32) for i in range(2)]
            rrt = [kk.tok(f"m_r{i}") for i in range(2)]
            w1b = [A(f"m_w1_{i}", [128, 8, 512], BF16) for i in range(3)]
            w1t = [kk.tok(f"m_w1_{i}", dma=True) for i in range(3)]
            w2b = [A(f"m_w2_{i}", [128, 32, 128], BF16) for i in range(3)]
            w2t = [kk.tok(f"m_w2_{i}", dma=True) for i in range(3)]
            yb = A("m_y", [128, 8, TT], F32)
            ybt = [kk.tok(f"m_y{c}") for c in range(8)]
            ysq = A("m_ysq", [128, 8, TT], BF16)
            ysqt = [kk.tok(f"m_ysq{c}") for c in range(8)]
            tmp = A("m_tmp", [128, TT], F32)
            tmpt = kk.tok("m_tmp")
            rstd = [A(f"m_rstd{i}", [128, TT], F32) for i in range(2)]
            rstdt = [kk.tok(f"m_rstd{i}") for i in range(2)]

            w1v = self.w1_b[l].rearrange("(r a) c -> r (a c)", a=2)
            w2v = self.w2_b[l]
            c1, c2 = self.cast_tok[("w1", l)], self.cast_tok[("w2", l)]

            def mk1(q):
                def ld(buf, tk):
                    kk.dma("sp", buf[:], w1v[:, q * 512:(q + 1) * 512].rearrange("(k p) n -> p k n", p=128),
                           reads=[c1], writes=[tk], st=tk)
                return ld

            def mk2(c):
                def ld(buf, tk):
                    kk.dma("sp", buf[:].rearrange("p j n -> p (j n)"), w2v[c], reads=[c2], writes=[tk], st=tk)
                return ld
            s1 = Stream(kk, w1b, w1t, [mk1(q) for _ in range(NT) for q in range(8)])
            s2 = Stream(kk, w2b, w2t, [mk2(c) for _ in range(NT) for c in range(8)])
            PS_S, PS_U, PS_D = 0, (1, 2, 3), (4, 5)
            ps, pst = self.ps, self.pst

            def norm_in(i):
                b = i % 2
                kk.dma("sp", hb[b][:], hsrc[:, i * TT:(i + 1) * TT].rearrange("(k p) t -> p k t", p=128),
                       writes=[hbt[b]], st=hbt[b])
                kk.op("act", lambda e: e.activation(out=sq[:], in_=hb[b][:], func=AF.Square),
                      reads=[hbt[b]], writes=[sqt])
                kk.mm_group(ps[PS_S][:], [(self.ones[:], sq[:, k, :]) for k in range(8)],
                            reads=[sqt, self.t_const], writes=[pst[PS_S]])
                self.rstd_from_ss(ps[PS_S][:], pst[PS_S], tmp[:], tmpt, rstd[0][:], rstdt[0])
                for k in range(8):
                    kk.op("dve", lambda e, k=k: e.scalar_tensor_tensor(
                        out=ub[b][:, k, :], in0=hb[b][:, k, :], scalar=self.gcol[:, g2 + k:g2 + k + 1],
                        in1=rstd[0][:], op0=ALU.mult, op1=ALU.mult),
                        reads=[hbt[b], rstdt[0], self.t_const], writes=[ubt[b]])

            def up(i):
                b = i % 2
                for j in range(32):
                    q = i * 8 + j // 4
                    wb, wt = s1.get(q)
                    jj = j % 4
                    p = PS_U[j % 3]
                    kk.mm_group(ps[p][:], [(wb[:, k, jj * 128:(jj + 1) * 128], ub[b][:, k, :]) for k in range(8)],
                                reads=[wt, ubt[b]], writes=[pst[p]])
                    r = j % 2
                    kk.op("act", lambda e, p=p, r=r: e.activation(out=rr[r][:], in_=ps[p][:], func=AF.Relu),
                          reads=[pst[p]], writes=[rrt[r]])
                    kk.op("pool", lambda e, r=r, j=j: e.tensor_tensor(out=hid[:, j, :], in0=rr[r][:], in1=rr[r][:],
                                                                     op=ALU.mult),
                          reads=[rrt[r]], writes=[hidt[j]])

            def down(i):
                b = i % 2
                for c in range(8):
                    wb, wt = s2.get(i * 8 + c)
                    p = PS_D[c % 2]
                    kk.mm_group(ps[p][:], [(wb[:, j, :], hid[:, j, :]) for j in range(32)],
                                reads=[wt] + hidt, writes=[pst[p]])
                    kk.op("dve", lambda e, p=p, c=c: e.tensor_copy(out=yb[:, c, :], in_=ps[p][:]),
                          reads=[pst[p]], writes=[ybt[c]])
                    kk.op("act", lambda e, c=c: e.activation(out=ysq[:, c, :], in_=yb[:, c, :], func=AF.Square),
                          reads=[ybt[c]], writes=[ysqt[c]])
                import os
                stp = os.environ.get("KSTOP", "")
                if stp == "down0a":
                    return
                kk.mm_group(ps[PS_S][:], [(self.ones[:], ysq[:, c, :]) for c in range(8)],
                            reads=ysqt + [self.t_const], writes=[pst[PS_S]])
                self.rstd_from_ss(ps[PS_S][:], pst[PS_S], tmp[:], tmpt, rstd[1][:], rstdt[1])
                for c in range(8):
                    kk.op("dve", lambda e, c=c: e.scalar_tensor_tensor(
                        out=yb[:, c, :], in0=yb[:, c, :], scalar=self.gcol[:, g3 + c:g3 + c + 1],
                        in1=rstd[1][:], op0=ALU.mult, op1=ALU.mult),
                        reads=[rstdt[1], self.t_const], writes=[ybt[c]])
                if stp == "down0b":
                    return
                kk.op("pool", lambda e: e.tensor_tensor(out=hb[b][:], in0=hb[b][:], in1=yb[:], op=ALU.add),
                      reads=ybt, writes=[hbt[b]])
                if stp == "down0c":
                    return
                kk.dma("sp", hdst[:, i * TT:(i + 1) * TT].rearrange("(k p) t -> p k t", p=128), hb[b][:],
                       reads=[hbt[b]], writes=[hst[b]], st=hst[b])

            import os
            stop = os.environ.get("KSTOP", "")
            if stop == "cast":
                return
            norm_in(0)
            if stop == "norm0":
                return
            for i in range(NT):
                up(i)
                if stop == "up0":
                    return
                if i + 1 < NT and stop != "down0x":
                    norm_in(i + 1)
                if stop == "norm1":
                    return
                down(i)
                if stop.startswith("down0"):
                    return


FULL_PHASES = [(("s5" if l % 2 == 0 else "fox") if w == 0 else "mlp", l) for l in range(DEPTH) for w in range(2)]


def prep_inputs(inputs, b):
    m = {}
    m["x"] = np.ascontiguousarray(inputs["x"][b].T)
    g = np.asarray(inputs["norm_gains"], np.float32)
    m["gcol"] = np.ascontiguousarray(g.reshape(DEPTH * 4, 8, 128).transpose(2, 0, 1).reshape(128, DEPTH * 4 * 8))
    m["w1"] = np.ascontiguousarray(np.asarray(inputs["mlp_w1"], np.float32).reshape(DEPTH, -1, 2048))
    w2 = np.asarray(inputs["mlp_w2"], np.float32).reshape(DEPTH, 32, 128, 8, 128)
    m["w2"] = np.ascontiguousarray(w2.transpose(0, 3, 2, 1, 4).reshape(DEPTH, -1, 2048))
    NS = 2
    are = np.asarray(inputs["s5_a_re"], np.float32); aim = np.asarray(inputs["s5_a_im"], np.float32)
    ldt = np.asarray(inputs["s5_log_dt"], np.float32)
    def pairT(a):
        return np.ascontiguousarray(a.reshape(NS, 32, 2, 64).transpose(0, 2, 3, 1).reshape(NS, 128, 32))
    m["s5_at_re"] = pairT(are)
    m["s5_at_im"] = pairT(aim)
    m["s5_ldt"] = pairT(np.broadcast_to(ldt[:, :, None], (NS, 64, 64)))
    def pairB(b_):
        return np.ascontiguousarray(b_.reshape(NS, 32, 2, 64, 16).transpose(0, 2, 3, 1, 4).reshape(NS, 128, 512))
    m["s5_bre"] = pairB(np.asarray(inputs["s5_b_re"], np.float32))
    m["s5_bim"] = pairB(np.asarray(inputs["s5_b_im"], np.float32))
    m["s5_cre"] = pairB(np.asarray(inputs["s5_c_re"], np.float32).transpose(0, 1, 3, 2))
    m["s5_cim"] = pairB(np.asarray(inputs["s5_c_im"], np.float32).transpose(0, 1, 3, 2))
    m["s5_dcol"] = np.ascontiguousarray(np.asarray(inputs["s5_d"], np.float32).reshape(NS, 8, 128).transpose(0, 2, 1))
    m["wglu"] = np.ascontiguousarray(np.asarray(inputs["s5_w_glu"], np.float32))
    m["ident"] = np.eye(128, dtype=np.float32)
    NF = 2
    win = np.asarray(inputs["fox_w_in"], np.float32)
    m["win"] = np.ascontiguousarray(win[:, :, :4096].reshape(NF, 1024, 4, 16, 64).transpose(0, 3, 1, 2, 4).reshape(NF, 16, 1024, 256))
    m["wf"] = np.ascontiguousarray(win[:, :, 4096:4112])
    m["bf"] = np.ascontiguousarray(np.asarray(inputs["fox_b_f"], np.float32).reshape(NF, 16, 1))
    m["wout"] = np.ascontiguousarray(np.asarray(inputs["fox_w_out"], np.float32))
    m["tri"] = np.triu(np.ones((128, 128), np.float32))
    return m


def run(inputs, phases=None, cores=NCORES, trace=False):
    prog = Prog(phases or FULL_PHASES)
    nc = prog.build()
    shared = None
    in_maps = []
    for b in range(cores):
        m = prep_inputs(inputs, b)
        if shared is None:
            shared = m
        else:
            for k_ in m:
                if k_ != "x":
                    m[k_] = shared[k_]
        in_maps.append(m)
    res = run_bass_kernel_spmd(nc, in_maps, core_ids=list(range(cores)), trace=trace)
    outs = [np.ascontiguousarray(r["out"].T) for r in res.results]
    return np.stack(outs, 0), res


def kernel(**inputs):
    out, _ = run(inputs)
    return out.astype(np.float32)
```

```python
import numpy as np
from contextlib import ExitStack
import concourse.bass as bass
import concourse.mybir as mybir
from concourse.bass_utils import run_bass_kernel_spmd

F32 = mybir.dt.float32
BF16 = mybir.dt.bfloat16
AF = mybir.ActivationFunctionType
ALU = mybir.AluOpType

D = 1024
L = 4096
DEPTH = 4
HID = 4096
EPS = 1e-6
NCORES = 8


class Tok:
    __slots__ = ("name", "w", "r", "ds")

    def __init__(self, name, ds=None):
        self.name = name
        self.w = None
        self.r = {}
        self.ds = ds


class DSem:
    def __init__(self, sem):
        self.sem = sem
        self.cnt = 0


class Eng:
    def __init__(self, name, obj, sem):
        self.name = name
        self.obj = obj
        self.sem = sem
        self.cnt = 0
        self.seen = {}


class K:
    def __init__(self, nc):
        self.nc = nc
        self.engs = {}
        for name, obj in (("pe", nc.tensor), ("act", nc.scalar), ("dve", nc.vector),
                          ("pool", nc.gpsimd), ("sp", nc.sync)):
            self.engs[name] = Eng(name, obj, nc.alloc_semaphore("s_" + name))
        self.dsems = {}
        self.uid = 0

    def tok(self, name, dma=False):
        t = Tok(name)
        if dma:
            if name not in self.dsems:
                self.dsems[name] = DSem(self.nc.alloc_semaphore("d_" + name))
            t.ds = self.dsems[name]
        return t

    def _wait(self, e, ev):
        sem, val = ev
        if sem is e.sem and e.name == "pe":
            return
        key = id(sem)
        if e.seen.get(key, 0) >= val:
            return
        e.obj.wait_ge(sem, val)
        e.seen[key] = val

    def _deps(self, e, reads, writes):
        for t in reads:
            if t.w is not None:
                self._wait(e, t.w)
        for t in writes:
            if t.w is not None:
                self._wait(e, t.w)
            for ev in t.r.values():
                self._wait(e, ev)

    def _reg(self, ev, reads, writes):
        for t in reads:
            t.r[id(ev[0])] = ev
        for t in writes:
            t.w = ev
            t.r = {}

    def op(self, en, fn, reads=(), writes=()):
        e = self.engs[en]
        self._deps(e, reads, writes)
        inst = fn(e.obj)
        e.cnt += 1
        inst.then_inc(e.sem, 1)
        self._reg((e.sem, e.cnt), reads, writes)
        return inst

    def mm_group(self, out_ap, pairs, reads=(), writes=(), **kw):
        e = self.engs["pe"]
        self._deps(e, reads, writes)
        n = len(pairs)
        inst = None
        for i, (lhsT, rhs) in enumerate(pairs):
            inst = e.obj.matmul(out_ap, lhsT, rhs, start=(i == 0), stop=(i == n - 1), **kw)
        e.cnt += 1
        inst.then_inc(e.sem, 1)
        self._reg((e.sem, e.cnt), reads, writes)

    def dma(self, qn, out, in_, reads=(), writes=(), st=None, **kw):
        e = self.engs[qn]
        self._deps(e, reads, writes)
        inst = e.obj.dma_start(out=out, in_=in_, **kw)
        ds = st.ds
        ds.cnt += 16
        inst.then_inc(ds.sem, 16)
        self._reg((ds.sem, ds.cnt), reads, writes)

    def barrier(self):
        comp = [self.engs[n] for n in ("pe", "act", "dve", "pool")]
        for e in self.engs.values():
            for e2 in comp:
                if e2 is not e and e2.cnt > 0:
                    self._wait(e, (e2.sem, e2.cnt))
            for ds in self.dsems.values():
                if ds.cnt > 0:
                    self._wait(e, (ds.sem, ds.cnt))


class Stream:
    def __init__(self, kk, bufs, toks, loads):
        self.kk, self.bufs, self.toks, self.loads = kk, bufs, toks, loads
        self.nxt = 0

    def get(self, i):
        nb = len(self.bufs)
        while self.nxt < len(self.loads) and self.nxt <= i + nb - 1:
            j = self.nxt
            self.loads[j](self.bufs[j % nb], self.toks[j % nb])
            self.nxt += 1
        return self.bufs[i % nb], self.toks[i % nb]


class Prog:
    def __init__(self, phases):
        self.phases = phases
        nc = self.nc = bass.Bass("TRN2", target_bir_lowering=False)
        self.kk = K(nc)
        dt = nc.dram_tensor
        self.x = dt("x", [D, L], F32, kind="ExternalInput").ap()
        self.out = dt("out", [D, L], F32, kind="ExternalOutput").ap()
        self.gcol_d = dt("gcol", [128, DEPTH * 4 * 8], F32, kind="ExternalInput").ap()
        self.w1_d = dt("w1", [DEPTH, D * HID // 2048, 2048], F32, kind="ExternalInput").ap()
        self.w2_d = dt("w2", [DEPTH, D * HID // 2048, 2048], F32, kind="ExternalInput").ap()
        self.w1_b = dt("w1b", [DEPTH, D * HID // 2048, 2048], BF16, kind="Internal").ap()
        self.w2_b = dt("w2b", [DEPTH, 8, 128, 4096], BF16, kind="Internal").ap()
        NS = 2
        self.s5_at_re = dt("s5_at_re", [NS, 128, 32], F32, kind="ExternalInput").ap()
        self.s5_at_im = dt("s5_at_im", [NS, 128, 32], F32, kind="ExternalInput").ap()
        self.s5_ldt = dt("s5_ldt", [NS, 128, 32], F32, kind="ExternalInput").ap()
        self.s5_b_re = dt("s5_bre", [NS, 128, 512], F32, kind="ExternalInput").ap()
        self.s5_b_im = dt("s5_bim", [NS, 128, 512], F32, kind="ExternalInput").ap()
        self.s5_c_re = dt("s5_cre", [NS, 128, 512], F32, kind="ExternalInput").ap()
        self.s5_c_im = dt("s5_cim", [NS, 128, 512], F32, kind="ExternalInput").ap()
        self.s5_dcol = dt("s5_dcol", [NS, 128, 8], F32, kind="ExternalInput").ap()
        self.wglu_d = dt("wglu", [NS, 1024, 2048], F32, kind="ExternalInput").ap()
        self.wglu_b = dt("wglub", [NS, 1024, 2048], BF16, kind="Internal").ap()
        NF = 2
        self.win_d = dt("win", [NF, 16, 1024, 256], F32, kind="ExternalInput").ap()
        self.win_b = dt("winb", [NF, 16, 1024, 256], BF16, kind="Internal").ap()
        self.wf_d = dt("wf", [NF, 1024, 16], F32, kind="ExternalInput").ap()
        self.wf_b = dt("wfb", [NF, 1024, 16], BF16, kind="Internal").ap()
        self.bf_d = dt("bf", [NF, 16, 1], F32, kind="ExternalInput").ap()
        self.wout_d = dt("wout", [NF, 1024, 1024], F32, kind="ExternalInput").ap()
        self.wout_b = dt("woutb", [NF, 1024, 1024], BF16, kind="Internal").ap()
        self.tri_d = dt("tri", [128, 128], F32, kind="ExternalInput").ap()
        self.cumq = dt("cumq", [16, 3, L], BF16, kind="Internal").ap()
        self.o_d = dt("o_d", [D, L], BF16, kind="Internal").ap()
        self.ps = [nc.alloc_psum_tensor(f"ps{i}", [128, 512], F32) for i in range(8)]
        self.pst = [self.kk.tok(f"ps{i}") for i in range(8)]
        self.ones = nc.alloc_sbuf_tensor("ones", [128, 128], BF16)
        self.gcol = nc.alloc_sbuf_tensor("gcol_sb", [128, DEPTH * 4 * 8], F32)
        self.epsc = nc.alloc_sbuf_tensor("epsc", [128, 1], F32)
        self.hpic = nc.alloc_sbuf_tensor("hpic", [128, 1], F32)
        self.onec = nc.alloc_sbuf_tensor("onec", [128, 1], F32)
        self.ident = nc.alloc_sbuf_tensor("ident_sb", [128, 128], F32)
        self.ident_d = dt("ident", [128, 128], F32, kind="ExternalInput").ap()
        self.cast_tok = {}

    def build(self):
        kk, nc = self.kk, self.nc
        t_const = kk.tok("const", dma=True)
        kk.op("dve", lambda e: e.memset(self.ones[:], 1.0), writes=[t_const])
        kk.op("dve", lambda e: e.memset(self.epsc[:], EPS), writes=[t_const])
        kk.op("dve", lambda e: e.memset(self.hpic[:], float(np.pi / 2)), writes=[t_const])
        kk.op("dve", lambda e: e.memset(self.onec[:], 1.0), writes=[t_const])
        kk.dma("sp", self.gcol[:], self.gcol_d, writes=[t_const], st=t_const)
        kk.dma("sp", self.ident[:], self.ident_d, writes=[t_const], st=t_const)
        self.t_const = t_const
        layers = sorted({l for (_, l) in self.phases})
        for l in layers:
            w2cast = self.w2_b.rearrange("l c p (a m) -> l (c p a) m", a=2)
            for nm, src, dst in (("w1", self.w1_d, self.w1_b), ("w2", self.w2_d, w2cast)):
                if any(p == "mlp" and pl == l for p, pl in self.phases):
                    t = kk.tok(f"cast_{nm}{l}", dma=True)
                    kk.dma("pool", dst[l], src[l], writes=[t], st=t)
                    self.cast_tok[(nm, l)] = t
            if any(p == "s5" and pl == l for p, pl in self.phases):
                t = kk.tok(f"cast_glu{l}", dma=True)
                kk.dma("pool", self.wglu_b[l // 2], self.wglu_d[l // 2], writes=[t], st=t)
                self.cast_tok[("glu", l)] = t
            if any(p == "fox" and pl == l for p, pl in self.phases):
                jf = l // 2
                for nm, src, dst in (("win", self.win_d[jf].rearrange("h (r a) n -> (h r) (a n)", a=8),
                                      self.win_b[jf].rearrange("h (r a) n -> (h r) (a n)", a=8)),
                                     ("wf", self.wf_d[jf].rearrange("(r a) n -> r (a n)", a=128),
                                      self.wf_b[jf].rearrange("(r a) n -> r (a n)", a=128)),
                                     ("wout", self.wout_d[jf].rearrange("(r a) n -> r (a n)", a=2),
                                      self.wout_b[jf].rearrange("(r a) n -> r (a n)", a=2))):
                    t = kk.tok(f"cast_{nm}{l}", dma=True)
                    kk.dma("pool", dst, src, writes=[t], st=t)
                    self.cast_tok[(nm, l)] = t
        kk.barrier()
        cur = self.x
        for (p, l) in self.phases:
            if p == "mlp":
                self.mlp_phase(l, cur, self.out)
            elif p == "s5":
                self.s5_phase(l, cur, self.out)
            elif p == "fox":
                self.fox_phase(l, cur, self.out)
            cur = self.out
            kk.barrier()
        return nc

    def rstd_from_ss(self, ss_ps, ss_tok, tmp, tmp_tok, rstd, rstd_tok):
        kk = self.kk
        kk.op("act", lambda e: e.activation(out=tmp, in_=ss_ps, func=AF.Sqrt, bias=self.epsc[:], scale=1.0 / D),
              reads=[ss_tok, self.t_const], writes=[tmp_tok])
        kk.op("dve", lambda e: e.reciprocal(out=rstd, in_=tmp), reads=[tmp_tok], writes=[rstd_tok])

    def s5_phase(self, l, hsrc, hdst):
        kk, nc = self.kk, self.nc
        j5 = l // 2
        TT = 512
        NT = L // TT
        g0 = (l * 4 + 0) * 8
        g1 = (l * 4 + 1) * 8
        ps, pst = self.ps, self.pst
        tc = self.t_const
        with ExitStack() as es0:
            ufm = es0.enter_context(nc.sbuf_tensor(f"s_u_{l}", [128, 8, L], BF16))
            ut = [[kk.tok(f"s_u{k}_{n}") for n in range(NT)] for k in range(8)]
            with ExitStack() as es:
                def A(name, shape, dtp):
                    return es.enter_context(nc.sbuf_tensor(f"{name}_{l}", shape, dtp))
                hb = [A(f"s1_h{i}", [128, 8, TT], F32) for i in range(2)]
                hbt = [kk.tok(f"s1_h{i}", dma=True) for i in range(2)]
                sq = [A(f"s1_sq{i}", [128, 8, TT], BF16) for i in range(2)]
                sqt = [kk.tok(f"s1_sq{i}") for i in range(2)]
                tmp = A("s1_tmp", [128, TT], F32)
                tmpt = kk.tok("s1_tmp")
                rstd = [A(f"s1_rstd{i}", [128, TT], F32) for i in range(2)]
                rstdt = [kk.tok(f"s1_rstd{i}") for i in range(2)]
                for i in range(NT):
                    b = i % 2
                    kk.dma("sp", hb[b][:], hsrc[:, i * TT:(i + 1) * TT].rearrange("(k p) t -> p k t", p=128),
                           writes=[hbt[b]], st=hbt[b])
                    kk.op("act", lambda e: e.activation(out=sq[b][:], in_=hb[b][:], func=AF.Square),
                          reads=[hbt[b]], writes=[sqt[b]])
                    kk.mm_group(ps[6 + b][:], [(self.ones[:], sq[b][:, k, :]) for k in range(8)],
                                reads=[sqt[b], tc], writes=[pst[6 + b]])
                    self.rstd_from_ss(ps[6 + b][:], pst[6 + b], tmp[:], tmpt, rstd[b][:], rstdt[b])
                    for k in range(8):
                        kk.op("dve", lambda e, k=k: e.scalar_tensor_tensor(
                            out=ufm[:, k, i * TT:(i + 1) * TT], in0=hb[b][:, k, :],
                            scalar=self.gcol[:, g0 + k:g0 + k + 1], in1=rstd[b][:], op0=ALU.mult, op1=ALU.mult),
                            reads=[hbt[b], rstdt[b], tc], writes=[ut[k][i]])
            kk.barrier()
            import os
            S5STOP = os.environ.get("S5STOP", "")
            if S5STOP == "p1":
                return
            with ExitStack() as es:
                def A(name, shape, dtp):
                    return es.enter_context(nc.sbuf_tensor(f"{name}_{l}", shape, dtp))
                NSL = 48
                sc = A("s2_sc", [128, NSL, 32], F32)
                SC = {}

                def S(name):
                    if name not in SC:
                        assert len(SC) < NSL
                        SC[name] = (len(SC), kk.tok("sc_" + name, dma=name in ("atre", "atim", "ldt")))
                    i_, t_ = SC[name]
                    return sc[:, i_, :], t_

                def tt(o, a, b, op):
                    (oa, ot), (aa, at), (ba, bt) = S(o), S(a), S(b)
                    kk.op("dve", lambda e: e.tensor_tensor(out=oa, in0=aa, in1=ba, op=op), reads=[at, bt], writes=[ot])

                def tsc(o, a, s1, op0, s2=None, op1=None):
                    (oa, ot), (aa, at) = S(o), S(a)
                    if s2 is None:
                        kk.op("dve", lambda e: e.tensor_scalar(out=oa, in0=aa, scalar1=s1, scalar2=None, op0=op0),
                              reads=[at], writes=[ot])
                    else:
                        kk.op("dve", lambda e: e.tensor_scalar(out=oa, in0=aa, scalar1=s1, scalar2=s2, op0=op0, op1=op1),
                              reads=[at], writes=[ot])

                def act(o, a, func, **kw):
                    (oa, ot), (aa, at) = S(o), S(a)
                    kk.op("act", lambda e: e.activation(out=oa, in_=aa, func=func, **kw), reads=[at, tc], writes=[ot])

                for nm, src in (("atre", self.s5_at_re), ("atim", self.s5_at_im), ("ldt", self.s5_ldt)):
                    oa, ot = S(nm)
                    kk.dma("sp", oa, src[j5], writes=[ot], st=ot)
                act("dt", "ldt", AF.Exp)
                tt("lr", "atre", "dt", ALU.mult)
                tt("th", "atim", "dt", ALU.mult)
                act("rho", "lr", AF.Exp)
                act("s_0", "th", AF.Sin, scale=1.0 / 16)
                act("c_0", "th", AF.Sin, scale=1.0 / 16, bias=self.hpic[:])

                def csq(ci, si, co, so):
                    tt("cc", ci, ci, ALU.mult)
                    tt("ss", si, si, ALU.mult)
                    tt("cs", ci, si, ALU.mult)
                    tt(co, "cc", "ss", ALU.subtract)
                    tsc(so, "cs", 2.0, ALU.mult)
                csq("c_0", "s_0", "c_1", "s_1")
                csq("c_1", "s_1", "c_0", "s_0")
                csq("c_0", "s_0", "c_1", "s_1")
                csq("c_1", "s_1", "ec0", "es0")
                for k in range(1, 10):
                    csq(f"ec{k - 1}", f"es{k - 1}", f"ec{k}", f"es{k}")
                tt("abr", "rho", "ec0", ALU.mult)
                tt("abi", "rho", "es0", ALU.mult)
                tsc("am1", "abr", -1.0, ALU.add)
                tt("den", "atre", "atre", ALU.mult)
                tt("d2", "atim", "atim", ALU.mult)
                tt("den", "den", "d2", ALU.add)
                (oa, ot), (aa, at) = S("rden"), S("den")
                kk.op("dve", lambda e: e.reciprocal(out=oa, in_=aa), reads=[at], writes=[ot])
                tt("nr", "am1", "atre", ALU.mult)
                tt("t", "abi", "atim", ALU.mult)
                tt("nr", "nr", "t", ALU.add)
                tt("ni", "abi", "atre", ALU.mult)
                tt("t", "am1", "atim", ALU.mult)
                tt("ni", "ni", "t", ALU.subtract)
                tt("kr", "nr", "rden", ALU.mult)
                tt("ki", "ni", "rden", ALU.mult)

                def col(name, q):
                    i_, t_ = SC[name]
                    return sc[:, i_, q:q + 1], t_
                if S5STOP == "sc":
                    return

                bre = A("s2_bre", [128, 32, 16], F32)
                bim = A("s2_bim", [128, 32, 16], F32)
                cre = A("s2_cre", [128, 32, 16], F32)
                cim = A("s2_cim", [128, 32, 16], F32)
                dcol = A("s2_dcol", [128, 8], F32)
                pt = kk.tok("s2_par", dma=True)
                for dst, src in ((bre, self.s5_b_re), (bim, self.s5_b_im), (cre, self.s5_c_re), (cim, self.s5_c_im)):
                    kk.dma("sp", dst[:].rearrange("p q h -> p (q h)"), src[j5], writes=[pt], st=pt)
                kk.dma("sp", dcol[:], self.s5_dcol[j5], writes=[pt], st=pt)
                wtmp = [[A(f"s2_wt{c}{m}", [128, 128], F32) for m in range(4)] for c in range(2)]
                wtmpt = [[kk.tok(f"s2_wt{c}{m}") for m in range(4)] for c in range(2)]
                WB = [[A(f"s2_wb{c}{m}", [128, 128], BF16) for m in range(4)] for c in range(2)]
                WBt = [[kk.tok(f"s2_wb{c}{m}") for m in range(4)] for c in range(2)]
                CW = [[A(f"s2_cw{c}{m}", [128, 128], BF16) for m in range(4)] for c in range(2)]
                CWt = [[kk.tok(f"s2_cw{c}{m}") for m in range(4)] for c in range(2)]
                t16 = [A(f"s2_t16{i}", [128, 16], F32) for i in range(2)]
                t16t = [kk.tok(f"s2_t16{i}") for i in range(2)]
                tabc = [A(f"s2_tc{m}", [128, 512], F32) for m in range(4)]
                tabs = [A(f"s2_ts{m}", [128, 512], F32) for m in range(4)]
                tabt = [kk.tok(f"s2_tab{m}") for m in range(4)]
                tw = [A(f"s2_tw{i}", [128, 256], F32) for i in range(2)]
                twt = [kk.tok(f"s2_tw{i}") for i in range(2)]
                rbc = [A(f"s2_rb{m}", [128, 512], F32) for m in range(4)]
                rbt = [kk.tok(f"s2_rb{m}") for m in range(4)]
                onesf = A("s2_onesf", [128, 512], F32)
                ini = A("s2_ini", [128, 4, 2], F32)
                init = [kk.tok(f"s2_ini{m}") for m in range(4)]
                itmp = A("s2_itmp", [128, 4, 2], F32)
                T = [[A(f"s2_T{pb}{i}", [128, 512], F32) for i in range(8)] for pb in range(2)]
                Tt = [[kk.tok(f"s2_T{pb}{i}") for i in range(8)] for pb in range(2)]
                U = [A(f"s2_U{i}", [128, 512], F32) for i in range(4)]
                Ut = [kk.tok(f"s2_U{i}") for i in range(4)]
                xb = [[A(f"s2_x{pb}{c}", [128, 512], BF16) for c in range(2)] for pb in range(2)]
                xbt = [[kk.tok(f"s2_x{pb}{c}") for c in range(2)] for pb in range(2)]
                yv = [A(f"s2_yv{i}", [128, 512], F32) for i in range(2)]
                yvt = [kk.tok(f"s2_yv{i}") for i in range(2)]
                for c in range(2):
                    for m in range(4):
                        kk.op("pool", lambda e, c=c, m=m: e.memset(wtmp[c][m][:], 0.0), writes=[wtmpt[c][m]])
                        kk.op("pool", lambda e, c=c, m=m: e.memset(CW[c][m][:], 0.0), writes=[CWt[c][m]])
                for m in range(4):
                    kk.op("pool", lambda e, m=m: e.memset(tabc[m][:, 0:1], 1.0), writes=[tabt[m]])
                    kk.op("pool", lambda e, m=m: e.memset(tabs[m][:, 0:1], 0.0), writes=[tabt[m]])
                kk.op("pool", lambda e: e.memset(onesf[:], 1.0), writes=[tc])

                pcount = 0
                for k in range(8):
                    for m in range(4):
                        q = 4 * k + m
                        (kr, krt), (ki, kit) = col("kr", q), col("ki", q)
                        kk.op("dve", lambda e: e.tensor_scalar(out=t16[0][:], in0=bim[:, q, :], scalar1=ki, scalar2=None,
                                                               op0=ALU.mult), reads=[pt, kit], writes=[t16t[0]])
                        kk.op("dve", lambda e: e.tensor_scalar(out=t16[1][:], in0=bim[:, q, :], scalar1=kr, scalar2=None,
                                                               op0=ALU.mult), reads=[pt, krt], writes=[t16t[1]])
                        for i2 in range(2):
                            r0, r1 = 64 * i2, 64 * i2 + 64
                            c0 = 32 * m + 16 * i2
                            kk.op("dve", lambda e: e.scalar_tensor_tensor(
                                out=wtmp[0][m][r0:r1, c0:c0 + 16], in0=bre[r0:r1, q, :], scalar=kr[r0:r1, :],
                                in1=t16[0][r0:r1, :], op0=ALU.mult, op1=ALU.subtract),
                                reads=[pt, krt, t16t[0]], writes=[wtmpt[0][m]])
                            kk.op("dve", lambda e: e.scalar_tensor_tensor(
                                out=wtmp[1][m][r0:r1, c0:c0 + 16], in0=bre[r0:r1, q, :], scalar=ki[r0:r1, :],
                                in1=t16[1][r0:r1, :], op0=ALU.mult, op1=ALU.add),
                                reads=[pt, kit, t16t[1]], writes=[wtmpt[1][m]])
                            kk.op("pool", lambda e: e.tensor_copy(out=CW[0][m][r0:r1, c0:c0 + 16], in_=cre[r0:r1, q, :]),
                                  reads=[pt], writes=[CWt[0][m]])
                            kk.op("pool", lambda e: e.tensor_scalar(out=CW[1][m][r0:r1, c0:c0 + 16], in0=cim[r0:r1, q, :],
                                                                    scalar1=-1.0, scalar2=None, op0=ALU.mult),
                                  reads=[pt], writes=[CWt[1][m]])
                        if S5STOP == "wg1":
                            return
                        for c in range(2):
                            kk.op("pe", lambda e: e.transpose(out=ps[6][:, 0:128], in_=wtmp[c][m][:], identity=self.ident[:]),
                                  reads=[wtmpt[c][m], tc], writes=[pst[6]])
                            kk.op("act", lambda e: e.activation(out=WB[c][m][:], in_=ps[6][:, 0:128], func=AF.Copy),
                                  reads=[pst[6]], writes=[WBt[c][m]])
                        if S5STOP == "wg2":
                            return
                        for lv in range(9):
                            w = 1 << lv
                            (ec, ect), (esn, est) = col(f"ec{lv}", q), col(f"es{lv}", q)
                            kk.op("dve", lambda e: e.tensor_scalar(out=tw[0][:, 0:w], in0=tabs[m][:, 0:w], scalar1=esn,
                                                                   scalar2=None, op0=ALU.mult),
                                  reads=[tabt[m], est], writes=[twt[0]])
                            kk.op("dve", lambda e: e.tensor_scalar(out=tw[1][:, 0:w], in0=tabs[m][:, 0:w], scalar1=ec,
                                                                   scalar2=None, op0=ALU.mult),
                                  reads=[tabt[m], ect], writes=[twt[1]])
                            kk.op("dve", lambda e: e.scalar_tensor_tensor(
                                out=tabs[m][:, w:2 * w], in0=tabc[m][:, 0:w], scalar=esn, in1=tw[1][:, 0:w],
                                op0=ALU.mult, op1=ALU.add), reads=[twt[1], est], writes=[tabt[m]])
                            kk.op("dve", lambda e: e.scalar_tensor_tensor(
                                out=tabc[m][:, w:2 * w], in0=tabc[m][:, 0:w], scalar=ec, in1=tw[0][:, 0:w],
                                op0=ALU.mult, op1=ALU.subtract), reads=[twt[0], ect], writes=[tabt[m]])
                        if S5STOP == "wg3":
                            return
                        (rh, rht) = col("rho", q)
                        kk.op("dve", lambda e: e.tensor_scalar(out=rbc[m][:], in0=onesf[:], scalar1=rh, scalar2=None,
                                                               op0=ALU.mult), reads=[rht, tc], writes=[rbt[m]])
                        if S5STOP == "wg4":
                            return
                        kk.op("dve", lambda e: e.memset(ini[:, m, :], 0.0), writes=[init[m]])
                        if S5STOP == "m0":
                            return
                        if S5STOP in ("m1", "m2", "m3") and m == int(S5STOP[1]):
                            return
                    if S5STOP == "wgen":
                        return
                    items = [(n, m) for n in range(NT) for m in range(4)]

                    def stA(it, n, m):
                        cs_ = slice(n * TT, (n + 1) * TT)
                        pb = it % 2
                        pa, pbk = (0, 1) if pb == 0 else (2, 3)
                        Tb, Ttb = T[pb], Tt[pb]
                        kk.mm_group(ps[pa][:], [(WB[0][m][:], ufm[:, k, cs_])], reads=[WBt[0][m], ut[k][n]],
                                    writes=[pst[pa]])
                        kk.mm_group(ps[pbk][:], [(WB[1][m][:], ufm[:, k, cs_])], reads=[WBt[1][m], ut[k][n]],
                                    writes=[pst[pbk]])

                    def stA2(it, n, m):
                        pb = it % 2
                        pa, pbk = (0, 1) if pb == 0 else (2, 3)
                        Tb, Ttb = T[pb], Tt[pb]
                        for (ti, pp, tab) in ((0, pa, tabc), (1, pbk, tabs), (2, pbk, tabc), (3, pa, tabs)):
                            kk.op("dve", lambda e, ti=ti, pp=pp, tab=tab: e.tensor_tensor(
                                out=Tb[ti][:], in0=ps[pp][:], in1=tab[m][:], op=ALU.mult),
                                reads=[pst[pp], tabt[m]], writes=[Ttb[ti]])

                    def stB(it, n, m):
                        Tb, Ttb = T[it % 2], Tt[it % 2]
                        kk.op("dve", lambda e: e.tensor_tensor(out=Tb[4][:], in0=Tb[0][:], in1=Tb[1][:], op=ALU.add),
                              reads=[Ttb[0], Ttb[1]], writes=[Ttb[4]])
                        kk.op("dve", lambda e: e.tensor_tensor(out=Tb[5][:], in0=Tb[2][:], in1=Tb[3][:], op=ALU.subtract),
                              reads=[Ttb[2], Ttb[3]], writes=[Ttb[5]])

                    def stC(it, n, m):
                        q = 4 * k + m
                        Tb, Ttb = T[it % 2], Tt[it % 2]
                        kk.op("dve", lambda e: e.tensor_tensor_scan(out=Tb[6][:], data0=rbc[m][:], data1=Tb[4][:],
                                                                    initial=ini[:, m, 0:1], op0=ALU.mult, op1=ALU.add),
                              reads=[rbt[m], Ttb[4], init[m]], writes=[Ttb[6]])
                        kk.op("dve", lambda e: e.tensor_tensor_scan(out=Tb[7][:], data0=rbc[m][:], data1=Tb[5][:],
                                                                    initial=ini[:, m, 1:2], op0=ALU.mult, op1=ALU.add),
                              reads=[rbt[m], Ttb[5], init[m]], writes=[Ttb[7]])
                        (e9c, e9ct), (e9s, e9st) = col("ec9", q), col("es9", q)
                        kk.op("dve", lambda e: e.tensor_scalar(out=itmp[:, m, 0:1], in0=Tb[7][:, TT - 1:TT], scalar1=e9s,
                                                               scalar2=None, op0=ALU.mult),
                              reads=[Ttb[7], e9st], writes=[init[m]])
                        kk.op("dve", lambda e: e.tensor_scalar(out=itmp[:, m, 1:2], in0=Tb[7][:, TT - 1:TT], scalar1=e9c,
                                                               scalar2=None, op0=ALU.mult),
                              reads=[Ttb[7], e9ct], writes=[init[m]])
                        kk.op("dve", lambda e: e.scalar_tensor_tensor(
                            out=ini[:, m, 0:1], in0=Tb[6][:, TT - 1:TT], scalar=e9c, in1=itmp[:, m, 0:1],
                            op0=ALU.mult, op1=ALU.subtract), reads=[Ttb[6], e9ct], writes=[init[m]])
                        kk.op("dve", lambda e: e.scalar_tensor_tensor(
                            out=ini[:, m, 1:2], in0=Tb[6][:, TT - 1:TT], scalar=e9s, in1=itmp[:, m, 1:2],
                            op0=ALU.mult, op1=ALU.add), reads=[Ttb[6], e9st], writes=[init[m]])

                    def stD(it, n, m):
                        pb = it % 2
                        Tb, Ttb = T[pb], Tt[pb]
                        for (ti, zi_, tab) in ((0, 6, tabc), (1, 7, tabs), (2, 6, tabs), (3, 7, tabc)):
                            kk.op("pool", lambda e, ti=ti, zi_=zi_, tab=tab: e.tensor_tensor(
                                out=U[ti][:], in0=Tb[zi_][:], in1=tab[m][:], op=ALU.mult),
                                reads=[Ttb[zi_], tabt[m]], writes=[Ut[ti]])
                        kk.op("pool", lambda e: e.tensor_tensor(out=xb[pb][0][:], in0=U[0][:], in1=U[1][:], op=ALU.subtract),
                              reads=[Ut[0], Ut[1]], writes=[xbt[pb][0]])
                        kk.op("pool", lambda e: e.tensor_tensor(out=xb[pb][1][:], in0=U[2][:], in1=U[3][:], op=ALU.add),
                              reads=[Ut[2], Ut[3]], writes=[xbt[pb][1]])

                    def stE(it, n, m):
                        cs_ = slice(n * TT, (n + 1) * TT)
                        pb = it % 2
                        py = 4 + (n % 2)
                        e_pe = kk.engs["pe"]
                        kk._deps(e_pe, [CWt[0][m], CWt[1][m], xbt[pb][0], xbt[pb][1]], [pst[py]] if m == 0 else [])
                        e_pe.obj.matmul(ps[py][:], CW[0][m][:], xb[pb][0][:], start=(m == 0), stop=False)
                        inst = e_pe.obj.matmul(ps[py][:], CW[1][m][:], xb[pb][1][:], start=False, stop=(m == 3))
                        e_pe.cnt += 1
                        inst.then_inc(e_pe.sem, 1)
                        kk._reg((e_pe.sem, e_pe.cnt), [CWt[0][m], CWt[1][m], xbt[pb][0], xbt[pb][1]], [pst[py]])
                        if m == 3:
                            yb_ = n % 2
                            kk.op("dve", lambda e: e.scalar_tensor_tensor(
                                out=yv[yb_][:], in0=ufm[:, k, cs_], scalar=dcol[:, k:k + 1], in1=ps[py][:],
                                op0=ALU.mult, op1=ALU.add), reads=[ut[k][n], pt, pst[py]], writes=[yvt[yb_]])
                            kk.op("act", lambda e: e.activation(out=ufm[:, k, cs_], in_=yv[yb_][:],
                                                                func=AF.Gelu_apprx_tanh),
                                  reads=[yvt[yb_]], writes=[ut[k][n]])

                    NI = len(items)
                    stA(0, *items[0])
                    for s_ in range(NI + 1):
                        if s_ + 1 < NI:
                            stA(s_ + 1, *items[s_ + 1])
                        if s_ < NI:
                            stA2(s_, *items[s_])
                            stB(s_, *items[s_])
                        if s_ >= 1:
                            stC(s_ - 1, *items[s_ - 1])
                            stD(s_ - 1, *items[s_ - 1])
                            stE(s_ - 1, *items[s_ - 1])
                    if S5STOP == "k0":
                        return
            kk.barrier()
            if S5STOP == "p2":
                return
            with ExitStack() as es:
                def A(name, shape, dtp):
                    return es.enter_context(nc.sbuf_tensor(f"{name}_{l}", shape, dtp))
                wg = A("s3_wg", [128, 8, 2048], BF16)
                wgt = kk.tok("s3_wg", dma=True)
                cg = self.cast_tok[("glu", l)]
                for k in range(8):
                    kk.dma("sp", wg[:, k, :], self.wglu_b[j5][k * 128:(k + 1) * 128, :], reads=[cg], writes=[wgt], st=wgt)
                hb = [A(f"s3_h{i}", [128, 8, TT], F32) for i in range(2)]
                hbt = [kk.tok(f"s3_h{i}", dma=True) for i in range(2)]
                hst = [kk.tok(f"s3_hs{i}", dma=True) for i in range(2)]
                yb = A("s3_y", [128, 8, TT], F32)
                ybt = [kk.tok(f"s3_y{c}") for c in range(8)]
                ysq = A("s3_ysq", [128, 8, TT], BF16)
                ysqt = [kk.tok(f"s3_ysq{c}") for c in range(8)]
                sg = [A(f"s3_sg{i}", [128, TT], F32) for i in range(2)]
                sgt = [kk.tok(f"s3_sg{i}") for i in range(2)]
                tmp = A("s3_tmp", [128, TT], F32)
                tmpt = kk.tok("s3_tmp")
                rstd = A("s3_rstd", [128, TT], F32)
                rstdt = kk.tok("s3_rstd")
                for i in range(NT):
                    b = i % 2
                    cs_ = slice(i * TT, (i + 1) * TT)
                    kk.dma("sp", hb[b][:], hsrc[:, cs_].rearrange("(k p) t -> p k t", p=128), writes=[hbt[b]], st=hbt[b])
                    for c in range(8):
                        pv, pg = (0, 1) if c % 2 == 0 else (2, 3)
                        kk.mm_group(ps[pv][:], [(wg[:, k, c * 128:(c + 1) * 128], ufm[:, k, cs_]) for k in range(8)],
                                    reads=[wgt] + [ut[k][i] for k in range(8)], writes=[pst[pv]])
                        kk.mm_group(ps[pg][:], [(wg[:, k, 1024 + c * 128:1024 + (c + 1) * 128], ufm[:, k, cs_])
                                                for k in range(8)],
                                    reads=[wgt] + [ut[k][i] for k in range(8)], writes=[pst[pg]])
                        s_ = c % 2
                        kk.op("act", lambda e: e.activation(out=sg[s_][:], in_=ps[pg][:], func=AF.Sigmoid),
                              reads=[pst[pg]], writes=[sgt[s_]])
                        kk.op("dve", lambda e: e.tensor_tensor(out=yb[:, c, :], in0=ps[pv][:], in1=sg[s_][:], op=ALU.mult),
                              reads=[pst[pv], sgt[s_]], writes=[ybt[c]])
                        kk.op("act", lambda e: e.activation(out=ysq[:, c, :], in_=yb[:, c, :], func=AF.Square),
                              reads=[ybt[c]], writes=[ysqt[c]])
                    kk.mm_group(ps[6][:], [(self.ones[:], ysq[:, c, :]) for c in range(8)],
                                reads=ysqt + [tc], writes=[pst[6]])
                    self.rstd_from_ss(ps[6][:], pst[6], tmp[:], tmpt, rstd[:], rstdt)
                    for c in range(8):
                        kk.op("dve", lambda e, c=c: e.scalar_tensor_tensor(
                            out=yb[:, c, :], in0=yb[:, c, :], scalar=self.gcol[:, g1 + c:g1 + c + 1],
                            in1=rstd[:], op0=ALU.mult, op1=ALU.mult),
                            reads=[rstdt, tc], writes=[ybt[c]])
                    kk.op("pool", lambda e: e.tensor_tensor(out=hb[b][:], in0=hb[b][:], in1=yb[:], op=ALU.add),
                          reads=ybt, writes=[hbt[b]])
                    kk.dma("sp", hdst[:, cs_].rearrange("(k p) t -> p k t", p=128), hb[b][:],
                           reads=[hbt[b]], writes=[hst[b]], st=hst[b])

    def norm_full(self, l, hsrc, gidx, ufm, ut, pfx):
        kk, nc = self.kk, self.nc
        TT = 512
        NT = L // TT
        ps, pst, tc = self.ps, self.pst, self.t_const
        with ExitStack() as es:
            def A(name, shape, dtp):
                return es.enter_context(nc.sbuf_tensor(f"{pfx}{name}_{l}", shape, dtp))
            hb = [A(f"h{i}", [128, 8, TT], F32) for i in range(2)]
            hbt = [kk.tok(f"{pfx}h{i}", dma=True) for i in range(2)]
            sq = [A(f"sq{i}", [128, 8, TT], BF16) for i in range(2)]
            sqt = [kk.tok(f"{pfx}sq{i}") for i in range(2)]
            tmp = A("tmp", [128, TT], F32)
            tmpt = kk.tok(f"{pfx}tmp")
            rstd = [A(f"rstd{i}", [128, TT], F32) for i in range(2)]
            rstdt = [kk.tok(f"{pfx}rstd{i}") for i in range(2)]
            for i in range(NT):
                b = i % 2
                kk.dma("sp", hb[b][:], hsrc[:, i * TT:(i + 1) * TT].rearrange("(k p) t -> p k t", p=128),
                       writes=[hbt[b]], st=hbt[b])
                kk.op("act", lambda e: e.activation(out=sq[b][:], in_=hb[b][:], func=AF.Square),
                      reads=[hbt[b]], writes=[sqt[b]])
                kk.mm_group(ps[6 + b][:], [(self.ones[:], sq[b][:, k, :]) for k in range(8)],
                            reads=[sqt[b], tc], writes=[pst[6 + b]])
                self.rstd_from_ss(ps[6 + b][:], pst[6 + b], tmp[:], tmpt, rstd[b][:], rstdt[b])
                for k in range(8):
                    kk.op("dve", lambda e, k=k: e.scalar_tensor_tensor(
                        out=ufm[:, k, i * TT:(i + 1) * TT], in0=hb[b][:, k, :],
                        scalar=self.gcol[:, gidx + k:gidx + k + 1], in1=rstd[b][:], op0=ALU.mult, op1=ALU.mult),
                        reads=[hbt[b], rstdt[b], tc], writes=[ut[k][i]])
        kk.barrier()

    def fox_phase(self, l, hsrc, hdst):
        kk, nc = self.kk, self.nc
        jf = l // 2
        TT = 512
        NT = L // TT
        g0 = (l * 4 + 0) * 8
        g1 = (l * 4 + 1) * 8
        ps, pst, tc = self.ps, self.pst, self.t_const
        o_d = self.o_d
        odt = kk.tok("o_d")
        with ExitStack() as es0:
            ufm = es0.enter_context(nc.sbuf_tensor(f"f_u_{l}", [128, 8, L], BF16))
            ut = [[kk.tok(f"f_u{k}_{n}") for n in range(NT)] for k in range(8)]
            uall = [ut[k][n] for k in range(8) for n in range(NT)]
            self.norm_full(l, hsrc, g0, ufm, ut, "f1_")
            negcT = es0.enter_context(nc.sbuf_tensor(f"f_negcT_{l}", [128, 32, 16], F32))
            negct = kk.tok("f_negcT")
            cq_t = kk.tok("f_cq", dma=True)
            with ExitStack() as es:
                def A(name, shape, dtp):
                    return es.enter_context(nc.sbuf_tensor(f"{name}_{l}", shape, dtp))
                wf = A("f2_wf", [128, 8, 16], BF16)
                wft = kk.tok("f2_wf", dma=True)
                kk.dma("sp", wf[:], self.wf_b[jf].rearrange("(k p) n -> p k n", p=128), reads=[self.cast_tok[("wf", l)]],
                       writes=[wft], st=wft)
                bf_ = A("f2_bf", [16, 1], F32)
                nbf = A("f2_nbf", [16, 1], F32)
                bft = kk.tok("f2_bf", dma=True)
                kk.dma("sp", bf_[:], self.bf_d[jf], writes=[bft], st=bft)
                kk.op("dve", lambda e: e.tensor_scalar(out=nbf[:], in0=bf_[:], scalar1=-1.0, scalar2=None, op0=ALU.mult),
                      reads=[bft], writes=[bft])
                cum = A("f2_cum", [16, L], F32)
                cumt = kk.tok("f2_cum")
                ex = [A(f"f2_ex{i}", [16, TT], F32) for i in range(2)]
                ext = [kk.tok(f"f2_ex{i}") for i in range(2)]
                one16 = A("f2_one16", [16, TT], F32)
                kk.op("pool", lambda e: e.memset(one16[:], 1.0), writes=[tc])
                zero1 = A("f2_zero", [16, 1], F32)
                kk.op("pool", lambda e: e.memset(zero1[:], 0.0), writes=[tc])
                for n in range(NT):
                    cs_ = slice(n * TT, (n + 1) * TT)
                    b = n % 2
                    kk.mm_group(ps[b][0:16, :], [(wf[:, k, :], ufm[:, k, cs_]) for k in range(8)],
                                reads=[wft] + [ut[k][n] for k in range(8)], writes=[pst[b]])
                    kk.op("act", lambda e: e.activation(out=ex[b][:], in_=ps[b][0:16, :], func=AF.Exp, scale=-1.0,
                                                        bias=nbf[:]), reads=[pst[b], bft], writes=[ext[b]])
                    kk.op("act", lambda e: e.activation(out=ex[b][:], in_=ex[b][:], func=AF.Ln, bias=self.onec[0:16, :]),
                          reads=[ext[b], tc], writes=[ext[b]])
                    kk.op("dve", lambda e: e.tensor_scalar(out=ex[b][:], in0=ex[b][:], scalar1=-1.0, scalar2=None,
                                                           op0=ALU.mult), reads=[ext[b]], writes=[ext[b]])
                    kk.op("dve", lambda e: e.tensor_tensor_scan(
                        out=cum[:, cs_], data0=one16[:], data1=ex[b][:],
                        initial=(zero1[:] if n == 0 else cum[:, n * TT - 1:n * TT]), op0=ALU.mult, op1=ALU.add),
                        reads=[ext[b], tc, cumt], writes=[cumt])
                for tb in range(32):
                    kk.op("pe", lambda e, tb=tb: e.transpose(out=ps[6][:, tb * 16:(tb + 1) * 16],
                                                             in_=cum[:, tb * 128:(tb + 1) * 128],
                                                             identity=self.ident[0:16, 0:16]),
                          reads=[cumt, tc], writes=[pst[6]])
                kk.op("dve", lambda e: e.tensor_scalar(out=negcT[:].rearrange("p a b -> p (a b)"), in0=ps[6][:],
                                                       scalar1=-1.0, scalar2=None, op0=ALU.mult),
                      reads=[pst[6]], writes=[negct])
                c8 = A("f2_c8", [16, L], F32)
                c8t = kk.tok("f2_c8")
                cp = [A(f"f2_cp{i}", [16, L], BF16) for i in range(3)]
                cpt = [kk.tok(f"f2_cp{i}", dma=True) for i in range(3)]
                kk.op("dve", lambda e: e.tensor_scalar(out=c8[:], in0=cum[:], scalar1=8.0, scalar2=None, op0=ALU.mult),
                      reads=[cumt], writes=[c8t])
                for i in range(3):
                    kk.op("dve", lambda e, i=i: e.tensor_copy(out=cp[i][:], in_=c8[:]), reads=[c8t], writes=[cpt[i]])
                    if i < 2:
                        kk.op("dve", lambda e, i=i: e.tensor_tensor(out=c8[:], in0=c8[:], in1=cp[i][:], op=ALU.subtract),
                              reads=[cpt[i]], writes=[c8t])
                    kk.dma("sp", self.cumq[:, i, :], cp[i][:], reads=[cpt[i]], writes=[cq_t], st=cpt[i])
            kk.barrier()
            with ExitStack() as es:
                def A(name, shape, dtp):
                    return es.enter_context(nc.sbuf_tensor(f"{name}_{l}", shape, dtp))
                qa = [A(f"f3_q{i}", [128, L], BF16) for i in range(2)]
                ka = [A(f"f3_k{i}", [128, L], BF16) for i in range(2)]
                qat = [kk.tok(f"f3_q{i}", dma=True) for i in range(2)]
                kat = [kk.tok(f"f3_k{i}") for i in range(2)]
                va = [A(f"f3_v{i}", [128, 32, 65], BF16) for i in range(2)]
                vat = [kk.tok(f"f3_v{i}") for i in range(2)]
                wh = [A(f"f3_w{i}", [128, 8, 256], BF16) for i in range(2)]
                wht = [kk.tok(f"f3_w{i}", dma=True) for i in range(2)]
                P = [A(f"f3_P{i}", [128, TT], BF16) for i in range(4)]
                Pt = [kk.tok(f"f3_P{i}") for i in range(4)]
                eg = A("f3_eg", [64, TT], F32)
                egt = kk.tok("f3_eg")
                rs = A("f3_rs", [128, TT], F32)
                rst = kk.tok("f3_rs")
                den = A("f3_den", [64, TT], F32)
                dent = kk.tok("f3_den")
                osb = [A(f"f3_o{i}", [64, TT], BF16) for i in range(2)]
                osbt = [kk.tok(f"f3_o{i}", dma=True) for i in range(2)]
                tri = A("f3_tri", [128, 128], BF16)
                trif = A("f3_trif", [128, 128], F32)
                trit = kk.tok("f3_tri", dma=True)
                onesf = A("f3_onesf", [128, 64], F32)
                kk.op("pool", lambda e: e.memset(onesf[:], 1.0), writes=[tc])
                kk.dma("sp", trif[:], self.tri_d, writes=[trit], st=trit)
                kk.op("dve", lambda e: e.tensor_copy(out=tri[:], in_=trif[:]), reads=[trit], writes=[trit])
                for i in range(2):
                    kk.op("pool", lambda e, i=i: e.memset(ka[i][64:96, :], 1.0), writes=[kat[i]])
                    kk.op("pool", lambda e, i=i: e.memset(va[i][:, :, 64:65], 1.0), writes=[vat[i]])
                cw = self.cast_tok[("win", l)]
                pcount = 0
                for h in range(16):
                    hb_ = h % 2
                    q_, k_, v_, w_ = qa[hb_], ka[hb_], va[hb_], wh[hb_]
                    kk.dma("sp", w_[:], self.win_b[jf, h].rearrange("(k p) n -> p k n", p=128), reads=[cw],
                           writes=[wht[hb_]], st=wht[hb_])
                    kk.dma("sp", q_[64:67, :], self.cumq[h], reads=[cq_t], writes=[qat[hb_]], st=qat[hb_])
                    for n in range(NT):
                        cs_ = slice(n * TT, (n + 1) * TT)
                        un = [ut[k][n] for k in range(8)]
                        kk.mm_group(ps[3][0:64, :], [(w_[:, k, 0:64], ufm[:, k, cs_]) for k in range(8)],
                                    reads=[wht[hb_]] + un, writes=[pst[3]])
                        kk.op("act", lambda e: e.activation(out=q_[0:64, cs_], in_=ps[3][0:64, :], func=AF.Copy),
                              reads=[pst[3]], writes=[qat[hb_]])
                        kk.mm_group(ps[7][0:64, :], [(w_[:, k, 64:128], ufm[:, k, cs_]) for k in range(8)],
                                    reads=[wht[hb_]] + un, writes=[pst[7]])
                        kk.op("dve", lambda e: e.tensor_copy(out=k_[0:64, cs_], in_=ps[7][0:64, :]),
                              reads=[pst[7]], writes=[kat[hb_]])
                    for tg in range(4):
                        for t8 in range(8):
                            tb = tg * 8 + t8
                            e_pe = kk.engs["pe"]
                            rd = [wht[hb_]] + [ut[k][tb // 4] for k in range(8)]
                            kk._deps(e_pe, rd, [pst[7]] if t8 == 0 else [])
                            inst = None
                            for k in range(8):
                                inst = e_pe.obj.matmul(ps[7][:, t8 * 64:(t8 + 1) * 64], ufm[:, k, tb * 128:(tb + 1) * 128],
                                                       w_[:, k, 128:192], start=(k == 0), stop=(k == 7))
                            e_pe.cnt += 1
                            inst.then_inc(e_pe.sem, 1)
                            kk._reg((e_pe.sem, e_pe.cnt), rd, [pst[7]])
                        kk.op("dve", lambda e: e.tensor_copy(out=v_[:, tg * 8:(tg + 1) * 8, 0:64],
                                                             in_=ps[7][:].rearrange("p (a b) -> p a b", b=64)),
                              reads=[pst[7]], writes=[vat[hb_]])
                    for qc in range(NT):
                        qs = slice(qc * TT, (qc + 1) * TT)
                        po = 4 + (qc % 2)
                        un = [ut[k][qc] for k in range(8)]
                        kk.mm_group(ps[3][0:64, :], [(w_[:, k, 192:256], ufm[:, k, qs]) for k in range(8)],
                                    reads=[wht[hb_]] + un, writes=[pst[3]])
                        kk.op("act", lambda e: e.activation(out=eg[:], in_=ps[3][0:64, :], func=AF.Exp, scale=-1.0),
                              reads=[pst[3]], writes=[egt])
                        nkt = 4 * qc + 4
                        slots = {}

                        def issue_S(kt):
                            nonlocal pcount
                            j = kt - 4 * qc
                            c0 = max(0, j) * 128
                            pb = pcount % 3
                            pp = pcount % 4
                            pcount += 1
                            slots[kt] = (c0, pp)
                            kk.mm_group(ps[pb][:, c0:TT],
                                        [(k_[0:67, kt * 128:(kt + 1) * 128], q_[0:67, qc * TT + c0:(qc + 1) * TT])],
                                        reads=[kat[hb_], qat[hb_]], writes=[pst[pb]])
                            kk.op("act", lambda e: e.activation(out=P[pp][:, c0:TT], in_=ps[pb][:, c0:TT], func=AF.Exp,
                                                                scale=0.125, bias=negcT[:, kt, h:h + 1]),
                                  reads=[pst[pb], negct], writes=[Pt[pp]])
                            if j >= 0:
                                kk.op("pool", lambda e: e.tensor_tensor(out=P[pp][:, c0:c0 + 128], in0=P[pp][:, c0:c0 + 128],
                                                                        in1=tri[:], op=ALU.mult),
                                      reads=[trit], writes=[Pt[pp]])
                        issue_S(0)
                        if nkt > 1:
                            issue_S(1)
                        for kt in range(nkt):
                            if kt + 2 < nkt:
                                issue_S(kt + 2)
                            c0, pp = slots[kt]
                            e_pe = kk.engs["pe"]
                            kk._deps(e_pe, [vat[hb_], Pt[pp]], [pst[po]] if kt == 0 else [])
                            inst = e_pe.obj.matmul(ps[po][0:65, c0:TT], v_[:, kt, 0:65], P[pp][:, c0:TT],
                                                   start=(kt == 0), stop=(kt == nkt - 1))
                            e_pe.cnt += 1
                            inst.then_inc(e_pe.sem, 1)
                            kk._reg((e_pe.sem, e_pe.cnt), [vat[hb_], Pt[pp]], [pst[po]])
                        ob = qc % 2
                        kk.op("dve", lambda e: e.tensor_copy(out=rs[64:65, :], in_=ps[po][64:65, :]),
                              reads=[pst[po]], writes=[rst])
                        kk.mm_group(ps[6][0:64, :], [(onesf[64:65, 0:64], rs[64:65, :])], reads=[rst, tc], writes=[pst[6]])
                        kk.op("dve", lambda e: e.scalar_tensor_tensor(out=den[:], in0=eg[:], scalar=1.0, in1=ps[6][0:64, :],
                                                                      op0=ALU.add, op1=ALU.mult),
                              reads=[egt, pst[6]], writes=[dent])
                        kk.op("dve", lambda e: e.reciprocal(out=den[:], in_=den[:]), reads=[dent], writes=[dent])
                        kk.op("dve", lambda e: e.tensor_tensor(out=osb[ob][:], in0=ps[po][0:64, :], in1=den[:], op=ALU.mult),
                              reads=[pst[po], dent], writes=[osbt[ob]])
                        kk.dma("sp", o_d[h * 64:(h + 1) * 64, qs], osb[ob][:], reads=[osbt[ob]], writes=[odt], st=osbt[ob])
        kk.barrier()
        with ExitStack() as es:
            def A(name, shape, dtp):
                return es.enter_context(nc.sbuf_tensor(f"{name}_{l}", shape, dtp))
            wo = A("f4_wo", [128, 8, 1024], BF16)
            wot = kk.tok("f4_wo", dma=True)
            kk.dma("sp", wo[:], self.wout_b[jf].rearrange("(k p) n -> p k n", p=128), reads=[self.cast_tok[("wout", l)]],
                   writes=[wot], st=wot)
            ob = [A(f"f4_o{i}", [128, 8, TT], BF16) for i in range(2)]
            obt = [kk.tok(f"f4_o{i}", dma=True) for i in range(2)]
            hb = [A(f"f4_h{i}", [128, 8, TT], F32) for i in range(2)]
            hbt = [kk.tok(f"f4_h{i}", dma=True) for i in range(2)]
            hst = [kk.tok(f"f4_hs{i}", dma=True) for i in range(2)]
            yb = A("f4_y", [128, 8, TT], F32)
            ybt = [kk.tok(f"f4_y{c}") for c in range(8)]
            ysq = A("f4_ysq", [128, 8, TT], BF16)
            ysqt = [kk.tok(f"f4_ysq{c}") for c in range(8)]
            tmp = A("f4_tmp", [128, TT], F32)
            tmpt = kk.tok("f4_tmp")
            rstd = A("f4_rstd", [128, TT], F32)
            rstdt = kk.tok("f4_rstd")
            for i in range(NT):
                b = i % 2
                cs_ = slice(i * TT, (i + 1) * TT)
                kk.dma("sp", hb[b][:], hsrc[:, cs_].rearrange("(k p) t -> p k t", p=128), writes=[hbt[b]], st=hbt[b])
                kk.dma("sp", ob[b][:], o_d[:, cs_].rearrange("(k p) t -> p k t", p=128), reads=[odt], writes=[obt[b]],
                       st=obt[b])
                for c in range(8):
                    p = c % 2
                    kk.mm_group(ps[p][:], [(wo[:, k, c * 128:(c + 1) * 128], ob[b][:, k, :]) for k in range(8)],
                                reads=[wot, obt[b]], writes=[pst[p]])
                    kk.op("dve", lambda e: e.tensor_copy(out=yb[:, c, :], in_=ps[p][:]), reads=[pst[p]], writes=[ybt[c]])
                    kk.op("act", lambda e: e.activation(out=ysq[:, c, :], in_=yb[:, c, :], func=AF.Square),
                          reads=[ybt[c]], writes=[ysqt[c]])
                kk.mm_group(ps[6][:], [(self.ones[:], ysq[:, c, :]) for c in range(8)], reads=ysqt + [tc], writes=[pst[6]])
                self.rstd_from_ss(ps[6][:], pst[6], tmp[:], tmpt, rstd[:], rstdt)
                for c in range(8):
                    kk.op("dve", lambda e, c=c: e.scalar_tensor_tensor(
                        out=yb[:, c, :], in0=yb[:, c, :], scalar=self.gcol[:, g1 + c:g1 + c + 1],
                        in1=rstd[:], op0=ALU.mult, op1=ALU.mult), reads=[rstdt, tc], writes=[ybt[c]])
                kk.op("pool", lambda e: e.tensor_tensor(out=hb[b][:], in0=hb[b][:], in1=yb[:], op=ALU.add),
                      reads=ybt, writes=[hbt[b]])
                kk.dma("sp", hdst[:, cs_].rearrange("(k p) t -> p k t", p=128), hb[b][:],
                       reads=[hbt[b]], writes=[hst[b]], st=hst[b])

    def mlp_phase(self, l, hsrc, hdst):
        kk, nc = self.kk, self.nc
        TT = 512
        NT = L // TT
        g2 = (l * 4 + 2) * 8
        g3 = (l * 4 + 3) * 8
        with ExitStack() as es:
            def A(name, shape, dtp):
                return es.enter_context(nc.sbuf_tensor(f"{name}_{l}", shape, dtp))
            hb = [A(f"m_h{i}", [128, 8, TT], F32) for i in range(2)]
            hbt = [kk.tok(f"m_h{i}", dma=True) for i in range(2)]
            hst = [kk.tok(f"m_hs{i}", dma=True) for i in range(2)]
            sq = A("m_sq", [128, 8, TT], BF16)
            sqt = kk.tok("m_sq")
            ub = [A(f"m_u{i}", [128, 8, TT], BF16) for i in range(2)]
            ubt = [kk.tok(f"m_u{i}") for i in range(2)]
            hid = A("m_hid", [128, 32, TT], BF16)
            hidt = [kk.tok(f"m_hid{j}") for j in range(32)]
            rr = [A(f"m_r{i}", [128, TT], F32) for i in range(2)]
            rrt = [kk.tok(f"m_r{i}") for i in range(2)]
            w1b = [A(f"m_w1_{i}", [128, 8, 512], BF16) for i in range(3)]
            w1t = [kk.tok(f"m_w1_{i}", dma=True) for i in range(3)]
            w2b = [A(f"m_w2_{i}", [128, 32, 128], BF16) for i in range(3)]
            w2t = [kk.tok(f"m_w2_{i}", dma=True) for i in range(3)]
            yb = A("m_y", [128, 8, TT], F32)
            ybt = [kk.tok(f"m_y{c}") for c in range(8)]
            ysq = A("m_ysq", [128, 8, TT], BF16)
            ysqt = [kk.tok(f"m_ysq{c}") for c in range(8)]
            tmp = A("m_tmp", [128, TT], F32)
            tmpt = kk.tok("m_tmp")
            rstd = [A(f"m_rstd{i}", [128, TT], F32) for i in range(2)]
            rstdt = [kk.tok(f"m_rstd{i}") for i in range(2)]

            w1v = self.w1_b[l].rearrange("(r a) c -> r (a c)", a=2)
            w2v = self.w2_b[l]
            c1, c2 = self.cast_tok[("w1", l)], self.cast_tok[("w2", l)]

            def mk1(q):
                def ld(buf, tk):
                    kk.dma("sp", buf[:], w1v[:, q * 512:(q + 1) * 512].rearrange("(k p) n -> p k n", p=128),
                           reads=[c1], writes=[tk], st=tk)
                return ld

            def mk2(c):
                def ld(buf, tk):
                    kk.dma("sp", buf[:].rearrange("p j n -> p (j n)"), w2v[c], reads=[c2], writes=[tk], st=tk)
                return ld
            s1 = Stream(kk, w1b, w1t, [mk1(q) for _ in range(NT) for q in range(8)])
            s2 = Stream(kk, w2b, w2t, [mk2(c) for _ in range(NT) for c in range(8)])
            PS_S, PS_U, PS_D = 0, (1, 2, 3), (4, 5)
            ps, pst = self.ps, self.pst

            def norm_in(i):
                b = i % 2
                kk.dma("sp", hb[b][:], hsrc[:, i * TT:(i + 1) * TT].rearrange("(k p) t -> p k t", p=128),
                       writes=[hbt[b]], st=hbt[b])
                kk.op("act", lambda e: e.activation(out=sq[:], in_=hb[b][:], func=AF.Square),
                      reads=[hbt[b]], writes=[sqt])
                kk.mm_group(ps[PS_S][:], [(self.ones[:], sq[:, k, :]) for k in range(8)],
                            reads=[sqt, self.t_const], writes=[pst[PS_S]])
                self.rstd_from_ss(ps[PS_S][:], pst[PS_S], tmp[:], tmpt, rstd[0][:], rstdt[0])
                for k in range(8):
                    kk.op("dve", lambda e, k=k: e.scalar_tensor_tensor(
                        out=ub[b][:, k, :], in0=hb[b][:, k, :], scalar=self.gcol[:, g2 + k:g2 + k + 1],
                        in1=rstd[0][:], op0=ALU.mult, op1=ALU.mult),
                        reads=[hbt[b], rstdt[0], self.t_const], writes=[ubt[b]])

            def up(i):
                b = i % 2
                for j in range(32):
                    q = i * 8 + j // 4
                    wb, wt = s1.get(q)
                    jj = j % 4
                    p = PS_U[j % 3]
                    kk.mm_group(ps[p][:], [(wb[:, k, jj * 128:(jj + 1) * 128], ub[b][:, k, :]) for k in range(8)],
                                reads=[wt, ubt[b]], writes=[pst[p]])
                    r = j % 2
                    kk.op("act", lambda e, p=p, r=r: e.activation(out=rr[r][:], in_=ps[p][:], func=AF.Relu),
                          reads=[pst[p]], writes=[rrt[r]])
                    kk.op("pool", lambda e, r=r, j=j: e.tensor_tensor(out=hid[:, j, :], in0=rr[r][:], in1=rr[r][:],
                                                                     op=ALU.mult),
                          reads=[rrt[r]], writes=[hidt[j]])

            def down(i):
                b = i % 2
                for c in range(8):
                    wb, wt = s2.get(i * 8 + c)
                    p = PS_D[c % 2]
                    kk.mm_group(ps[p][:], [(wb[:, j, :], hid[:, j, :]) for j in range(32)],
                                reads=[wt] + hidt, writes=[pst[p]])
                    kk.op("dve", lambda e, p=p, c=c: e.tensor_copy(out=yb[:, c, :], in_=ps[p][:]),
                          reads=[pst[p]], writes=[ybt[c]])
                    kk.op("act", lambda e, c=c: e.activation(out=ysq[:, c, :], in_=yb[:, c, :], func=AF.Square),
                          reads=[ybt[c]], writes=[ysqt[c]])
                import os
                stp = os.environ.get("KSTOP", "")
                if stp == "down0a":
                    return
                kk.mm_group(ps[PS_S][:], [(self.ones[:], ysq[:, c, :]) for c in range(8)],
                            reads=ysqt + [self.t_const], writes=[pst[PS_S]])
                self.rstd_from_ss(ps[PS_S][:], pst[PS_S], tmp[:], tmpt, rstd[1][:], rstdt[1])
                for c in range(8):
                    kk.op("dve", lambda e, c=c: e.scalar_tensor_tensor(
                        out=yb[:, c, :], in0=yb[:, c, :], scalar=self.gcol[:, g3 + c:g3 + c + 1],
                        in1=rstd[1][:], op0=ALU.mult, op1=ALU.mult),
                        reads=[rstdt[1], self.t_const], writes=[ybt[c]])
                if stp == "down0b":
                    return
                kk.op("pool", lambda e: e.tensor_tensor(out=hb[b][:], in0=hb[b][:], in1=yb[:], op=ALU.add),
                      reads=ybt, writes=[hbt[b]])
                if stp == "down0c":
                    return
                kk.dma("sp", hdst[:, i * TT:(i + 1) * TT].rearrange("(k p) t -> p k t", p=128), hb[b][:],
                       reads=[hbt[b]], writes=[hst[b]], st=hst[b])

            import os
            stop = os.environ.get("KSTOP", "")
            if stop == "cast":
                return
            norm_in(0)
            if stop == "norm0":
                return
            for i in range(NT):
                up(i)
                if stop == "up0":
                    return
                if i + 1 < NT and stop != "down0x":
                    norm_in(i + 1)
                if stop == "norm1":
                    return
                down(i)
                if stop.startswith("down0"):
                    return


FULL_PHASES = [(("s5" if l % 2 == 0 else "fox") if w == 0 else "mlp", l) for l in range(DEPTH) for w in range(2)]


def prep_inputs(inputs, b):
    m = {}
    m["x"] = np.ascontiguousarray(inputs["x"][b].T)
    g = np.asarray(inputs["norm_gains"], np.float32)
    m["gcol"] = np.ascontiguousarray(g.reshape(DEPTH * 4, 8, 128).transpose(2, 0, 1).reshape(128, DEPTH * 4 * 8))
    m["w1"] = np.ascontiguousarray(np.asarray(inputs["mlp_w1"], np.float32).reshape(DEPTH, -1, 2048))
    w2 = np.asarray(inputs["mlp_w2"], np.float32).reshape(DEPTH, 32, 128, 8, 128)
    m["w2"] = np.ascontiguousarray(w2.transpose(0, 3, 2, 1, 4).reshape(DEPTH, -1, 2048))
    NS = 2
    are = np.asarray(inputs["s5_a_re"], np.float32); aim = np.asarray(inputs["s5_a_im"], np.float32)
    ldt = np.asarray(inputs["s5_log_dt"], np.float32)
    def pairT(a):
        return np.ascontiguousarray(a.reshape(NS, 32, 2, 64).transpose(0, 2, 3, 1).reshape(NS, 128, 32))
    m["s5_at_re"] = pairT(are)
    m["s5_at_im"] = pairT(aim)
    m["s5_ldt"] = pairT(np.broadcast_to(ldt[:, :, None], (NS, 64, 64)))
    def pairB(b_):
        return np.ascontiguousarray(b_.reshape(NS, 32, 2, 64, 16).transpose(0, 2, 3, 1, 4).reshape(NS, 128, 512))
    m["s5_bre"] = pairB(np.asarray(inputs["s5_b_re"], np.float32))
    m["s5_bim"] = pairB(np.asarray(inputs["s5_b_im"], np.float32))
    m["s5_cre"] = pairB(np.asarray(inputs["s5_c_re"], np.float32).transpose(0, 1, 3, 2))
    m["s5_cim"] = pairB(np.asarray(inputs["s5_c_im"], np.float32).transpose(0, 1, 3, 2))
    m["s5_dcol"] = np.ascontiguousarray(np.asarray(inputs["s5_d"], np.float32).reshape(NS, 8, 128).transpose(0, 2, 1))
    m["wglu"] = np.ascontiguousarray(np.asarray(inputs["s5_w_glu"], np.float32))
    m["ident"] = np.eye(128, dtype=np.float32)
    NF = 2
    win = np.asarray(inputs["fox_w_in"], np.float32)
    m["win"] = np.ascontiguousarray(win[:, :, :4096].reshape(NF, 1024, 4, 16, 64).transpose(0, 3, 1, 2, 4).reshape(NF, 16, 1024, 256))
    m["wf"] = np.ascontiguousarray(win[:, :, 4096:4112])
    m["bf"] = np.ascontiguousarray(np.asarray(inputs["fox_b_f"], np.float32).reshape(NF, 16, 1))
    m["wout"] = np.ascontiguousarray(np.asarray(inputs["fox_w_out"], np.float32))
    m["tri"] = np.triu(np.ones((128, 128), np.float32))
    return m


def run(inputs, phases=None, cores=NCORES, trace=False):
    prog = Prog(phases or FULL_PHASES)
    nc = prog.build()
    shared = None
    in_maps = []
    for b in range(cores):
        m = prep_inputs(inputs, b)
        if shared is None:
            shared = m
        else:
            for k_ in m:
                if k_ != "x":
                    m[k_] = shared[k_]
        in_maps.append(m)
    res = run_bass_kernel_spmd(nc, in_maps, core_ids=list(range(cores)), trace=trace)
    outs = [np.ascontiguousarray(r["out"].T) for r in res.results]
    return np.stack(outs, 0), res


def kernel(**inputs):
    out, _ = run(inputs)
    return out.astype(np.float32)
```

```python
import numpy as np
from contextlib import ExitStack
import concourse.bass as bass
import concourse.mybir as mybir
from concourse.bass_utils import run_bass_kernel_spmd

F32 = mybir.dt.float32
BF16 = mybir.dt.bfloat16
AF = mybir.ActivationFunctionType
ALU = mybir.AluOpType

D = 1024
L = 4096
DEPTH = 4
HID = 4096
EPS = 1e-6
NCORES = 8


class Tok:
    __slots__ = ("name", "w", "r", "ds")

    def __init__(self, name, ds=None):
        self.name = name
        self.w = None
        self.r = {}
        self.ds = ds


class DSem:
    def __init__(self, sem):
        self.sem = sem
        self.cnt = 0


class Eng:
    def __init__(self, name, obj, sem):
        self.name = name
        self.obj = obj
        self.sem = sem
        self.cnt = 0
        self.seen = {}


class K:
    def __init__(self, nc):
        self.nc = nc
        self.engs = {}
        for name, obj in (("pe", nc.tensor), ("act", nc.scalar), ("dve", nc.vector),
                          ("pool", nc.gpsimd), ("sp", nc.sync)):
            self.engs[name] = Eng(name, obj, nc.alloc_semaphore("s_" + name))
        self.dsems = {}
        self.uid = 0

    def tok(self, name, dma=False):
        t = Tok(name)
        if dma:
            if name not in self.dsems:
                self.dsems[name] = DSem(self.nc.alloc_semaphore("d_" + name))
            t.ds = self.dsems[name]
        return t

    def _wait(self, e, ev):
        sem, val = ev
        if sem is e.sem and e.name == "pe":
            return
        key = id(sem)
        if e.seen.get(key, 0) >= val:
            return
        e.obj.wait_ge(sem, val)
        e.seen[key] = val

    def _deps(self, e, reads, writes):
        for t in reads:
            if t.w is not None:
                self._wait(e, t.w)
        for t in writes:
            if t.w is not None:
                self._wait(e, t.w)
            for ev in t.r.values():
                self._wait(e, ev)

    def _reg(self, ev, reads, writes):
        for t in reads:
            t.r[id(ev[0])] = ev
        for t in writes:
            t.w = ev
            t.r = {}

    def op(self, en, fn, reads=(), writes=()):
        e = self.engs[en]
        self._deps(e, reads, writes)
        inst = fn(e.obj)
        e.cnt += 1
        inst.then_inc(e.sem, 1)
        self._reg((e.sem, e.cnt), reads, writes)
        return inst

    def mm_group(self, out_ap, pairs, reads=(), writes=(), **kw):
        e = self.engs["pe"]
        self._deps(e, reads, writes)
        n = len(pairs)
        inst = None
        for i, (lhsT, rhs) in enumerate(pairs):
            inst = e.obj.matmul(out_ap, lhsT, rhs, start=(i == 0), stop=(i == n - 1), **kw)
        e.cnt += 1
        inst.then_inc(e.sem, 1)
        self._reg((e.sem, e.cnt), reads, writes)

    def dma(self, qn, out, in_, reads=(), writes=(), st=None, **kw):
        e = self.engs[qn]
        self._deps(e, reads, writes)
        inst = e.obj.dma_start(out=out, in_=in_, **kw)
        ds = st.ds
        ds.cnt += 16
        inst.then_inc(ds.sem, 16)
        self._reg((ds.sem, ds.cnt), reads, writes)

    def barrier(self):
        comp = [self.engs[n] for n in ("pe", "act", "dve", "pool")]
        for e in self.engs.values():
            for e2 in comp:
                if e2 is not e and e2.cnt > 0:
                    self._wait(e, (e2.sem, e2.cnt))
            for ds in self.dsems.values():
                if ds.cnt > 0:
                    self._wait(e, (ds.sem, ds.cnt))


class Stream:
    def __init__(self, kk, bufs, toks, loads):
        self.kk, self.bufs, self.toks, self.loads = kk, bufs, toks, loads
        self.nxt = 0

    def get(self, i):
        nb = len(self.bufs)
        while self.nxt < len(self.loads) and self.nxt <= i + nb - 1:
            j = self.nxt
            self.loads[j](self.bufs[j % nb], self.toks[j % nb])
            self.nxt += 1
        return self.bufs[i % nb], self.toks[i % nb]


class Prog:
    def __init__(self, phases):
        self.phases = phases
        nc = self.nc = bass.Bass("TRN2", target_bir_lowering=False)
        self.kk = K(nc)
        dt = nc.dram_tensor
        self.x = dt("x", [D, L], F32, kind="ExternalInput").ap()
        self.out = dt("out", [D, L], F32, kind="ExternalOutput").ap()
        self.gcol_d = dt("gcol", [128, DEPTH * 4 * 8], F32, kind="ExternalInput").ap()
        self.w1_d = dt("w1", [DEPTH, D * HID // 2048, 2048], F32, kind="ExternalInput").ap()
        self.w2_d = dt("w2", [DEPTH, D * HID // 2048, 2048], F32, kind="ExternalInput").ap()
        self.w1_b = dt("w1b", [DEPTH, D * HID // 2048, 2048], BF16, kind="Internal").ap()
        self.w2_b = dt("w2b", [DEPTH, 8, 128, 4096], BF16, kind="Internal").ap()
        NS = 2
        self.s5_at_re = dt("s5_at_re", [NS, 128, 32], F32, kind="ExternalInput").ap()
        self.s5_at_im = dt("s5_at_im", [NS, 128, 32], F32, kind="ExternalInput").ap()
        self.s5_ldt = dt("s5_ldt", [NS, 128, 32], F32, kind="ExternalInput").ap()
        self.s5_b_re = dt("s5_bre", [NS, 128, 512], F32, kind="ExternalInput").ap()
        self.s5_b_im = dt("s5_bim", [NS, 128, 512], F32, kind="ExternalInput").ap()
        self.s5_c_re = dt("s5_cre", [NS, 128, 512], F32, kind="ExternalInput").ap()
        self.s5_c_im = dt("s5_cim", [NS, 128, 512], F32, kind="ExternalInput").ap()
        self.s5_dcol = dt("s5_dcol", [NS, 128, 8], F32, kind="ExternalInput").ap()
        self.wglu_d = dt("wglu", [NS, 1024, 2048], F32, kind="ExternalInput").ap()
        self.wglu_b = dt("wglub", [NS, 1024, 2048], BF16, kind="Internal").ap()
        NF = 2
        self.win_d = dt("win", [NF, 8, 1024, 512], F32, kind="ExternalInput").ap()
        self.win_b = dt("winb", [NF, 8, 1024, 512], BF16, kind="Internal").ap()
        self.wf_d = dt("wf", [NF, 1024, 16], F32, kind="ExternalInput").ap()
        self.wf_b = dt("wfb", [NF, 1024, 16], BF16, kind="Internal").ap()
        self.bf_d = dt("bf", [NF, 16, 1], F32, kind="ExternalInput").ap()
        self.wout_d = dt("wout", [NF, 1024, 1024], F32, kind="ExternalInput").ap()
        self.wout_b = dt("woutb", [NF, 1024, 1024], BF16, kind="Internal").ap()
        self.tri_d = dt("tri", [128, 128], F32, kind="ExternalInput").ap()
        self.cumq = dt("cumq", [16, 3, L], BF16, kind="Internal").ap()
        self.o_d = dt("o_d", [D, L], BF16, kind="Internal").ap()
        self.ps = [nc.alloc_psum_tensor(f"ps{i}", [128, 512], F32) for i in range(8)]
        self.pst = [self.kk.tok(f"ps{i}") for i in range(8)]
        self.ones = nc.alloc_sbuf_tensor("ones", [128, 128], BF16)
        self.gcol = nc.alloc_sbuf_tensor("gcol_sb", [128, DEPTH * 4 * 8], F32)
        self.epsc = nc.alloc_sbuf_tensor("epsc", [128, 1], F32)
        self.hpic = nc.alloc_sbuf_tensor("hpic", [128, 1], F32)
        self.onec = nc.alloc_sbuf_tensor("onec", [128, 1], F32)
        self.ident = nc.alloc_sbuf_tensor("ident_sb", [128, 128], F32)
        self.ident_d = dt("ident", [128, 128], F32, kind="ExternalInput").ap()
        self.cast_tok = {}

    def build(self):
        kk, nc = self.kk, self.nc
        t_const = kk.tok("const", dma=True)
        kk.op("dve", lambda e: e.memset(self.ones[:], 1.0), writes=[t_const])
        kk.op("dve", lambda e: e.memset(self.epsc[:], EPS), writes=[t_const])
        kk.op("dve", lambda e: e.memset(self.hpic[:], float(np.pi / 2)), writes=[t_const])
        kk.op("dve", lambda e: e.memset(self.onec[:], 1.0), writes=[t_const])
        kk.dma("sp", self.gcol[:], self.gcol_d, writes=[t_const], st=t_const)
        kk.dma("sp", self.ident[:], self.ident_d, writes=[t_const], st=t_const)
        self.t_const = t_const
        layers = sorted({l for (_, l) in self.phases})
        for l in layers:
            w2cast = self.w2_b.rearrange("l c p (a m) -> l (c p a) m", a=2)
            for nm, src, dst in (("w1", self.w1_d, self.w1_b), ("w2", self.w2_d, w2cast)):
                if any(p == "mlp" and pl == l for p, pl in self.phases):
                    t = kk.tok(f"cast_{nm}{l}", dma=True)
                    kk.dma("pool", dst[l], src[l], writes=[t], st=t)
                    self.cast_tok[(nm, l)] = t
            if any(p == "s5" and pl == l for p, pl in self.phases):
                t = kk.tok(f"cast_glu{l}", dma=True)
                kk.dma("pool", self.wglu_b[l // 2], self.wglu_d[l // 2], writes=[t], st=t)
                self.cast_tok[("glu", l)] = t
            if any(p == "fox" and pl == l for p, pl in self.phases):
                jf = l // 2
                for nm, src, dst in (("win", self.win_d[jf].rearrange("h (r a) n -> (h r) (a n)", a=4),
                                      self.win_b[jf].rearrange("h (r a) n -> (h r) (a n)", a=4)),
                                     ("wf", self.wf_d[jf].rearrange("(r a) n -> r (a n)", a=128),
                                      self.wf_b[jf].rearrange("(r a) n -> r (a n)", a=128)),
                                     ("wout", self.wout_d[jf].rearrange("(r a) n -> r (a n)", a=2),
                                      self.wout_b[jf].rearrange("(r a) n -> r (a n)", a=2))):
                    t = kk.tok(f"cast_{nm}{l}", dma=True)
                    kk.dma("pool", dst, src, writes=[t], st=t)
                    self.cast_tok[(nm, l)] = t
        kk.barrier()
        cur = self.x
        for (p, l) in self.phases:
            if p == "mlp":
                self.mlp_phase(l, cur, self.out)
            elif p == "s5":
                self.s5_phase(l, cur, self.out)
            elif p == "fox":
                self.fox_phase(l, cur, self.out)
            cur = self.out
            kk.barrier()
        return nc

    def rstd_from_ss(self, ss_ps, ss_tok, tmp, tmp_tok, rstd, rstd_tok):
        kk = self.kk
        kk.op("act", lambda e: e.activation(out=tmp, in_=ss_ps, func=AF.Sqrt, bias=self.epsc[:], scale=1.0 / D),
              reads=[ss_tok, self.t_const], writes=[tmp_tok])
        kk.op("dve", lambda e: e.reciprocal(out=rstd, in_=tmp), reads=[tmp_tok], writes=[rstd_tok])

    def s5_phase(self, l, hsrc, hdst):
        kk, nc = self.kk, self.nc
        j5 = l // 2
        TT = 512
        NT = L // TT
        g0 = (l * 4 + 0) * 8
        g1 = (l * 4 + 1) * 8
        ps, pst = self.ps, self.pst
        tc = self.t_const
        with ExitStack() as es0:
            ufm = es0.enter_context(nc.sbuf_tensor(f"s_u_{l}", [128, 8, L], BF16))
            ut = [[kk.tok(f"s_u{k}_{n}") for n in range(NT)] for k in range(8)]
            with ExitStack() as es:
                def A(name, shape, dtp):
                    return es.enter_context(nc.sbuf_tensor(f"{name}_{l}", shape, dtp))
                hb = [A(f"s1_h{i}", [128, 8, TT], F32) for i in range(2)]
                hbt = [kk.tok(f"s1_h{i}", dma=True) for i in range(2)]
                sq = [A(f"s1_sq{i}", [128, 8, TT], BF16) for i in range(2)]
                sqt = [kk.tok(f"s1_sq{i}") for i in range(2)]
                tmp = A("s1_tmp", [128, TT], F32)
                tmpt = kk.tok("s1_tmp")
                rstd = [A(f"s1_rstd{i}", [128, TT], F32) for i in range(2)]
                rstdt = [kk.tok(f"s1_rstd{i}") for i in range(2)]
                for i in range(NT):
                    b = i % 2
                    kk.dma("sp", hb[b][:], hsrc[:, i * TT:(i + 1) * TT].rearrange("(k p) t -> p k t", p=128),
                           writes=[hbt[b]], st=hbt[b])
                    kk.op("act", lambda e: e.activation(out=sq[b][:], in_=hb[b][:], func=AF.Square),
                          reads=[hbt[b]], writes=[sqt[b]])
                    kk.mm_group(ps[6 + b][:], [(self.ones[:], sq[b][:, k, :]) for k in range(8)],
                                reads=[sqt[b], tc], writes=[pst[6 + b]])
                    self.rstd_from_ss(ps[6 + b][:], pst[6 + b], tmp[:], tmpt, rstd[b][:], rstdt[b])
                    for k in range(8):
                        kk.op("dve", lambda e, k=k: e.scalar_tensor_tensor(
                            out=ufm[:, k, i * TT:(i + 1) * TT], in0=hb[b][:, k, :],
                            scalar=self.gcol[:, g0 + k:g0 + k + 1], in1=rstd[b][:], op0=ALU.mult, op1=ALU.mult),
                            reads=[hbt[b], rstdt[b], tc], writes=[ut[k][i]])
            kk.barrier()
            import os
            S5STOP = os.environ.get("S5STOP", "")
            if S5STOP == "p1":
                return
            with ExitStack() as es:
                def A(name, shape, dtp):
                    return es.enter_context(nc.sbuf_tensor(f"{name}_{l}", shape, dtp))
                NSL = 48
                sc = A("s2_sc", [128, NSL, 32], F32)
                SC = {}

                def S(name):
                    if name not in SC:
                        assert len(SC) < NSL
                        SC[name] = (len(SC), kk.tok("sc_" + name, dma=name in ("atre", "atim", "ldt")))
                    i_, t_ = SC[name]
                    return sc[:, i_, :], t_

                def tt(o, a, b, op):
                    (oa, ot), (aa, at), (ba, bt) = S(o), S(a), S(b)
                    kk.op("dve", lambda e: e.tensor_tensor(out=oa, in0=aa, in1=ba, op=op), reads=[at, bt], writes=[ot])

                def tsc(o, a, s1, op0, s2=None, op1=None):
                    (oa, ot), (aa, at) = S(o), S(a)
                    if s2 is None:
                        kk.op("dve", lambda e: e.tensor_scalar(out=oa, in0=aa, scalar1=s1, scalar2=None, op0=op0),
                              reads=[at], writes=[ot])
                    else:
                        kk.op("dve", lambda e: e.tensor_scalar(out=oa, in0=aa, scalar1=s1, scalar2=s2, op0=op0, op1=op1),
                              reads=[at], writes=[ot])

                def act(o, a, func, **kw):
                    (oa, ot), (aa, at) = S(o), S(a)
                    kk.op("act", lambda e: e.activation(out=oa, in_=aa, func=func, **kw), reads=[at, tc], writes=[ot])

                for nm, src in (("atre", self.s5_at_re), ("atim", self.s5_at_im), ("ldt", self.s5_ldt)):
                    oa, ot = S(nm)
                    kk.dma("sp", oa, src[j5], writes=[ot], st=ot)
                act("dt", "ldt", AF.Exp)
                tt("lr", "atre", "dt", ALU.mult)
                tt("th", "atim", "dt", ALU.mult)
                act("rho", "lr", AF.Exp)
                act("s_0", "th", AF.Sin, scale=1.0 / 16)
                act("c_0", "th", AF.Sin, scale=1.0 / 16, bias=self.hpic[:])

                def csq(ci, si, co, so):
                    tt("cc", ci, ci, ALU.mult)
                    tt("ss", si, si, ALU.mult)
                    tt("cs", ci, si, ALU.mult)
                    tt(co, "cc", "ss", ALU.subtract)
                    tsc(so, "cs", 2.0, ALU.mult)
                csq("c_0", "s_0", "c_1", "s_1")
                csq("c_1", "s_1", "c_0", "s_0")
                csq("c_0", "s_0", "c_1", "s_1")
                csq("c_1", "s_1", "ec0", "es0")
                for k in range(1, 10):
                    csq(f"ec{k - 1}", f"es{k - 1}", f"ec{k}", f"es{k}")
                tt("abr", "rho", "ec0", ALU.mult)
                tt("abi", "rho", "es0", ALU.mult)
                tsc("am1", "abr", -1.0, ALU.add)
                tt("den", "atre", "atre", ALU.mult)
                tt("d2", "atim", "atim", ALU.mult)
                tt("den", "den", "d2", ALU.add)
                (oa, ot), (aa, at) = S("rden"), S("den")
                kk.op("dve", lambda e: e.reciprocal(out=oa, in_=aa), reads=[at], writes=[ot])
                tt("nr", "am1", "atre", ALU.mult)
                tt("t", "abi", "atim", ALU.mult)
                tt("nr", "nr", "t", ALU.add)
                tt("ni", "abi", "atre", ALU.mult)
                tt("t", "am1", "atim", ALU.mult)
                tt("ni", "ni", "t", ALU.subtract)
                tt("kr", "nr", "rden", ALU.mult)
                tt("ki", "ni", "rden", ALU.mult)

                def col(name, q):
                    i_, t_ = SC[name]
                    return sc[:, i_, q:q + 1], t_
                if S5STOP == "sc":
                    return

                bre = A("s2_bre", [128, 32, 16], F32)
                bim = A("s2_bim", [128, 32, 16], F32)
                cre = A("s2_cre", [128, 32, 16], F32)
                cim = A("s2_cim", [128, 32, 16], F32)
                dcol = A("s2_dcol", [128, 8], F32)
                pt = kk.tok("s2_par", dma=True)
                for dst, src in ((bre, self.s5_b_re), (bim, self.s5_b_im), (cre, self.s5_c_re), (cim, self.s5_c_im)):
                    kk.dma("sp", dst[:].rearrange("p q h -> p (q h)"), src[j5], writes=[pt], st=pt)
                kk.dma("sp", dcol[:], self.s5_dcol[j5], writes=[pt], st=pt)
                wtmp = [[A(f"s2_wt{c}{m}", [128, 128], F32) for m in range(4)] for c in range(2)]
                wtmpt = [[kk.tok(f"s2_wt{c}{m}") for m in range(4)] for c in range(2)]
                WB = [[A(f"s2_wb{c}{m}", [128, 128], BF16) for m in range(4)] for c in range(2)]
                WBt = [[kk.tok(f"s2_wb{c}{m}") for m in range(4)] for c in range(2)]
                CW = [[A(f"s2_cw{c}{m}", [128, 128], BF16) for m in range(4)] for c in range(2)]
                CWt = [[kk.tok(f"s2_cw{c}{m}") for m in range(4)] for c in range(2)]
                t16 = [A(f"s2_t16{i}", [128, 16], F32) for i in range(2)]
                t16t = [kk.tok(f"s2_t16{i}") for i in range(2)]
                tabc = [A(f"s2_tc{m}", [128, 512], F32) for m in range(4)]
                tabs = [A(f"s2_ts{m}", [128, 512], F32) for m in range(4)]
                tabt = [kk.tok(f"s2_tab{m}") for m in range(4)]
                tw = [A(f"s2_tw{i}", [128, 256], F32) for i in range(2)]
                twt = [kk.tok(f"s2_tw{i}") for i in range(2)]
                rbc = [A(f"s2_rb{m}", [128, 512], F32) for m in range(4)]
                rbt = [kk.tok(f"s2_rb{m}") for m in range(4)]
                onesf = A("s2_onesf", [128, 512], F32)
                ini = A("s2_ini", [128, 4, 2], F32)
                init = [kk.tok(f"s2_ini{m}") for m in range(4)]
                itmp = A("s2_itmp", [128, 4, 2], F32)
                T = [[A(f"s2_T{pb}{i}", [128, 512], F32) for i in range(8)] for pb in range(2)]
                Tt = [[kk.tok(f"s2_T{pb}{i}") for i in range(8)] for pb in range(2)]
                U = [A(f"s2_U{i}", [128, 512], F32) for i in range(4)]
                Ut = [kk.tok(f"s2_U{i}") for i in range(4)]
                xb = [[A(f"s2_x{pb}{c}", [128, 512], BF16) for c in range(2)] for pb in range(2)]
                xbt = [[kk.tok(f"s2_x{pb}{c}") for c in range(2)] for pb in range(2)]
                yv = [A(f"s2_yv{i}", [128, 512], F32) for i in range(2)]
                yvt = [kk.tok(f"s2_yv{i}") for i in range(2)]
                for c in range(2):
                    for m in range(4):
                        kk.op("pool", lambda e, c=c, m=m: e.memset(wtmp[c][m][:], 0.0), writes=[wtmpt[c][m]])
                        kk.op("pool", lambda e, c=c, m=m: e.memset(CW[c][m][:], 0.0), writes=[CWt[c][m]])
                for m in range(4):
                    kk.op("pool", lambda e, m=m: e.memset(tabc[m][:, 0:1], 1.0), writes=[tabt[m]])
                    kk.op("pool", lambda e, m=m: e.memset(tabs[m][:, 0:1], 0.0), writes=[tabt[m]])
                kk.op("pool", lambda e: e.memset(onesf[:], 1.0), writes=[tc])

                pcount = 0
                for k in range(8):
                    for m in range(4):
                        q = 4 * k + m
                        (kr, krt), (ki, kit) = col("kr", q), col("ki", q)
                        kk.op("dve", lambda e: e.tensor_scalar(out=t16[0][:], in0=bim[:, q, :], scalar1=ki, scalar2=None,
                                                               op0=ALU.mult), reads=[pt, kit], writes=[t16t[0]])
                        kk.op("dve", lambda e: e.tensor_scalar(out=t16[1][:], in0=bim[:, q, :], scalar1=kr, scalar2=None,
                                                               op0=ALU.mult), reads=[pt, krt], writes=[t16t[1]])
                        for i2 in range(2):
                            r0, r1 = 64 * i2, 64 * i2 + 64
                            c0 = 32 * m + 16 * i2
                            kk.op("dve", lambda e: e.scalar_tensor_tensor(
                                out=wtmp[0][m][r0:r1, c0:c0 + 16], in0=bre[r0:r1, q, :], scalar=kr[r0:r1, :],
                                in1=t16[0][r0:r1, :], op0=ALU.mult, op1=ALU.subtract),
                                reads=[pt, krt, t16t[0]], writes=[wtmpt[0][m]])
                            kk.op("dve", lambda e: e.scalar_tensor_tensor(
                                out=wtmp[1][m][r0:r1, c0:c0 + 16], in0=bre[r0:r1, q, :], scalar=ki[r0:r1, :],
                                in1=t16[1][r0:r1, :], op0=ALU.mult, op1=ALU.add),
                                reads=[pt, kit, t16t[1]], writes=[wtmpt[1][m]])
                            kk.op("pool", lambda e: e.tensor_copy(out=CW[0][m][r0:r1, c0:c0 + 16], in_=cre[r0:r1, q, :]),
                                  reads=[pt], writes=[CWt[0][m]])
                            kk.op("pool", lambda e: e.tensor_scalar(out=CW[1][m][r0:r1, c0:c0 + 16], in0=cim[r0:r1, q, :],
                                                                    scalar1=-1.0, scalar2=None, op0=ALU.mult),
                                  reads=[pt], writes=[CWt[1][m]])
                        if S5STOP == "wg1":
                            return
                        for c in range(2):
                            kk.op("pe", lambda e: e.transpose(out=ps[6][:, 0:128], in_=wtmp[c][m][:], identity=self.ident[:]),
                                  reads=[wtmpt[c][m], tc], writes=[pst[6]])
                            kk.op("act", lambda e: e.activation(out=WB[c][m][:], in_=ps[6][:, 0:128], func=AF.Copy),
                                  reads=[pst[6]], writes=[WBt[c][m]])
                        if S5STOP == "wg2":
                            return
                        for lv in range(9):
                            w = 1 << lv
                            (ec, ect), (esn, est) = col(f"ec{lv}", q), col(f"es{lv}", q)
                            kk.op("dve", lambda e: e.tensor_scalar(out=tw[0][:, 0:w], in0=tabs[m][:, 0:w], scalar1=esn,
                                                                   scalar2=None, op0=ALU.mult),
                                  reads=[tabt[m], est], writes=[twt[0]])
                            kk.op("dve", lambda e: e.tensor_scalar(out=tw[1][:, 0:w], in0=tabs[m][:, 0:w], scalar1=ec,
                                                                   scalar2=None, op0=ALU.mult),
                                  reads=[tabt[m], ect], writes=[twt[1]])
                            kk.op("dve", lambda e: e.scalar_tensor_tensor(
                                out=tabs[m][:, w:2 * w], in0=tabc[m][:, 0:w], scalar=esn, in1=tw[1][:, 0:w],
                                op0=ALU.mult, op1=ALU.add), reads=[twt[1], est], writes=[tabt[m]])
                            kk.op("dve", lambda e: e.scalar_tensor_tensor(
                                out=tabc[m][:, w:2 * w], in0=tabc[m][:, 0:w], scalar=ec, in1=tw[0][:, 0:w],
                                op0=ALU.mult, op1=ALU.subtract), reads=[twt[0], ect], writes=[tabt[m]])
                        if S5STOP == "wg3":
                            return
                        (rh, rht) = col("rho", q)
                        kk.op("dve", lambda e: e.tensor_scalar(out=rbc[m][:], in0=onesf[:], scalar1=rh, scalar2=None,
                                                               op0=ALU.mult), reads=[rht, tc], writes=[rbt[m]])
                        if S5STOP == "wg4":
                            return
                        kk.op("dve", lambda e: e.memset(ini[:, m, :], 0.0), writes=[init[m]])
                        if S5STOP == "m0":
                            return
                        if S5STOP in ("m1", "m2", "m3") and m == int(S5STOP[1]):
                            return
                    if S5STOP == "wgen":
                        return
                    items = [(n, m) for n in range(NT) for m in range(4)]

                    def stA(it, n, m):
                        cs_ = slice(n * TT, (n + 1) * TT)
                        pb = it % 2
                        pa, pbk = (0, 1) if pb == 0 else (2, 3)
                        Tb, Ttb = T[pb], Tt[pb]
                        kk.mm_group(ps[pa][:], [(WB[0][m][:], ufm[:, k, cs_])], reads=[WBt[0][m], ut[k][n]],
                                    writes=[pst[pa]])
                        kk.mm_group(ps[pbk][:], [(WB[1][m][:], ufm[:, k, cs_])], reads=[WBt[1][m], ut[k][n]],
                                    writes=[pst[pbk]])

                    def stA2(it, n, m):
                        pb = it % 2
                        pa, pbk = (0, 1) if pb == 0 else (2, 3)
                        Tb, Ttb = T[pb], Tt[pb]
                        for (ti, pp, tab) in ((0, pa, tabc), (1, pbk, tabs), (2, pbk, tabc), (3, pa, tabs)):
                            kk.op("dve", lambda e, ti=ti, pp=pp, tab=tab: e.tensor_tensor(
                                out=Tb[ti][:], in0=ps[pp][:], in1=tab[m][:], op=ALU.mult),
                                reads=[pst[pp], tabt[m]], writes=[Ttb[ti]])

                    def stB(it, n, m):
                        Tb, Ttb = T[it % 2], Tt[it % 2]
                        kk.op("dve", lambda e: e.tensor_tensor(out=Tb[4][:], in0=Tb[0][:], in1=Tb[1][:], op=ALU.add),
                              reads=[Ttb[0], Ttb[1]], writes=[Ttb[4]])
                        kk.op("dve", lambda e: e.tensor_tensor(out=Tb[5][:], in0=Tb[2][:], in1=Tb[3][:], op=ALU.subtract),
                              reads=[Ttb[2], Ttb[3]], writes=[Ttb[5]])

                    def stC(it, n, m):
                        q = 4 * k + m
                        Tb, Ttb = T[it % 2], Tt[it % 2]
                        kk.op("dve", lambda e: e.tensor_tensor_scan(out=Tb[6][:], data0=rbc[m][:], data1=Tb[4][:],
                                                                    initial=ini[:, m, 0:1], op0=ALU.mult, op1=ALU.add),
                              reads=[rbt[m], Ttb[4], init[m]], writes=[Ttb[6]])
                        kk.op("dve", lambda e: e.tensor_tensor_scan(out=Tb[7][:], data0=rbc[m][:], data1=Tb[5][:],
                                                                    initial=ini[:, m, 1:2], op0=ALU.mult, op1=ALU.add),
                              reads=[rbt[m], Ttb[5], init[m]], writes=[Ttb[7]])
                        (e9c, e9ct), (e9s, e9st) = col("ec9", q), col("es9", q)
                        kk.op("dve", lambda e: e.tensor_scalar(out=itmp[:, m, 0:1], in0=Tb[7][:, TT - 1:TT], scalar1=e9s,
                                                               scalar2=None, op0=ALU.mult),
                              reads=[Ttb[7], e9st], writes=[init[m]])
                        kk.op("dve", lambda e: e.tensor_scalar(out=itmp[:, m, 1:2], in0=Tb[7][:, TT - 1:TT], scalar1=e9c,
                                                               scalar2=None, op0=ALU.mult),
                              reads=[Ttb[7], e9ct], writes=[init[m]])
                        kk.op("dve", lambda e: e.scalar_tensor_tensor(
                            out=ini[:, m, 0:1], in0=Tb[6][:, TT - 1:TT], scalar=e9c, in1=itmp[:, m, 0:1],
                            op0=ALU.mult, op1=ALU.subtract), reads=[Ttb[6], e9ct], writes=[init[m]])
                        kk.op("dve", lambda e: e.scalar_tensor_tensor(
                            out=ini[:, m, 1:2], in0=Tb[6][:, TT - 1:TT], scalar=e9s, in1=itmp[:, m, 1:2],
                            op0=ALU.mult, op1=ALU.add), reads=[Ttb[6], e9st], writes=[init[m]])

                    def stD(it, n, m):
                        pb = it % 2
                        Tb, Ttb = T[pb], Tt[pb]
                        for (ti, zi_, tab) in ((0, 6, tabc), (1, 7, tabs), (2, 6, tabs), (3, 7, tabc)):
                            kk.op("pool", lambda e, ti=ti, zi_=zi_, tab=tab: e.tensor_tensor(
                                out=U[ti][:], in0=Tb[zi_][:], in1=tab[m][:], op=ALU.mult),
                                reads=[Ttb[zi_], tabt[m]], writes=[Ut[ti]])
                        kk.op("pool", lambda e: e.tensor_tensor(out=xb[pb][0][:], in0=U[0][:], in1=U[1][:], op=ALU.subtract),
                              reads=[Ut[0], Ut[1]], writes=[xbt[pb][0]])
                        kk.op("pool", lambda e: e.tensor_tensor(out=xb[pb][1][:], in0=U[2][:], in1=U[3][:], op=ALU.add),
                              reads=[Ut[2], Ut[3]], writes=[xbt[pb][1]])

                    def stE(it, n, m):
                        cs_ = slice(n * TT, (n + 1) * TT)
                        pb = it % 2
                        py = 4 + (n % 2)
                        e_pe = kk.engs["pe"]
                        kk._deps(e_pe, [CWt[0][m], CWt[1][m], xbt[pb][0], xbt[pb][1]], [pst[py]] if m == 0 else [])
                        e_pe.obj.matmul(ps[py][:], CW[0][m][:], xb[pb][0][:], start=(m == 0), stop=False)
                        inst = e_pe.obj.matmul(ps[py][:], CW[1][m][:], xb[pb][1][:], start=False, stop=(m == 3))
                        e_pe.cnt += 1
                        inst.then_inc(e_pe.sem, 1)
                        kk._reg((e_pe.sem, e_pe.cnt), [CWt[0][m], CWt[1][m], xbt[pb][0], xbt[pb][1]], [pst[py]])
                        if m == 3:
                            yb_ = n % 2
                            kk.op("dve", lambda e: e.scalar_tensor_tensor(
                                out=yv[yb_][:], in0=ufm[:, k, cs_], scalar=dcol[:, k:k + 1], in1=ps[py][:],
                                op0=ALU.mult, op1=ALU.add), reads=[ut[k][n], pt, pst[py]], writes=[yvt[yb_]])
                            kk.op("act", lambda e: e.activation(out=ufm[:, k, cs_], in_=yv[yb_][:],
                                                                func=AF.Gelu_apprx_tanh),
                                  reads=[yvt[yb_]], writes=[ut[k][n]])

                    NI = len(items)
                    stA(0, *items[0])
                    for s_ in range(NI + 1):
                        if s_ + 1 < NI:
                            stA(s_ + 1, *items[s_ + 1])
                        if s_ < NI:
                            stA2(s_, *items[s_])
                            stB(s_, *items[s_])
                        if s_ >= 1:
                            stC(s_ - 1, *items[s_ - 1])
                            stD(s_ - 1, *items[s_ - 1])
                            stE(s_ - 1, *items[s_ - 1])
                    if S5STOP == "k0":
                        return
            kk.barrier()
            if S5STOP == "p2":
                return
            with ExitStack() as es:
                def A(name, shape, dtp):
                    return es.enter_context(nc.sbuf_tensor(f"{name}_{l}", shape, dtp))
                wg = A("s3_wg", [128, 8, 2048], BF16)
                wgt = kk.tok("s3_wg", dma=True)
                cg = self.cast_tok[("glu", l)]
                for k in range(8):
                    kk.dma("sp", wg[:, k, :], self.wglu_b[j5][k * 128:(k + 1) * 128, :], reads=[cg], writes=[wgt], st=wgt)
                hb = [A(f"s3_h{i}", [128, 8, TT], F32) for i in range(2)]
                hbt = [kk.tok(f"s3_h{i}", dma=True) for i in range(2)]
                hst = [kk.tok(f"s3_hs{i}", dma=True) for i in range(2)]
                yb = A("s3_y", [128, 8, TT], F32)
                ybt = [kk.tok(f"s3_y{c}") for c in range(8)]
                ysq = A("s3_ysq", [128, 8, TT], BF16)
                ysqt = [kk.tok(f"s3_ysq{c}") for c in range(8)]
                sg = [A(f"s3_sg{i}", [128, TT], F32) for i in range(2)]
                sgt = [kk.tok(f"s3_sg{i}") for i in range(2)]
                tmp = A("s3_tmp", [128, TT], F32)
                tmpt = kk.tok("s3_tmp")
                rstd = A("s3_rstd", [128, TT], F32)
                rstdt = kk.tok("s3_rstd")
                for i in range(NT):
                    b = i % 2
                    cs_ = slice(i * TT, (i + 1) * TT)
                    kk.dma("sp", hb[b][:], hsrc[:, cs_].rearrange("(k p) t -> p k t", p=128), writes=[hbt[b]], st=hbt[b])
                    for c in range(8):
                        pv, pg = (0, 1) if c % 2 == 0 else (2, 3)
                        kk.mm_group(ps[pv][:], [(wg[:, k, c * 128:(c + 1) * 128], ufm[:, k, cs_]) for k in range(8)],
                                    reads=[wgt] + [ut[k][i] for k in range(8)], writes=[pst[pv]])
                        kk.mm_group(ps[pg][:], [(wg[:, k, 1024 + c * 128:1024 + (c + 1) * 128], ufm[:, k, cs_])
                                                for k in range(8)],
                                    reads=[wgt] + [ut[k][i] for k in range(8)], writes=[pst[pg]])
                        s_ = c % 2
                        kk.op("act", lambda e: e.activation(out=sg[s_][:], in_=ps[pg][:], func=AF.Sigmoid),
                              reads=[pst[pg]], writes=[sgt[s_]])
                        kk.op("dve", lambda e: e.tensor_tensor(out=yb[:, c, :], in0=ps[pv][:], in1=sg[s_][:], op=ALU.mult),
                              reads=[pst[pv], sgt[s_]], writes=[ybt[c]])
                        kk.op("act", lambda e: e.activation(out=ysq[:, c, :], in_=yb[:, c, :], func=AF.Square),
                              reads=[ybt[c]], writes=[ysqt[c]])
                    kk.mm_group(ps[6][:], [(self.ones[:], ysq[:, c, :]) for c in range(8)],
                                reads=ysqt + [tc], writes=[pst[6]])
                    self.rstd_from_ss(ps[6][:], pst[6], tmp[:], tmpt, rstd[:], rstdt)
                    for c in range(8):
                        kk.op("dve", lambda e, c=c: e.scalar_tensor_tensor(
                            out=yb[:, c, :], in0=yb[:, c, :], scalar=self.gcol[:, g1 + c:g1 + c + 1],
                            in1=rstd[:], op0=ALU.mult, op1=ALU.mult),
                            reads=[rstdt, tc], writes=[ybt[c]])
                    kk.op("pool", lambda e: e.tensor_tensor(out=hb[b][:], in0=hb[b][:], in1=yb[:], op=ALU.add),
                          reads=ybt, writes=[hbt[b]])
                    kk.dma("sp", hdst[:, cs_].rearrange("(k p) t -> p k t", p=128), hb[b][:],
                           reads=[hbt[b]], writes=[hst[b]], st=hst[b])

    def norm_full(self, l, hsrc, gidx, ufm, ut, pfx):
        kk, nc = self.kk, self.nc
        TT = 512
        NT = L // TT
        ps, pst, tc = self.ps, self.pst, self.t_const
        with ExitStack() as es:
            def A(name, shape, dtp):
                return es.enter_context(nc.sbuf_tensor(f"{pfx}{name}_{l}", shape, dtp))
            hb = [A(f"h{i}", [128, 8, TT], F32) for i in range(2)]
            hbt = [kk.tok(f"{pfx}h{i}", dma=True) for i in range(2)]
            sq = [A(f"sq{i}", [128, 8, TT], BF16) for i in range(2)]
            sqt = [kk.tok(f"{pfx}sq{i}") for i in range(2)]
            tmp = A("tmp", [128, TT], F32)
            tmpt = kk.tok(f"{pfx}tmp")
            rstd = [A(f"rstd{i}", [128, TT], F32) for i in range(2)]
            rstdt = [kk.tok(f"{pfx}rstd{i}") for i in range(2)]
            for i in range(NT):
                b = i % 2
                kk.dma("sp", hb[b][:], hsrc[:, i * TT:(i + 1) * TT].rearrange("(k p) t -> p k t", p=128),
                       writes=[hbt[b]], st=hbt[b])
                kk.op("act", lambda e: e.activation(out=sq[b][:], in_=hb[b][:], func=AF.Square),
                      reads=[hbt[b]], writes=[sqt[b]])
                kk.mm_group(ps[6 + b][:], [(self.ones[:], sq[b][:, k, :]) for k in range(8)],
                            reads=[sqt[b], tc], writes=[pst[6 + b]])
                self.rstd_from_ss(ps[6 + b][:], pst[6 + b], tmp[:], tmpt, rstd[b][:], rstdt[b])
                for k in range(8):
                    kk.op("dve", lambda e, k=k: e.scalar_tensor_tensor(
                        out=ufm[:, k, i * TT:(i + 1) * TT], in0=hb[b][:, k, :],
                        scalar=self.gcol[:, gidx + k:gidx + k + 1], in1=rstd[b][:], op0=ALU.mult, op1=ALU.mult),
                        reads=[hbt[b], rstdt[b], tc], writes=[ut[k][i]])
        kk.barrier()

    def fox_phase(self, l, hsrc, hdst):
        kk, nc = self.kk, self.nc
        jf = l // 2
        TT = 512
        NT = L // TT
        g0 = (l * 4 + 0) * 8
        g1 = (l * 4 + 1) * 8
        ps, pst, tc = self.ps, self.pst, self.t_const
        o_d = self.o_d
        odt = kk.tok("o_d")
        with ExitStack() as es0:
            ufm = es0.enter_context(nc.sbuf_tensor(f"f_u_{l}", [128, 8, L], BF16))
            ut = [[kk.tok(f"f_u{k}_{n}") for n in range(NT)] for k in range(8)]
            uall = [ut[k][n] for k in range(8) for n in range(NT)]
            self.norm_full(l, hsrc, g0, ufm, ut, "f1_")
            negcT = es0.enter_context(nc.sbuf_tensor(f"f_negcT_{l}", [128, 32, 16], F32))
            negct = kk.tok("f_negcT")
            cq_t = kk.tok("f_cq", dma=True)
            with ExitStack() as es:
                def A(name, shape, dtp):
                    return es.enter_context(nc.sbuf_tensor(f"{name}_{l}", shape, dtp))
                wf = A("f2_wf", [128, 8, 16], BF16)
                wft = kk.tok("f2_wf", dma=True)
                kk.dma("sp", wf[:], self.wf_b[jf].rearrange("(k p) n -> p k n", p=128), reads=[self.cast_tok[("wf", l)]],
                       writes=[wft], st=wft)
                bf_ = A("f2_bf", [16, 1], F32)
                nbf = A("f2_nbf", [16, 1], F32)
                bft = kk.tok("f2_bf", dma=True)
                kk.dma("sp", bf_[:], self.bf_d[jf], writes=[bft], st=bft)
                kk.op("dve", lambda e: e.tensor_scalar(out=nbf[:], in0=bf_[:], scalar1=-1.0, scalar2=None, op0=ALU.mult),
                      reads=[bft], writes=[bft])
                cum = A("f2_cum", [16, L], F32)
                cumt = kk.tok("f2_cum")
                ex = [A(f"f2_ex{i}", [16, TT], F32) for i in range(2)]
                ext = [kk.tok(f"f2_ex{i}") for i in range(2)]
                one16 = A("f2_one16", [16, TT], F32)
                kk.op("pool", lambda e: e.memset(one16[:], 1.0), writes=[tc])
                zero1 = A("f2_zero", [16, 1], F32)
                kk.op("pool", lambda e: e.memset(zero1[:], 0.0), writes=[tc])
                for n in range(NT):
                    cs_ = slice(n * TT, (n + 1) * TT)
                    b = n % 2
                    kk.mm_group(ps[b][0:16, :], [(wf[:, k, :], ufm[:, k, cs_]) for k in range(8)],
                                reads=[wft] + [ut[k][n] for k in range(8)], writes=[pst[b]])
                    kk.op("act", lambda e: e.activation(out=ex[b][:], in_=ps[b][0:16, :], func=AF.Exp, scale=-1.0,
                                                        bias=nbf[:]), reads=[pst[b], bft], writes=[ext[b]])
                    kk.op("act", lambda e: e.activation(out=ex[b][:], in_=ex[b][:], func=AF.Ln, bias=self.onec[0:16, :]),
                          reads=[ext[b], tc], writes=[ext[b]])
                    kk.op("dve", lambda e: e.tensor_scalar(out=ex[b][:], in0=ex[b][:], scalar1=-1.0, scalar2=None,
                                                           op0=ALU.mult), reads=[ext[b]], writes=[ext[b]])
                    kk.op("dve", lambda e: e.tensor_tensor_scan(
                        out=cum[:, cs_], data0=one16[:], data1=ex[b][:],
                        initial=(zero1[:] if n == 0 else cum[:, n * TT - 1:n * TT]), op0=ALU.mult, op1=ALU.add),
                        reads=[ext[b], tc, cumt], writes=[cumt])
                for tb in range(32):
                    kk.op("pe", lambda e, tb=tb: e.transpose(out=ps[6][:, tb * 16:(tb + 1) * 16],
                                                             in_=cum[:, tb * 128:(tb + 1) * 128],
                                                             identity=self.ident[0:16, 0:16]),
                          reads=[cumt, tc], writes=[pst[6]])
                kk.op("dve", lambda e: e.tensor_scalar(out=negcT[:].rearrange("p a b -> p (a b)"), in0=ps[6][:],
                                                       scalar1=-1.0, scalar2=None, op0=ALU.mult),
                      reads=[pst[6]], writes=[negct])
                c8 = A("f2_c8", [16, L], F32)
                c8t = kk.tok("f2_c8")
                cp = [A(f"f2_cp{i}", [16, L], BF16) for i in range(3)]
                cpt = [kk.tok(f"f2_cp{i}", dma=True) for i in range(3)]
                kk.op("dve", lambda e: e.tensor_scalar(out=c8[:], in0=cum[:], scalar1=8.0, scalar2=None, op0=ALU.mult),
                      reads=[cumt], writes=[c8t])
                for i in range(3):
                    kk.op("dve", lambda e, i=i: e.tensor_copy(out=cp[i][:], in_=c8[:]), reads=[c8t], writes=[cpt[i]])
                    if i < 2:
                        kk.op("dve", lambda e, i=i: e.tensor_tensor(out=c8[:], in0=c8[:], in1=cp[i][:], op=ALU.subtract),
                              reads=[cpt[i]], writes=[c8t])
                    kk.dma("sp", self.cumq[:, i, :], cp[i][:], reads=[cpt[i]], writes=[cq_t], st=cpt[i])
            kk.barrier()
            with ExitStack() as es:
                def A(name, shape, dtp):
                    return es.enter_context(nc.sbuf_tensor(f"{name}_{l}", shape, dtp))
                qa = [A(f"f3_q{i}", [128, L], BF16) for i in range(2)]
                ka = [A(f"f3_k{i}", [128, L], BF16) for i in range(2)]
                qat = [kk.tok(f"f3_q{i}", dma=True) for i in range(2)]
                kat = [kk.tok(f"f3_k{i}") for i in range(2)]
                va = [A(f"f3_v{i}", [128, 32, 65], BF16) for i in range(2)]
                vat = [kk.tok(f"f3_v{i}") for i in range(2)]
                wh = [A(f"f3_w{i}", [128, 8, 512], BF16) for i in range(2)]
                wht = [kk.tok(f"f3_w{i}", dma=True) for i in range(2)]
                P = [A(f"f3_P{i}", [128, TT], BF16) for i in range(4)]
                Pt = [kk.tok(f"f3_P{i}") for i in range(4)]
                eg = A("f3_eg", [64, TT], F32)
                egt = kk.tok("f3_eg")
                rs = A("f3_rs", [128, TT], F32)
                rst = kk.tok("f3_rs")
                den = A("f3_den", [64, TT], F32)
                dent = kk.tok("f3_den")
                osb = [A(f"f3_o{i}", [64, TT], BF16) for i in range(2)]
                osbt = [kk.tok(f"f3_o{i}", dma=True) for i in range(2)]
                tri = A("f3_tri", [128, 128], BF16)
                trif = A("f3_trif", [128, 128], F32)
                trit = kk.tok("f3_tri", dma=True)
                onesf = A("f3_onesf", [128, 64], F32)
                kk.op("pool", lambda e: e.memset(onesf[:], 1.0), writes=[tc])
                kk.dma("sp", trif[:], self.tri_d, writes=[trit], st=trit)
                kk.op("dve", lambda e: e.tensor_copy(out=tri[:], in_=trif[:]), reads=[trit], writes=[trit])
                for i in range(2):
                    kk.op("pool", lambda e, i=i: e.memset(ka[i][64:96, :], 1.0), writes=[kat[i]])
                    kk.op("pool", lambda e, i=i: e.memset(va[i][:, :, 64:65], 1.0), writes=[vat[i]])
                cw = self.cast_tok[("win", l)]
                pcount = 0
                eg2 = [eg, A("f3_egB", [64, TT], F32)]
                egt2 = [egt, kk.tok("f3_egB")]
                for jp in range(8):
                    w_, wt_ = wh[jp % 2], wht[jp % 2]
                    kk.dma("sp", w_[:], self.win_b[jf, jp].rearrange("(k p) n -> p k n", p=128), reads=[cw],
                           writes=[wt_], st=wt_)
                    for hi in range(2):
                        kk.dma("sp", qa[hi][64:67, :], self.cumq[2 * jp + hi], reads=[cq_t], writes=[qat[hi]], st=qat[hi])
                    for n in range(NT):
                        cs_ = slice(n * TT, (n + 1) * TT)
                        un = [ut[k][n] for k in range(8)]
                        kk.mm_group(ps[3][:, :], [(w_[:, k, 0:128], ufm[:, k, cs_]) for k in range(8)],
                                    reads=[wt_] + un, writes=[pst[3]])
                        for hi in range(2):
                            kk.op("act", lambda e, hi=hi: e.activation(out=qa[hi][0:64, cs_],
                                                                       in_=ps[3][64 * hi:64 * hi + 64, :], func=AF.Copy),
                                  reads=[pst[3]], writes=[qat[hi]])
                        kk.mm_group(ps[7][:, :], [(w_[:, k, 128:256], ufm[:, k, cs_]) for k in range(8)],
                                    reads=[wt_] + un, writes=[pst[7]])
                        for hi in range(2):
                            kk.op("dve", lambda e, hi=hi: e.tensor_copy(out=ka[hi][0:64, cs_],
                                                                        in_=ps[7][64 * hi:64 * hi + 64, :]),
                                  reads=[pst[7]], writes=[kat[hi]])
                    for tg in range(8):
                        for t4 in range(4):
                            tb = tg * 4 + t4
                            e_pe = kk.engs["pe"]
                            rd = [wt_] + [ut[k][tb // 4] for k in range(8)]
                            kk._deps(e_pe, rd, [pst[7]] if t4 == 0 else [])
                            inst = None
                            for k in range(8):
                                inst = e_pe.obj.matmul(ps[7][:, t4 * 128:(t4 + 1) * 128], ufm[:, k, tb * 128:(tb + 1) * 128],
                                                       w_[:, k, 256:384], start=(k == 0), stop=(k == 7))
                            e_pe.cnt += 1
                            inst.then_inc(e_pe.sem, 1)
                            kk._reg((e_pe.sem, e_pe.cnt), rd, [pst[7]])
                        for hi in range(2):
                            kk.op("dve", lambda e, hi=hi: e.tensor_copy(
                                out=va[hi][:, tg * 4:(tg + 1) * 4, 0:64],
                                in_=ps[7][:].rearrange("p (a b) -> p a b", b=128)[:, :, 64 * hi:64 * hi + 64]),
                                reads=[pst[7]], writes=[vat[hi]])
                    for qc in range(NT):
                        qs = slice(qc * TT, (qc + 1) * TT)
                        un = [ut[k][qc] for k in range(8)]
                        kk.mm_group(ps[3][:, :], [(w_[:, k, 384:512], ufm[:, k, qs]) for k in range(8)],
                                    reads=[wt_] + un, writes=[pst[3]])
                        for hi in range(2):
                            kk.op("act", lambda e, hi=hi: e.activation(out=eg2[hi][:], in_=ps[3][64 * hi:64 * hi + 64, :],
                                                                       func=AF.Exp, scale=-1.0),
                                  reads=[pst[3]], writes=[egt2[hi]])
                        for hi in range(2):
                            h = 2 * jp + hi
                            hb_ = hi
                            q_, k_, v_ = qa[hi], ka[hi], va[hi]
                            po = 4 + hi
                            nkt = 4 * qc + 4
                            slots = {}

                            def issue_S(kt):
                                nonlocal pcount
                                j = kt - 4 * qc
                                c0 = max(0, j) * 128
                                pb = pcount % 3
                                pp = pcount % 4
                                pcount += 1
                                slots[kt] = (c0, pp)
                                kk.mm_group(ps[pb][:, c0:TT],
                                            [(k_[0:67, kt * 128:(kt + 1) * 128], q_[0:67, qc * TT + c0:(qc + 1) * TT])],
                                            reads=[kat[hb_], qat[hb_]], writes=[pst[pb]])
                                kk.op("act", lambda e: e.activation(out=P[pp][:, c0:TT], in_=ps[pb][:, c0:TT], func=AF.Exp,
                                                                    scale=0.125, bias=negcT[:, kt, h:h + 1]),
                                      reads=[pst[pb], negct], writes=[Pt[pp]])
                                if j >= 0:
                                    kk.op("pool", lambda e: e.tensor_tensor(out=P[pp][:, c0:c0 + 128],
                                                                            in0=P[pp][:, c0:c0 + 128],
                                                                            in1=tri[:], op=ALU.mult),
                                          reads=[trit], writes=[Pt[pp]])
                            issue_S(0)
                            if nkt > 1:
                                issue_S(1)
                            for kt in range(nkt):
                                if kt + 2 < nkt:
                                    issue_S(kt + 2)
                                c0, pp = slots[kt]
                                e_pe = kk.engs["pe"]
                                kk._deps(e_pe, [vat[hb_], Pt[pp]], [pst[po]] if kt == 0 else [])
                                inst = e_pe.obj.matmul(ps[po][0:65, c0:TT], v_[:, kt, 0:65], P[pp][:, c0:TT],
                                                       start=(kt == 0), stop=(kt == nkt - 1))
                                e_pe.cnt += 1
                                inst.then_inc(e_pe.sem, 1)
                                kk._reg((e_pe.sem, e_pe.cnt), [vat[hb_], Pt[pp]], [pst[po]])
                            ob = hi
                            kk.op("dve", lambda e: e.tensor_copy(out=rs[64:65, :], in_=ps[po][64:65, :]),
                                  reads=[pst[po]], writes=[rst])
                            kk.mm_group(ps[6][0:64, :], [(onesf[64:65, 0:64], rs[64:65, :])], reads=[rst, tc],
                                        writes=[pst[6]])
                            kk.op("dve", lambda e: e.scalar_tensor_tensor(out=den[:], in0=eg2[hi][:], scalar=1.0,
                                                                          in1=ps[6][0:64, :], op0=ALU.add, op1=ALU.mult),
                                  reads=[egt2[hi], pst[6]], writes=[dent])
                            kk.op("dve", lambda e: e.reciprocal(out=den[:], in_=den[:]), reads=[dent], writes=[dent])
                            kk.op("dve", lambda e: e.tensor_tensor(out=osb[ob][:], in0=ps[po][0:64, :], in1=den[:],
                                                                   op=ALU.mult),
                                  reads=[pst[po], dent], writes=[osbt[ob]])
                            kk.dma("sp", o_d[h * 64:(h + 1) * 64, qs], osb[ob][:], reads=[osbt[ob]], writes=[odt],
                                   st=osbt[ob])
        kk.barrier()
        with ExitStack() as es:
            def A(name, shape, dtp):
                return es.enter_context(nc.sbuf_tensor(f"{name}_{l}", shape, dtp))
            wo = A("f4_wo", [128, 8, 1024], BF16)
            wot = kk.tok("f4_wo", dma=True)
            kk.dma("sp", wo[:], self.wout_b[jf].rearrange("(k p) n -> p k n", p=128), reads=[self.cast_tok[("wout", l)]],
                   writes=[wot], st=wot)
            ob = [A(f"f4_o{i}", [128, 8, TT], BF16) for i in range(2)]
            obt = [kk.tok(f"f4_o{i}", dma=True) for i in range(2)]
            hb = [A(f"f4_h{i}", [128, 8, TT], F32) for i in range(2)]
            hbt = [kk.tok(f"f4_h{i}", dma=True) for i in range(2)]
            hst = [kk.tok(f"f4_hs{i}", dma=True) for i in range(2)]
            yb = A("f4_y", [128, 8, TT], F32)
            ybt = [kk.tok(f"f4_y{c}") for c in range(8)]
            ysq = A("f4_ysq", [128, 8, TT], BF16)
            ysqt = [kk.tok(f"f4_ysq{c}") for c in range(8)]
            tmp = A("f4_tmp", [128, TT], F32)
            tmpt = kk.tok("f4_tmp")
            rstd = A("f4_rstd", [128, TT], F32)
            rstdt = kk.tok("f4_rstd")
            for i in range(NT):
                b = i % 2
                cs_ = slice(i * TT, (i + 1) * TT)
                kk.dma("sp", hb[b][:], hsrc[:, cs_].rearrange("(k p) t -> p k t", p=128), writes=[hbt[b]], st=hbt[b])
                kk.dma("sp", ob[b][:], o_d[:, cs_].rearrange("(k p) t -> p k t", p=128), reads=[odt], writes=[obt[b]],
                       st=obt[b])
                for c in range(8):
                    p = c % 2
                    kk.mm_group(ps[p][:], [(wo[:, k, c * 128:(c + 1) * 128], ob[b][:, k, :]) for k in range(8)],
                                reads=[wot, obt[b]], writes=[pst[p]])
                    kk.op("dve", lambda e: e.tensor_copy(out=yb[:, c, :], in_=ps[p][:]), reads=[pst[p]], writes=[ybt[c]])
                    kk.op("act", lambda e: e.activation(out=ysq[:, c, :], in_=yb[:, c, :], func=AF.Square),
                          reads=[ybt[c]], writes=[ysqt[c]])
                kk.mm_group(ps[6][:], [(self.ones[:], ysq[:, c, :]) for c in range(8)], reads=ysqt + [tc], writes=[pst[6]])
                self.rstd_from_ss(ps[6][:], pst[6], tmp[:], tmpt, rstd[:], rstdt)
                for c in range(8):
                    kk.op("dve", lambda e, c=c: e.scalar_tensor_tensor(
                        out=yb[:, c, :], in0=yb[:, c, :], scalar=self.gcol[:, g1 + c:g1 + c + 1],
                        in1=rstd[:], op0=ALU.mult, op1=ALU.mult), reads=[rstdt, tc], writes=[ybt[c]])
                kk.op("pool", lambda e: e.tensor_tensor(out=hb[b][:], in0=hb[b][:], in1=yb[:], op=ALU.add),
                      reads=ybt, writes=[hbt[b]])
                kk.dma("sp", hdst[:, cs_].rearrange("(k p) t -> p k t", p=128), hb[b][:],
                       reads=[hbt[b]], writes=[hst[b]], st=hst[b])

    def mlp_phase(self, l, hsrc, hdst):
        kk, nc = self.kk, self.nc
        TT = 512
        NT = L // TT
        g2 = (l * 4 + 2) * 8
        g3 = (l * 4 + 3) * 8
        with ExitStack() as es:
            def A(name, shape, dtp):
                return es.enter_context(nc.sbuf_tensor(f"{name}_{l}", shape, dtp))
            hb = [A(f"m_h{i}", [128, 8, TT], F32) for i in range(2)]
            hbt = [kk.tok(f"m_h{i}", dma=True) for i in range(2)]
            hst = [kk.tok(f"m_hs{i}", dma=True) for i in range(2)]
            sq = A("m_sq", [128, 8, TT], BF16)
            sqt = kk.tok("m_sq")
            ub = [A(f"m_u{i}", [128, 8, TT], BF16) for i in range(2)]
            ubt = [kk.tok(f"m_u{i}") for i in range(2)]
            hid = A("m_hid", [128, 32, TT], BF16)
            hidt = [kk.tok(f"m_hid{j}") for j in range(32)]
            rr = [A(f"m_r{i}", [128, TT], F32) for i in range(2)]
            rrt = [kk.tok(f"m_r{i}") for i in range(2)]
            w1b = [A(f"m_w1_{i}", [128, 8, 512], BF16) for i in range(3)]
            w1t = [kk.tok(f"m_w1_{i}", dma=True) for i in range(3)]
            w2b = [A(f"m_w2_{i}", [128, 32, 128], BF16) for i in range(3)]
            w2t = [kk.tok(f"m_w2_{i}", dma=True) for i in range(3)]
            yb = A("m_y", [128, 8, TT], F32)
            ybt = [kk.tok(f"m_y{c}") for c in range(8)]
            ysq = A("m_ysq", [128, 8, TT], BF16)
            ysqt = [kk.tok(f"m_ysq{c}") for c in range(8)]
            tmp = A("m_tmp", [128, TT], F32)
            tmpt = kk.tok("m_tmp")
            rstd = [A(f"m_rstd{i}", [128, TT], F32) for i in range(2)]
            rstdt = [kk.tok(f"m_rstd{i}") for i in range(2)]

            w1v = self.w1_b[l].rearrange("(r a) c -> r (a c)", a=2)
            w2v = self.w2_b[l]
            c1, c2 = self.cast_tok[("w1", l)], self.cast_tok[("w2", l)]

            def mk1(q):
                def ld(buf, tk):
                    kk.dma("sp", buf[:], w1v[:, q * 512:(q + 1) * 512].rearrange("(k p) n -> p k n", p=128),
                           reads=[c1], writes=[tk], st=tk)
                return ld

            def mk2(c):
                def ld(buf, tk):
                    kk.dma("sp", buf[:].rearrange("p j n -> p (j n)"), w2v[c], reads=[c2], writes=[tk], st=tk)
                return ld
            s1 = Stream(kk, w1b, w1t, [mk1(q) for _ in range(NT) for q in range(8)])
            s2 = Stream(kk, w2b, w2t, [mk2(c) for _ in range(NT) for c in range(8)])
            PS_S, PS_U, PS_D = 0, (1, 2, 3), (4, 5)
            ps, pst = self.ps, self.pst

            def load_h(i):
                b = i % 2
                kk.dma("sp", hb[b][:], hsrc[:, i * TT:(i + 1) * TT].rearrange("(k p) t -> p k t", p=128),
                       writes=[hbt[b]], st=hbt[b])

            def norm_in(i):
                b = i % 2
                kk.op("act", lambda e: e.activation(out=sq[:], in_=hb[b][:], func=AF.Square),
                      reads=[hbt[b]], writes=[sqt])
                kk.mm_group(ps[PS_S][:], [(self.ones[:], sq[:, k, :]) for k in range(8)],
                            reads=[sqt, self.t_const], writes=[pst[PS_S]])
                self.rstd_from_ss(ps[PS_S][:], pst[PS_S], tmp[:], tmpt, rstd[0][:], rstdt[0])
                for k in range(8):
                    kk.op("dve", lambda e, k=k: e.scalar_tensor_tensor(
                        out=ub[b][:, k, :], in0=hb[b][:, k, :], scalar=self.gcol[:, g2 + k:g2 + k + 1],
                        in1=rstd[0][:], op0=ALU.mult, op1=ALU.mult),
                        reads=[hbt[b], rstdt[0], self.t_const], writes=[ubt[b]])

            def up(i):
                b = i % 2
                if i + 1 < NT:
                    load_h(i + 1)
                s2.get(i * 8)
                for j in range(32):
                    q = i * 8 + j // 4
                    wb, wt = s1.get(q)
                    jj = j % 4
                    p = PS_U[j % 3]
                    kk.mm_group(ps[p][:], [(wb[:, k, jj * 128:(jj + 1) * 128], ub[b][:, k, :]) for k in range(8)],
                                reads=[wt, ubt[b]], writes=[pst[p]])
                    r = j % 2
                    kk.op("act", lambda e, p=p, r=r: e.activation(out=rr[r][:], in_=ps[p][:], func=AF.Relu),
                          reads=[pst[p]], writes=[rrt[r]])
                    kk.op("pool", lambda e, r=r, j=j: e.tensor_tensor(out=hid[:, j, :], in0=rr[r][:], in1=rr[r][:],
                                                                     op=ALU.mult),
                          reads=[rrt[r]], writes=[hidt[j]])

            def down(i):
                b = i % 2
                for c in range(8):
                    wb, wt = s2.get(i * 8 + c)
                    p = PS_D[c % 2]
                    kk.mm_group(ps[p][:], [(wb[:, j, :], hid[:, j, :]) for j in range(32)],
                                reads=[wt] + hidt, writes=[pst[p]])
                    kk.op("dve", lambda e, p=p, c=c: e.tensor_copy(out=yb[:, c, :], in_=ps[p][:]),
                          reads=[pst[p]], writes=[ybt[c]])
                    kk.op("act", lambda e, c=c: e.activation(out=ysq[:, c, :], in_=yb[:, c, :], func=AF.Square),
                          reads=[ybt[c]], writes=[ysqt[c]])
                import os
                stp = os.environ.get("KSTOP", "")
                if stp == "down0a":
                    return
                kk.mm_group(ps[PS_S][:], [(self.ones[:], ysq[:, c, :]) for c in range(8)],
                            reads=ysqt + [self.t_const], writes=[pst[PS_S]])
                self.rstd_from_ss(ps[PS_S][:], pst[PS_S], tmp[:], tmpt, rstd[1][:], rstdt[1])
                for c in range(8):
                    kk.op("dve", lambda e, c=c: e.scalar_tensor_tensor(
                        out=yb[:, c, :], in0=yb[:, c, :], scalar=self.gcol[:, g3 + c:g3 + c + 1],
                        in1=rstd[1][:], op0=ALU.mult, op1=ALU.mult),
                        reads=[rstdt[1], self.t_const], writes=[ybt[c]])
                if stp == "down0b":
                    return
                kk.op("pool", lambda e: e.tensor_tensor(out=hb[b][:], in0=hb[b][:], in1=yb[:], op=ALU.add),
                      reads=ybt, writes=[hbt[b]])
                if stp == "down0c":
                    return
                kk.dma("sp", hdst[:, i * TT:(i + 1) * TT].rearrange("(k p) t -> p k t", p=128), hb[b][:],
                       reads=[hbt[b]], writes=[hst[b]], st=hst[b])

            import os
            stop = os.environ.get("KSTOP", "")
            if stop == "cast":
                return
            load_h(0)
            norm_in(0)
            if stop == "norm0":
                return
            for i in range(NT):
                up(i)
                if stop == "up0":
                    return
                if i + 1 < NT and stop != "down0x":
                    norm_in(i + 1)
                if stop == "norm1":
                    return
                down(i)
                if stop.startswith("down0"):
                    return


FULL_PHASES = [(("s5" if l % 2 == 0 else "fox") if w == 0 else "mlp", l) for l in range(DEPTH) for w in range(2)]


def prep_inputs(inputs, b):
    m = {}
    m["x"] = np.ascontiguousarray(inputs["x"][b].T)
    g = np.asarray(inputs["norm_gains"], np.float32)
    m["gcol"] = np.ascontiguousarray(g.reshape(DEPTH * 4, 8, 128).transpose(2, 0, 1).reshape(128, DEPTH * 4 * 8))
    m["w1"] = np.ascontiguousarray(np.asarray(inputs["mlp_w1"], np.float32).reshape(DEPTH, -1, 2048))
    w2 = np.asarray(inputs["mlp_w2"], np.float32).reshape(DEPTH, 32, 128, 8, 128)
    m["w2"] = np.ascontiguousarray(w2.transpose(0, 3, 2, 1, 4).reshape(DEPTH, -1, 2048))
    NS = 2
    are = np.asarray(inputs["s5_a_re"], np.float32); aim = np.asarray(inputs["s5_a_im"], np.float32)
    ldt = np.asarray(inputs["s5_log_dt"], np.float32)
    def pairT(a):
        return np.ascontiguousarray(a.reshape(NS, 32, 2, 64).transpose(0, 2, 3, 1).reshape(NS, 128, 32))
    m["s5_at_re"] = pairT(are)
    m["s5_at_im"] = pairT(aim)
    m["s5_ldt"] = pairT(np.broadcast_to(ldt[:, :, None], (NS, 64, 64)))
    def pairB(b_):
        return np.ascontiguousarray(b_.reshape(NS, 32, 2, 64, 16).transpose(0, 2, 3, 1, 4).reshape(NS, 128, 512))
    m["s5_bre"] = pairB(np.asarray(inputs["s5_b_re"], np.float32))
    m["s5_bim"] = pairB(np.asarray(inputs["s5_b_im"], np.float32))
    m["s5_cre"] = pairB(np.asarray(inputs["s5_c_re"], np.float32).transpose(0, 1, 3, 2))
    m["s5_cim"] = pairB(np.asarray(inputs["s5_c_im"], np.float32).transpose(0, 1, 3, 2))
    m["s5_dcol"] = np.ascontiguousarray(np.asarray(inputs["s5_d"], np.float32).reshape(NS, 8, 128).transpose(0, 2, 1))
    m["wglu"] = np.ascontiguousarray(np.asarray(inputs["s5_w_glu"], np.float32))
    m["ident"] = np.eye(128, dtype=np.float32)
    NF = 2
    win = np.asarray(inputs["fox_w_in"], np.float32)
    m["win"] = np.ascontiguousarray(win[:, :, :4096].reshape(NF, 1024, 4, 8, 2, 64).transpose(0, 3, 1, 2, 4, 5).reshape(NF, 8, 1024, 512))
    m["wf"] = np.ascontiguousarray(win[:, :, 4096:4112])
    m["bf"] = np.ascontiguousarray(np.asarray(inputs["fox_b_f"], np.float32).reshape(NF, 16, 1))
    m["wout"] = np.ascontiguousarray(np.asarray(inputs["fox_w_out"], np.float32))
    m["tri"] = np.triu(np.ones((128, 128), np.float32))
    return m


def run(inputs, phases=None, cores=NCORES, trace=False):
    prog = Prog(phases or FULL_PHASES)
    nc = prog.build()
    shared = None
    in_maps = []
    for b in range(cores):
        m = prep_inputs(inputs, b)
        if shared is None:
            shared = m
        else:
            for k_ in m:
                if k_ != "x":
                    m[k_] = shared[k_]
        in_maps.append(m)
    res = run_bass_kernel_spmd(nc, in_maps, core_ids=list(range(cores)), trace=trace)
    outs = [np.ascontiguousarray(r["out"].T) for r in res.results]
    return np.stack(outs, 0), res


def kernel(**inputs):
    out, _ = run(inputs)
    return out.astype(np.float32)
```

```python
import numpy as np
from contextlib import ExitStack
import concourse.bass as bass
import concourse.mybir as mybir
from concourse.bass_utils import run_bass_kernel_spmd

F32 = mybir.dt.float32
BF16 = mybir.dt.bfloat16
AF = mybir.ActivationFunctionType
ALU = mybir.AluOpType

D = 1024
L = 4096
DEPTH = 4
HID = 4096
EPS = 1e-6
NCORES = 8


class Tok:
    __slots__ = ("name", "w", "r", "ds")

    def __init__(self, name, ds=None):
        self.name = name
        self.w = None
        self.r = {}
        self.ds = ds


class DSem:
    def __init__(self, sem):
        self.sem = sem
        self.cnt = 0


class Eng:
    def __init__(self, name, obj, sem):
        self.name = name
        self.obj = obj
        self.sem = sem
        self.cnt = 0
        self.seen = {}


class K:
    def __init__(self, nc):
        self.nc = nc
        self.engs = {}
        for name, obj in (("pe", nc.tensor), ("act", nc.scalar), ("dve", nc.vector),
                          ("pool", nc.gpsimd), ("sp", nc.sync)):
            self.engs[name] = Eng(name, obj, nc.alloc_semaphore("s_" + name))
        self.dsems = {}
        self.uid = 0

    def tok(self, name, dma=False):
        t = Tok(name)
        if dma:
            if name not in self.dsems:
                self.dsems[name] = DSem(self.nc.alloc_semaphore("d_" + name))
            t.ds = self.dsems[name]
        return t

    def _wait(self, e, ev):
        sem, val = ev
        if sem is e.sem and e.name == "pe":
            return
        key = id(sem)
        if e.seen.get(key, 0) >= val:
            return
        e.obj.wait_ge(sem, val)
        e.seen[key] = val

    def _deps(self, e, reads, writes):
        for t in reads:
            if t.w is not None:
                self._wait(e, t.w)
        for t in writes:
            if t.w is not None:
                self._wait(e, t.w)
            for ev in t.r.values():
                self._wait(e, ev)

    def _reg(self, ev, reads, writes):
        for t in reads:
            t.r[id(ev[0])] = ev
        for t in writes:
            t.w = ev
            t.r = {}

    def op(self, en, fn, reads=(), writes=()):
        e = self.engs[en]
        self._deps(e, reads, writes)
        inst = fn(e.obj)
        e.cnt += 1
        inst.then_inc(e.sem, 1)
        self._reg((e.sem, e.cnt), reads, writes)
        return inst

    def mm_group(self, out_ap, pairs, reads=(), writes=(), **kw):
        e = self.engs["pe"]
        self._deps(e, reads, writes)
        n = len(pairs)
        inst = None
        for i, (lhsT, rhs) in enumerate(pairs):
            inst = e.obj.matmul(out_ap, lhsT, rhs, start=(i == 0), stop=(i == n - 1), **kw)
        e.cnt += 1
        inst.then_inc(e.sem, 1)
        self._reg((e.sem, e.cnt), reads, writes)

    def dma(self, qn, out, in_, reads=(), writes=(), st=None, **kw):
        e = self.engs[qn]
        self._deps(e, reads, writes)
        inst = e.obj.dma_start(out=out, in_=in_, **kw)
        ds = st.ds
        ds.cnt += 16
        inst.then_inc(ds.sem, 16)
        self._reg((ds.sem, ds.cnt), reads, writes)

    def barrier(self):
        comp = [self.engs[n] for n in ("pe", "act", "dve", "pool")]
        for e in self.engs.values():
            for e2 in comp:
                if e2 is not e and e2.cnt > 0:
                    self._wait(e, (e2.sem, e2.cnt))
            for nm_, ds in self.dsems.items():
                if ds.cnt > 0 and not nm_.startswith("cast_"):
                    self._wait(e, (ds.sem, ds.cnt))


class Stream:
    def __init__(self, kk, bufs, toks, loads):
        self.kk, self.bufs, self.toks, self.loads = kk, bufs, toks, loads
        self.nxt = 0

    def get(self, i):
        nb = len(self.bufs)
        while self.nxt < len(self.loads) and self.nxt <= i + nb - 1:
            j = self.nxt
            self.loads[j](self.bufs[j % nb], self.toks[j % nb])
            self.nxt += 1
        return self.bufs[i % nb], self.toks[i % nb]


class Prog:
    def __init__(self, phases):
        self.phases = phases
        nc = self.nc = bass.Bass("TRN2", target_bir_lowering=False)
        self.kk = K(nc)
        dt = nc.dram_tensor
        self.x = dt("x", [D, L], F32, kind="ExternalInput").ap()
        self.out = dt("out", [D, L], F32, kind="ExternalOutput").ap()
        self.gcol_d = dt("gcol", [128, DEPTH * 4 * 8], F32, kind="ExternalInput").ap()
        self.w1_d = dt("w1", [DEPTH, D * HID // 2048, 2048], F32, kind="ExternalInput").ap()
        self.w2_d = dt("w2", [DEPTH, D * HID // 2048, 2048], F32, kind="ExternalInput").ap()
        self.w1_b = dt("w1b", [DEPTH, D * HID // 2048, 2048], BF16, kind="Internal").ap()
        self.w2_b = dt("w2b", [DEPTH, 8, 128, 4096], BF16, kind="Internal").ap()
        NS = 2
        self.s5_at_re = dt("s5_at_re", [NS, 128, 32], F32, kind="ExternalInput").ap()
        self.s5_at_im = dt("s5_at_im", [NS, 128, 32], F32, kind="ExternalInput").ap()
        self.s5_ldt = dt("s5_ldt", [NS, 128, 32], F32, kind="ExternalInput").ap()
        self.s5_b_re = dt("s5_bre", [NS, 128, 512], F32, kind="ExternalInput").ap()
        self.s5_b_im = dt("s5_bim", [NS, 128, 512], F32, kind="ExternalInput").ap()
        self.s5_c_re = dt("s5_cre", [NS, 128, 512], F32, kind="ExternalInput").ap()
        self.s5_c_im = dt("s5_cim", [NS, 128, 512], F32, kind="ExternalInput").ap()
        self.s5_dcol = dt("s5_dcol", [NS, 128, 8], F32, kind="ExternalInput").ap()
        self.wglu_d = dt("wglu", [NS, 1024, 2048], F32, kind="ExternalInput").ap()
        self.wglu_b = dt("wglub", [NS, 1024, 2048], BF16, kind="Internal").ap()
        NF = 2
        self.win_d = dt("win", [NF, 8, 1024, 512], F32, kind="ExternalInput").ap()
        self.win_b = dt("winb", [NF, 8, 1024, 512], BF16, kind="Internal").ap()
        self.wf_d = dt("wf", [NF, 1024, 16], F32, kind="ExternalInput").ap()
        self.wf_b = dt("wfb", [NF, 1024, 16], BF16, kind="Internal").ap()
        self.bf_d = dt("bf", [NF, 16, 1], F32, kind="ExternalInput").ap()
        self.wout_d = dt("wout", [NF, 1024, 1024], F32, kind="ExternalInput").ap()
        self.wout_b = dt("woutb", [NF, 1024, 1024], BF16, kind="Internal").ap()
        self.tri_d = dt("tri", [128, 128], F32, kind="ExternalInput").ap()
        self.cumq = dt("cumq", [16, 3, L], BF16, kind="Internal").ap()
        self.o_d = dt("o_d", [D, L], BF16, kind="Internal").ap()
        self.ps = [nc.alloc_psum_tensor(f"ps{i}", [128, 512], F32) for i in range(8)]
        self.pst = [self.kk.tok(f"ps{i}") for i in range(8)]
        self.ones = nc.alloc_sbuf_tensor("ones", [128, 128], BF16)
        self.gcol = nc.alloc_sbuf_tensor("gcol_sb", [128, DEPTH * 4 * 8], F32)
        self.epsc = nc.alloc_sbuf_tensor("epsc", [128, 1], F32)
        self.hpic = nc.alloc_sbuf_tensor("hpic", [128, 1], F32)
        self.onec = nc.alloc_sbuf_tensor("onec", [128, 1], F32)
        self.ident = nc.alloc_sbuf_tensor("ident_sb", [128, 128], F32)
        self.ident_d = dt("ident", [128, 128], F32, kind="ExternalInput").ap()
        self.cast_tok = {}

    def build(self):
        kk, nc = self.kk, self.nc
        t_const = kk.tok("const", dma=True)
        kk.op("dve", lambda e: e.memset(self.ones[:], 1.0), writes=[t_const])
        kk.op("dve", lambda e: e.memset(self.epsc[:], EPS), writes=[t_const])
        kk.op("dve", lambda e: e.memset(self.hpic[:], float(np.pi / 2)), writes=[t_const])
        kk.op("dve", lambda e: e.memset(self.onec[:], 1.0), writes=[t_const])
        kk.dma("sp", self.gcol[:], self.gcol_d, writes=[t_const], st=t_const)
        kk.dma("sp", self.ident[:], self.ident_d, writes=[t_const], st=t_const)
        self.t_const = t_const
        layers = sorted({l for (_, l) in self.phases})
        for l in layers:
            w2cast = self.w2_b.rearrange("l c p (a m) -> l (c p a) m", a=2)
            for nm, src, dst in (("w1", self.w1_d, self.w1_b), ("w2", self.w2_d, w2cast)):
                if any(p == "mlp" and pl == l for p, pl in self.phases):
                    t = kk.tok(f"cast_{nm}{l}", dma=True)
                    kk.dma("pool", dst[l], src[l], writes=[t], st=t)
                    self.cast_tok[(nm, l)] = t
            if any(p == "s5" and pl == l for p, pl in self.phases):
                t = kk.tok(f"cast_glu{l}", dma=True)
                kk.dma("pool", self.wglu_b[l // 2], self.wglu_d[l // 2], writes=[t], st=t)
                self.cast_tok[("glu", l)] = t
            if any(p == "fox" and pl == l for p, pl in self.phases):
                jf = l // 2
                for nm, src, dst in (("win", self.win_d[jf].rearrange("h (r a) n -> (h r) (a n)", a=4),
                                      self.win_b[jf].rearrange("h (r a) n -> (h r) (a n)", a=4)),
                                     ("wf", self.wf_d[jf].rearrange("(r a) n -> r (a n)", a=128),
                                      self.wf_b[jf].rearrange("(r a) n -> r (a n)", a=128)),
                                     ("wout", self.wout_d[jf].rearrange("(r a) n -> r (a n)", a=2),
                                      self.wout_b[jf].rearrange("(r a) n -> r (a n)", a=2))):
                    t = kk.tok(f"cast_{nm}{l}", dma=True)
                    kk.dma("pool", dst, src, writes=[t], st=t)
                    self.cast_tok[(nm, l)] = t
        kk.barrier()
        cur = self.x
        for (p, l) in self.phases:
            if p == "mlp":
                self.mlp_phase(l, cur, self.out)
            elif p == "s5":
                self.s5_phase(l, cur, self.out)
            elif p == "fox":
                self.fox_phase(l, cur, self.out)
            cur = self.out
            kk.barrier()
        return nc

    def rstd_from_ss(self, ss_ps, ss_tok, tmp, tmp_tok, rstd, rstd_tok):
        kk = self.kk
        kk.op("act", lambda e: e.activation(out=tmp, in_=ss_ps, func=AF.Sqrt, bias=self.epsc[:], scale=1.0 / D),
              reads=[ss_tok, self.t_const], writes=[tmp_tok])
        kk.op("dve", lambda e: e.reciprocal(out=rstd, in_=tmp), reads=[tmp_tok], writes=[rstd_tok])

    def s5_phase(self, l, hsrc, hdst):
        kk, nc = self.kk, self.nc
        j5 = l // 2
        TT = 512
        NT = L // TT
        g0 = (l * 4 + 0) * 8
        g1 = (l * 4 + 1) * 8
        ps, pst = self.ps, self.pst
        tc = self.t_const
        with ExitStack() as es0:
            ufm = es0.enter_context(nc.sbuf_tensor(f"s_u_{l}", [128, 8, L], BF16))
            ut = [[kk.tok(f"s_u{k}_{n}") for n in range(NT)] for k in range(8)]
            with ExitStack() as es:
                def A(name, shape, dtp):
                    return es.enter_context(nc.sbuf_tensor(f"{name}_{l}", shape, dtp))
                hb = [A(f"s1_h{i}", [128, 8, TT], F32) for i in range(2)]
                hbt = [kk.tok(f"s1_h{i}", dma=True) for i in range(2)]
                sq = [A(f"s1_sq{i}", [128, 8, TT], BF16) for i in range(2)]
                sqt = [kk.tok(f"s1_sq{i}") for i in range(2)]
                tmp = A("s1_tmp", [128, TT], F32)
                tmpt = kk.tok("s1_tmp")
                rstd = [A(f"s1_rstd{i}", [128, TT], F32) for i in range(2)]
                rstdt = [kk.tok(f"s1_rstd{i}") for i in range(2)]
                for i in range(NT):
                    b = i % 2
                    kk.dma("sp", hb[b][:], hsrc[:, i * TT:(i + 1) * TT].rearrange("(k p) t -> p k t", p=128),
                           writes=[hbt[b]], st=hbt[b])
                    kk.op("act", lambda e: e.activation(out=sq[b][:], in_=hb[b][:], func=AF.Square),
                          reads=[hbt[b]], writes=[sqt[b]])
                    kk.mm_group(ps[6 + b][:], [(self.ones[:], sq[b][:, k, :]) for k in range(8)],
                                reads=[sqt[b], tc], writes=[pst[6 + b]])
                    self.rstd_from_ss(ps[6 + b][:], pst[6 + b], tmp[:], tmpt, rstd[b][:], rstdt[b])
                    for k in range(8):
                        kk.op("dve", lambda e, k=k: e.scalar_tensor_tensor(
                            out=ufm[:, k, i * TT:(i + 1) * TT], in0=hb[b][:, k, :],
                            scalar=self.gcol[:, g0 + k:g0 + k + 1], in1=rstd[b][:], op0=ALU.mult, op1=ALU.mult),
                            reads=[hbt[b], rstdt[b], tc], writes=[ut[k][i]])
            kk.barrier()
            import os
            S5STOP = os.environ.get("S5STOP", "")
            if S5STOP == "p1":
                return
            with ExitStack() as es:
                def A(name, shape, dtp):
                    return es.enter_context(nc.sbuf_tensor(f"{name}_{l}", shape, dtp))
                NSL = 48
                sc = A("s2_sc", [128, NSL, 32], F32)
                SC = {}

                def S(name):
                    if name not in SC:
                        assert len(SC) < NSL
                        SC[name] = (len(SC), kk.tok("sc_" + name, dma=name in ("atre", "atim", "ldt")))
                    i_, t_ = SC[name]
                    return sc[:, i_, :], t_

                def tt(o, a, b, op):
                    (oa, ot), (aa, at), (ba, bt) = S(o), S(a), S(b)
                    kk.op("dve", lambda e: e.tensor_tensor(out=oa, in0=aa, in1=ba, op=op), reads=[at, bt], writes=[ot])

                def tsc(o, a, s1, op0, s2=None, op1=None):
                    (oa, ot), (aa, at) = S(o), S(a)
                    if s2 is None:
                        kk.op("dve", lambda e: e.tensor_scalar(out=oa, in0=aa, scalar1=s1, scalar2=None, op0=op0),
                              reads=[at], writes=[ot])
                    else:
                        kk.op("dve", lambda e: e.tensor_scalar(out=oa, in0=aa, scalar1=s1, scalar2=s2, op0=op0, op1=op1),
                              reads=[at], writes=[ot])

                def act(o, a, func, **kw):
                    (oa, ot), (aa, at) = S(o), S(a)
                    kk.op("act", lambda e: e.activation(out=oa, in_=aa, func=func, **kw), reads=[at, tc], writes=[ot])

                for nm, src in (("atre", self.s5_at_re), ("atim", self.s5_at_im), ("ldt", self.s5_ldt)):
                    oa, ot = S(nm)
                    kk.dma("sp", oa, src[j5], writes=[ot], st=ot)
                act("dt", "ldt", AF.Exp)
                tt("lr", "atre", "dt", ALU.mult)
                tt("th", "atim", "dt", ALU.mult)
                act("rho", "lr", AF.Exp)
                act("s_0", "th", AF.Sin, scale=1.0 / 16)
                act("c_0", "th", AF.Sin, scale=1.0 / 16, bias=self.hpic[:])

                def csq(ci, si, co, so):
                    tt("cc", ci, ci, ALU.mult)
                    tt("ss", si, si, ALU.mult)
                    tt("cs", ci, si, ALU.mult)
                    tt(co, "cc", "ss", ALU.subtract)
                    tsc(so, "cs", 2.0, ALU.mult)
                csq("c_0", "s_0", "c_1", "s_1")
                csq("c_1", "s_1", "c_0", "s_0")
                csq("c_0", "s_0", "c_1", "s_1")
                csq("c_1", "s_1", "ec0", "es0")
                for k in range(1, 10):
                    csq(f"ec{k - 1}", f"es{k - 1}", f"ec{k}", f"es{k}")
                tsc("nes9", "es9", -1.0, ALU.mult)
                tt("abr", "rho", "ec0", ALU.mult)
                tt("abi", "rho", "es0", ALU.mult)
                tsc("am1", "abr", -1.0, ALU.add)
                tt("den", "atre", "atre", ALU.mult)
                tt("d2", "atim", "atim", ALU.mult)
                tt("den", "den", "d2", ALU.add)
                (oa, ot), (aa, at) = S("rden"), S("den")
                kk.op("dve", lambda e: e.reciprocal(out=oa, in_=aa), reads=[at], writes=[ot])
                tt("nr", "am1", "atre", ALU.mult)
                tt("t", "abi", "atim", ALU.mult)
                tt("nr", "nr", "t", ALU.add)
                tt("ni", "abi", "atre", ALU.mult)
                tt("t", "am1", "atim", ALU.mult)
                tt("ni", "ni", "t", ALU.subtract)
                tt("kr", "nr", "rden", ALU.mult)
                tt("ki", "ni", "rden", ALU.mult)

                def col(name, q):
                    i_, t_ = SC[name]
                    return sc[:, i_, q:q + 1], t_
                if S5STOP == "sc":
                    return

                bre = A("s2_bre", [128, 32, 16], F32)
                bim = A("s2_bim", [128, 32, 16], F32)
                cre = A("s2_cre", [128, 32, 16], F32)
                cim = A("s2_cim", [128, 32, 16], F32)
                dcol = A("s2_dcol", [128, 8], F32)
                pt = kk.tok("s2_par", dma=True)
                for dst, src in ((bre, self.s5_b_re), (bim, self.s5_b_im), (cre, self.s5_c_re), (cim, self.s5_c_im)):
                    kk.dma("sp", dst[:].rearrange("p q h -> p (q h)"), src[j5], writes=[pt], st=pt)
                kk.dma("sp", dcol[:], self.s5_dcol[j5], writes=[pt], st=pt)
                wtmp = [[A(f"s2_wt{c}{m}", [128, 128], F32) for m in range(4)] for c in range(2)]
                wtmpt = [[kk.tok(f"s2_wt{c}{m}") for m in range(4)] for c in range(2)]
                WB = [[A(f"s2_wb{c}{m}", [128, 128], BF16) for m in range(4)] for c in range(2)]
                WBt = [[kk.tok(f"s2_wb{c}{m}") for m in range(4)] for c in range(2)]
                CW = [[A(f"s2_cw{c}{m}", [128, 128], BF16) for m in range(4)] for c in range(2)]
                CWt = [[kk.tok(f"s2_cw{c}{m}") for m in range(4)] for c in range(2)]
                t16 = [A(f"s2_t16{i}", [128, 16], F32) for i in range(2)]
                t16t = [kk.tok(f"s2_t16{i}") for i in range(2)]
                tabc = [A(f"s2_tc{m}", [128, 512], F32) for m in range(4)]
                tabs = [A(f"s2_ts{m}", [128, 512], F32) for m in range(4)]
                tabt = [kk.tok(f"s2_tab{m}") for m in range(4)]
                tw = [A(f"s2_tw{i}", [128, 256], F32) for i in range(2)]
                twt = [kk.tok(f"s2_tw{i}") for i in range(2)]
                rbc = [A(f"s2_rb{m}", [128, 512], F32) for m in range(4)]
                rbt = [kk.tok(f"s2_rb{m}") for m in range(4)]
                onesf = A("s2_onesf", [128, 512], F32)
                ini = A("s2_ini", [128, 4, 2], F32)
                init = [kk.tok(f"s2_ini{m}") for m in range(4)]
                itmp = A("s2_itmp", [128, 4, 2], F32)
                T = [[A(f"s2_T{pb}{i}", [128, 512], F32) for i in range(8)] for pb in range(2)]
                Tt = [[kk.tok(f"s2_T{pb}{i}") for i in range(8)] for pb in range(2)]
                U = [[A(f"s2_U{pb}{i}", [128, 512], BF16) for i in range(4)] for pb in range(2)]
                Ut = [[kk.tok(f"s2_U{pb}{i}") for i in range(4)] for pb in range(2)]
                NCW = [A(f"s2_ncw{m}", [128, 128], BF16) for m in range(4)]
                NCWt = [kk.tok(f"s2_ncw{m}") for m in range(4)]
                xb = [[A(f"s2_x{pb}{c}", [128, 512], BF16) for c in range(2)] for pb in range(2)]
                xbt = [[kk.tok(f"s2_x{pb}{c}") for c in range(2)] for pb in range(2)]
                yv = [A(f"s2_yv{i}", [128, 512], F32) for i in range(2)]
                yvt = [kk.tok(f"s2_yv{i}") for i in range(2)]
                for c in range(2):
                    for m in range(4):
                        kk.op("pool", lambda e, c=c, m=m: e.memset(wtmp[c][m][:], 0.0), writes=[wtmpt[c][m]])
                        kk.op("pool", lambda e, c=c, m=m: e.memset(CW[c][m][:], 0.0), writes=[CWt[c][m]])
                        if c == 0:
                            kk.op("pool", lambda e, m=m: e.memset(NCW[m][:], 0.0), writes=[NCWt[m]])
                for m in range(4):
                    kk.op("pool", lambda e, m=m: e.memset(tabc[m][:, 0:1], 1.0), writes=[tabt[m]])
                    kk.op("pool", lambda e, m=m: e.memset(tabs[m][:, 0:1], 0.0), writes=[tabt[m]])
                kk.op("pool", lambda e: e.memset(onesf[:], 1.0), writes=[tc])

                pcount = 0
                for k in range(8):
                    for m in range(4):
                        q = 4 * k + m
                        (kr, krt), (ki, kit) = col("kr", q), col("ki", q)
                        kk.op("dve", lambda e: e.tensor_scalar(out=t16[0][:], in0=bim[:, q, :], scalar1=ki, scalar2=None,
                                                               op0=ALU.mult), reads=[pt, kit], writes=[t16t[0]])
                        kk.op("dve", lambda e: e.tensor_scalar(out=t16[1][:], in0=bim[:, q, :], scalar1=kr, scalar2=None,
                                                               op0=ALU.mult), reads=[pt, krt], writes=[t16t[1]])
                        for i2 in range(2):
                            r0, r1 = 64 * i2, 64 * i2 + 64
                            c0 = 32 * m + 16 * i2
                            kk.op("dve", lambda e: e.scalar_tensor_tensor(
                                out=wtmp[0][m][r0:r1, c0:c0 + 16], in0=bre[r0:r1, q, :], scalar=kr[r0:r1, :],
                                in1=t16[0][r0:r1, :], op0=ALU.mult, op1=ALU.subtract),
                                reads=[pt, krt, t16t[0]], writes=[wtmpt[0][m]])
                            kk.op("dve", lambda e: e.scalar_tensor_tensor(
                                out=wtmp[1][m][r0:r1, c0:c0 + 16], in0=bre[r0:r1, q, :], scalar=ki[r0:r1, :],
                                in1=t16[1][r0:r1, :], op0=ALU.mult, op1=ALU.add),
                                reads=[pt, kit, t16t[1]], writes=[wtmpt[1][m]])
                            kk.op("pool", lambda e: e.tensor_copy(out=CW[0][m][r0:r1, c0:c0 + 16], in_=cre[r0:r1, q, :]),
                                  reads=[pt], writes=[CWt[0][m]])
                            kk.op("pool", lambda e: e.tensor_scalar(out=CW[1][m][r0:r1, c0:c0 + 16], in0=cim[r0:r1, q, :],
                                                                    scalar1=-1.0, scalar2=None, op0=ALU.mult),
                                  reads=[pt], writes=[CWt[1][m]])
                            kk.op("pool", lambda e: e.tensor_scalar(out=NCW[m][r0:r1, c0:c0 + 16], in0=cre[r0:r1, q, :],
                                                                    scalar1=-1.0, scalar2=None, op0=ALU.mult),
                                  reads=[pt], writes=[NCWt[m]])
                        if S5STOP == "wg1":
                            return
                        for c in range(2):
                            kk.op("pe", lambda e: e.transpose(out=ps[6][:, 0:128], in_=wtmp[c][m][:], identity=self.ident[:]),
                                  reads=[wtmpt[c][m], tc], writes=[pst[6]])
                            kk.op("act", lambda e: e.activation(out=WB[c][m][:], in_=ps[6][:, 0:128], func=AF.Copy),
                                  reads=[pst[6]], writes=[WBt[c][m]])
                        if S5STOP == "wg2":
                            return
                        for lv in range(9):
                            w = 1 << lv
                            (ec, ect), (esn, est) = col(f"ec{lv}", q), col(f"es{lv}", q)
                            kk.op("dve", lambda e: e.tensor_scalar(out=tw[0][:, 0:w], in0=tabs[m][:, 0:w], scalar1=esn,
                                                                   scalar2=None, op0=ALU.mult),
                                  reads=[tabt[m], est], writes=[twt[0]])
                            kk.op("dve", lambda e: e.tensor_scalar(out=tw[1][:, 0:w], in0=tabs[m][:, 0:w], scalar1=ec,
                                                                   scalar2=None, op0=ALU.mult),
                                  reads=[tabt[m], ect], writes=[twt[1]])
                            kk.op("dve", lambda e: e.scalar_tensor_tensor(
                                out=tabs[m][:, w:2 * w], in0=tabc[m][:, 0:w], scalar=esn, in1=tw[1][:, 0:w],
                                op0=ALU.mult, op1=ALU.add), reads=[twt[1], est], writes=[tabt[m]])
                            kk.op("dve", lambda e: e.scalar_tensor_tensor(
                                out=tabc[m][:, w:2 * w], in0=tabc[m][:, 0:w], scalar=ec, in1=tw[0][:, 0:w],
                                op0=ALU.mult, op1=ALU.subtract), reads=[twt[0], ect], writes=[tabt[m]])
                        if S5STOP == "wg3":
                            return
                        (rh, rht) = col("rho", q)
                        kk.op("dve", lambda e: e.tensor_scalar(out=rbc[m][:], in0=onesf[:], scalar1=rh, scalar2=None,
                                                               op0=ALU.mult), reads=[rht, tc], writes=[rbt[m]])
                        if S5STOP == "wg4":
                            return
                        kk.op("dve", lambda e: e.memset(ini[:, m, :], 0.0), writes=[init[m]])
                        if S5STOP == "m0":
                            return
                        if S5STOP in ("m1", "m2", "m3") and m == int(S5STOP[1]):
                            return
                    if S5STOP == "wgen":
                        return
                    items = [(n, m) for n in range(NT) for m in range(4)]

                    def stA(it, n, m):
                        cs_ = slice(n * TT, (n + 1) * TT)
                        pb = it % 2
                        pa, pbk = (0, 1) if pb == 0 else (2, 3)
                        Tb, Ttb = T[pb], Tt[pb]
                        kk.mm_group(ps[pa][:], [(WB[0][m][:], ufm[:, k, cs_])], reads=[WBt[0][m], ut[k][n]],
                                    writes=[pst[pa]])
                        kk.mm_group(ps[pbk][:], [(WB[1][m][:], ufm[:, k, cs_])], reads=[WBt[1][m], ut[k][n]],
                                    writes=[pst[pbk]])

                    def stA2(it, n, m):
                        pb = it % 2
                        pa, pbk = (0, 1) if pb == 0 else (2, 3)
                        Tb, Ttb = T[pb], Tt[pb]
                        for (ti, pp, tab) in ((0, pa, tabc), (1, pbk, tabs), (2, pbk, tabc), (3, pa, tabs)):
                            kk.op("dve", lambda e, ti=ti, pp=pp, tab=tab: e.tensor_tensor(
                                out=Tb[ti][:], in0=ps[pp][:], in1=tab[m][:], op=ALU.mult),
                                reads=[pst[pp], tabt[m]], writes=[Ttb[ti]])

                    def stB(it, n, m):
                        Tb, Ttb = T[it % 2], Tt[it % 2]
                        kk.op("dve", lambda e: e.tensor_tensor(out=Tb[4][:], in0=Tb[0][:], in1=Tb[1][:], op=ALU.add),
                              reads=[Ttb[0], Ttb[1]], writes=[Ttb[4]])
                        kk.op("pool", lambda e: e.tensor_tensor(out=Tb[5][:], in0=Tb[2][:], in1=Tb[3][:], op=ALU.subtract),
                              reads=[Ttb[2], Ttb[3]], writes=[Ttb[5]])

                    def stC(it, n, m):
                        q = 4 * k + m
                        Tb, Ttb = T[it % 2], Tt[it % 2]
                        kk.op("dve", lambda e: e.tensor_tensor_scan(out=Tb[6][:], data0=rbc[m][:], data1=Tb[4][:],
                                                                    initial=ini[:, m, 0:1], op0=ALU.mult, op1=ALU.add),
                              reads=[rbt[m], Ttb[4], init[m]], writes=[Ttb[6]])
                        kk.op("dve", lambda e: e.tensor_tensor_scan(out=Tb[7][:], data0=rbc[m][:], data1=Tb[5][:],
                                                                    initial=ini[:, m, 1:2], op0=ALU.mult, op1=ALU.add),
                              reads=[rbt[m], Ttb[5], init[m]], writes=[Ttb[7]])
                        (e9c, e9ct), (e9s, e9st) = col("ec9", q), col("es9", q)
                        (ne9s, ne9st) = col("nes9", q)
                        kk.op("act", lambda e: e.activation(out=itmp[:, m, 0:1], in_=Tb[7][:, TT - 1:TT], func=AF.Identity,
                                                            scale=ne9s), reads=[Ttb[7], ne9st], writes=[init[m]])
                        kk.op("act", lambda e: e.activation(out=itmp[:, m, 1:2], in_=Tb[7][:, TT - 1:TT], func=AF.Identity,
                                                            scale=e9c), reads=[Ttb[7], e9ct], writes=[init[m]])
                        kk.op("act", lambda e: e.activation(out=ini[:, m, 0:1], in_=Tb[6][:, TT - 1:TT], func=AF.Identity,
                                                            scale=e9c, bias=itmp[:, m, 0:1]),
                              reads=[Ttb[6], e9ct], writes=[init[m]])
                        kk.op("act", lambda e: e.activation(out=ini[:, m, 1:2], in_=Tb[6][:, TT - 1:TT], func=AF.Identity,
                                                            scale=e9s, bias=itmp[:, m, 1:2]),
                              reads=[Ttb[6], e9st], writes=[init[m]])

                    def stD(it, n, m):
                        pb = it % 2
                        Tb, Ttb = T[pb], Tt[pb]
                        for (ti, zi_, tab) in ((0, 6, tabc), (1, 7, tabs), (2, 6, tabs), (3, 7, tabc)):
                            kk.op("pool", lambda e, ti=ti, zi_=zi_, tab=tab: e.tensor_tensor(
                                out=U[pb][ti][:], in0=Tb[zi_][:], in1=tab[m][:], op=ALU.mult),
                                reads=[Ttb[zi_], tabt[m]], writes=[Ut[pb][ti]])

                    def stE(it, n, m):
                        cs_ = slice(n * TT, (n + 1) * TT)
                        pb = it % 2
                        py = 4 + (n % 2)
                        e_pe = kk.engs["pe"]
                        rd_ = [CWt[0][m], CWt[1][m], NCWt[m]] + Ut[pb]
                        kk._deps(e_pe, rd_, [pst[py]] if m == 0 else [])
                        e_pe.obj.matmul(ps[py][:], CW[0][m][:], U[pb][0][:], start=(m == 0), stop=False)
                        e_pe.obj.matmul(ps[py][:], NCW[m][:], U[pb][1][:], start=False, stop=False)
                        e_pe.obj.matmul(ps[py][:], CW[1][m][:], U[pb][2][:], start=False, stop=False)
                        inst = e_pe.obj.matmul(ps[py][:], CW[1][m][:], U[pb][3][:], start=False, stop=(m == 3))
                        e_pe.cnt += 1
                        inst.then_inc(e_pe.sem, 1)
                        kk._reg((e_pe.sem, e_pe.cnt), rd_, [pst[py]])
                        if m == 3:
                            yb_ = n % 2
                            kk.op("dve", lambda e: e.scalar_tensor_tensor(
                                out=yv[yb_][:], in0=ufm[:, k, cs_], scalar=dcol[:, k:k + 1], in1=ps[py][:],
                                op0=ALU.mult, op1=ALU.add), reads=[ut[k][n], pt, pst[py]], writes=[yvt[yb_]])
                            kk.op("act", lambda e: e.activation(out=ufm[:, k, cs_], in_=yv[yb_][:],
                                                                func=AF.Gelu_apprx_tanh),
                                  reads=[yvt[yb_]], writes=[ut[k][n]])

                    NI = len(items)
                    stA(0, *items[0])
                    for s_ in range(NI + 1):
                        if s_ + 1 < NI:
                            stA(s_ + 1, *items[s_ + 1])
                        if s_ < NI:
                            stA2(s_, *items[s_])
                            stB(s_, *items[s_])
                        if s_ >= 1:
                            stC(s_ - 1, *items[s_ - 1])
                            stD(s_ - 1, *items[s_ - 1])
                            stE(s_ - 1, *items[s_ - 1])
                    if S5STOP == "k0":
                        return
            kk.barrier()
            if S5STOP == "p2":
                return
            with ExitStack() as es:
                def A(name, shape, dtp):
                    return es.enter_context(nc.sbuf_tensor(f"{name}_{l}", shape, dtp))
                wg = A("s3_wg", [128, 8, 2048], BF16)
                wgt = kk.tok("s3_wg", dma=True)
                cg = self.cast_tok[("glu", l)]
                for k in range(8):
                    kk.dma("sp", wg[:, k, :], self.wglu_b[j5][k * 128:(k + 1) * 128, :], reads=[cg], writes=[wgt], st=wgt)
                hb = [A(f"s3_h{i}", [128, 8, TT], F32) for i in range(2)]
                hbt = [kk.tok(f"s3_h{i}", dma=True) for i in range(2)]
                hst = [kk.tok(f"s3_hs{i}", dma=True) for i in range(2)]
                yb = A("s3_y", [128, 8, TT], F32)
                ybt = [kk.tok(f"s3_y{c}") for c in range(8)]
                ysq = A("s3_ysq", [128, 8, TT], BF16)
                ysqt = [kk.tok(f"s3_ysq{c}") for c in range(8)]
                sg = [A(f"s3_sg{i}", [128, TT], F32) for i in range(2)]
                sgt = [kk.tok(f"s3_sg{i}") for i in range(2)]
                tmp = A("s3_tmp", [128, TT], F32)
                tmpt = kk.tok("s3_tmp")
                rstd = A("s3_rstd", [128, TT], F32)
                rstdt = kk.tok("s3_rstd")
                for i in range(NT):
                    b = i % 2
                    cs_ = slice(i * TT, (i + 1) * TT)
                    kk.dma("sp", hb[b][:], hsrc[:, cs_].rearrange("(k p) t -> p k t", p=128), writes=[hbt[b]], st=hbt[b])
                    for c in range(8):
                        pv, pg = (0, 1) if c % 2 == 0 else (2, 3)
                        kk.mm_group(ps[pv][:], [(wg[:, k, c * 128:(c + 1) * 128], ufm[:, k, cs_]) for k in range(8)],
                                    reads=[wgt] + [ut[k][i] for k in range(8)], writes=[pst[pv]])
                        kk.mm_group(ps[pg][:], [(wg[:, k, 1024 + c * 128:1024 + (c + 1) * 128], ufm[:, k, cs_])
                                                for k in range(8)],
                                    reads=[wgt] + [ut[k][i] for k in range(8)], writes=[pst[pg]])
                        s_ = c % 2
                        kk.op("act", lambda e: e.activation(out=sg[s_][:], in_=ps[pg][:], func=AF.Sigmoid),
                              reads=[pst[pg]], writes=[sgt[s_]])
                        kk.op("dve", lambda e: e.tensor_tensor(out=yb[:, c, :], in0=ps[pv][:], in1=sg[s_][:], op=ALU.mult),
                              reads=[pst[pv], sgt[s_]], writes=[ybt[c]])
                        kk.op("act", lambda e: e.activation(out=ysq[:, c, :], in_=yb[:, c, :], func=AF.Square),
                              reads=[ybt[c]], writes=[ysqt[c]])
                    kk.mm_group(ps[6][:], [(self.ones[:], ysq[:, c, :]) for c in range(8)],
                                reads=ysqt + [tc], writes=[pst[6]])
                    self.rstd_from_ss(ps[6][:], pst[6], tmp[:], tmpt, rstd[:], rstdt)
                    for c in range(8):
                        kk.op("dve", lambda e, c=c: e.scalar_tensor_tensor(
                            out=yb[:, c, :], in0=yb[:, c, :], scalar=self.gcol[:, g1 + c:g1 + c + 1],
                            in1=rstd[:], op0=ALU.mult, op1=ALU.mult),
                            reads=[rstdt, tc], writes=[ybt[c]])
                    kk.op("pool", lambda e: e.tensor_tensor(out=hb[b][:], in0=hb[b][:], in1=yb[:], op=ALU.add),
                          reads=ybt, writes=[hbt[b]])
                    kk.dma("sp", hdst[:, cs_].rearrange("(k p) t -> p k t", p=128), hb[b][:],
                           reads=[hbt[b]], writes=[hst[b]], st=hst[b])

    def norm_full(self, l, hsrc, gidx, ufm, ut, pfx):
        kk, nc = self.kk, self.nc
        TT = 512
        NT = L // TT
        ps, pst, tc = self.ps, self.pst, self.t_const
        with ExitStack() as es:
            def A(name, shape, dtp):
                return es.enter_context(nc.sbuf_tensor(f"{pfx}{name}_{l}", shape, dtp))
            hb = [A(f"h{i}", [128, 8, TT], F32) for i in range(2)]
            hbt = [kk.tok(f"{pfx}h{i}", dma=True) for i in range(2)]
            sq = [A(f"sq{i}", [128, 8, TT], BF16) for i in range(2)]
            sqt = [kk.tok(f"{pfx}sq{i}") for i in range(2)]
            tmp = A("tmp", [128, TT], F32)
            tmpt = kk.tok(f"{pfx}tmp")
            rstd = [A(f"rstd{i}", [128, TT], F32) for i in range(2)]
            rstdt = [kk.tok(f"{pfx}rstd{i}") for i in range(2)]
            for i in range(NT):
                b = i % 2
                kk.dma("sp", hb[b][:], hsrc[:, i * TT:(i + 1) * TT].rearrange("(k p) t -> p k t", p=128),
                       writes=[hbt[b]], st=hbt[b])
                kk.op("act", lambda e: e.activation(out=sq[b][:], in_=hb[b][:], func=AF.Square),
                      reads=[hbt[b]], writes=[sqt[b]])
                kk.mm_group(ps[6 + b][:], [(self.ones[:], sq[b][:, k, :]) for k in range(8)],
                            reads=[sqt[b], tc], writes=[pst[6 + b]])
                self.rstd_from_ss(ps[6 + b][:], pst[6 + b], tmp[:], tmpt, rstd[b][:], rstdt[b])
                for k in range(8):
                    kk.op("dve", lambda e, k=k: e.scalar_tensor_tensor(
                        out=ufm[:, k, i * TT:(i + 1) * TT], in0=hb[b][:, k, :],
                        scalar=self.gcol[:, gidx + k:gidx + k + 1], in1=rstd[b][:], op0=ALU.mult, op1=ALU.mult),
                        reads=[hbt[b], rstdt[b], tc], writes=[ut[k][i]])
        kk.barrier()

    def fox_phase(self, l, hsrc, hdst):
        kk, nc = self.kk, self.nc
        jf = l // 2
        TT = 512
        NT = L // TT
        g0 = (l * 4 + 0) * 8
        g1 = (l * 4 + 1) * 8
        ps, pst, tc = self.ps, self.pst, self.t_const
        o_d = self.o_d
        odt = kk.tok("o_d")
        with ExitStack() as es0:
            ufm = es0.enter_context(nc.sbuf_tensor(f"f_u_{l}", [128, 8, L], BF16))
            ut = [[kk.tok(f"f_u{k}_{n}") for n in range(NT)] for k in range(8)]
            uall = [ut[k][n] for k in range(8) for n in range(NT)]
            self.norm_full(l, hsrc, g0, ufm, ut, "f1_")
            negcT = es0.enter_context(nc.sbuf_tensor(f"f_negcT_{l}", [128, 32, 16], F32))
            negct = kk.tok("f_negcT")
            cq_t = kk.tok("f_cq", dma=True)
            with ExitStack() as es:
                def A(name, shape, dtp):
                    return es.enter_context(nc.sbuf_tensor(f"{name}_{l}", shape, dtp))
                wf = A("f2_wf", [128, 8, 16], BF16)
                wft = kk.tok("f2_wf", dma=True)
                kk.dma("sp", wf[:], self.wf_b[jf].rearrange("(k p) n -> p k n", p=128), reads=[self.cast_tok[("wf", l)]],
                       writes=[wft], st=wft)
                bf_ = A("f2_bf", [16, 1], F32)
                nbf = A("f2_nbf", [16, 1], F32)
                bft = kk.tok("f2_bf", dma=True)
                kk.dma("sp", bf_[:], self.bf_d[jf], writes=[bft], st=bft)
                kk.op("dve", lambda e: e.tensor_scalar(out=nbf[:], in0=bf_[:], scalar1=-1.0, scalar2=None, op0=ALU.mult),
                      reads=[bft], writes=[bft])
                cum = A("f2_cum", [16, L], F32)
                cumt = kk.tok("f2_cum")
                ex = [A(f"f2_ex{i}", [16, TT], F32) for i in range(2)]
                ext = [kk.tok(f"f2_ex{i}") for i in range(2)]
                one16 = A("f2_one16", [16, TT], F32)
                kk.op("pool", lambda e: e.memset(one16[:], 1.0), writes=[tc])
                zero1 = A("f2_zero", [16, 1], F32)
                kk.op("pool", lambda e: e.memset(zero1[:], 0.0), writes=[tc])
                for n in range(NT):
                    cs_ = slice(n * TT, (n + 1) * TT)
                    b = n % 2
                    kk.mm_group(ps[b][0:16, :], [(wf[:, k, :], ufm[:, k, cs_]) for k in range(8)],
                                reads=[wft] + [ut[k][n] for k in range(8)], writes=[pst[b]])
                    kk.op("act", lambda e: e.activation(out=ex[b][:], in_=ps[b][0:16, :], func=AF.Exp, scale=-1.0,
                                                        bias=nbf[:]), reads=[pst[b], bft], writes=[ext[b]])
                    kk.op("act", lambda e: e.activation(out=ex[b][:], in_=ex[b][:], func=AF.Ln, bias=self.onec[0:16, :]),
                          reads=[ext[b], tc], writes=[ext[b]])
                    kk.op("dve", lambda e: e.tensor_scalar(out=ex[b][:], in0=ex[b][:], scalar1=-1.0, scalar2=None,
                                                           op0=ALU.mult), reads=[ext[b]], writes=[ext[b]])
                    kk.op("dve", lambda e: e.tensor_tensor_scan(
                        out=cum[:, cs_], data0=one16[:], data1=ex[b][:],
                        initial=(zero1[:] if n == 0 else cum[:, n * TT - 1:n * TT]), op0=ALU.mult, op1=ALU.add),
                        reads=[ext[b], tc, cumt], writes=[cumt])
                for tb in range(32):
                    kk.op("pe", lambda e, tb=tb: e.transpose(out=ps[6][:, tb * 16:(tb + 1) * 16],
                                                             in_=cum[:, tb * 128:(tb + 1) * 128],
                                                             identity=self.ident[0:16, 0:16]),
                          reads=[cumt, tc], writes=[pst[6]])
                kk.op("dve", lambda e: e.tensor_scalar(out=negcT[:].rearrange("p a b -> p (a b)"), in0=ps[6][:],
                                                       scalar1=-1.0, scalar2=None, op0=ALU.mult),
                      reads=[pst[6]], writes=[negct])
                c8 = A("f2_c8", [16, L], F32)
                c8t = kk.tok("f2_c8")
                cp = [A(f"f2_cp{i}", [16, L], BF16) for i in range(3)]
                cpt = [kk.tok(f"f2_cp{i}", dma=True) for i in range(3)]
                kk.op("dve", lambda e: e.tensor_scalar(out=c8[:], in0=cum[:], scalar1=8.0, scalar2=None, op0=ALU.mult),
                      reads=[cumt], writes=[c8t])
                for i in range(3):
                    kk.op("dve", lambda e, i=i: e.tensor_copy(out=cp[i][:], in_=c8[:]), reads=[c8t], writes=[cpt[i]])
                    if i < 2:
                        kk.op("dve", lambda e, i=i: e.tensor_tensor(out=c8[:], in0=c8[:], in1=cp[i][:], op=ALU.subtract),
                              reads=[cpt[i]], writes=[c8t])
                    kk.dma("sp", self.cumq[:, i, :], cp[i][:], reads=[cpt[i]], writes=[cq_t], st=cpt[i])
            kk.barrier()
            with ExitStack() as es:
                def A(name, shape, dtp):
                    return es.enter_context(nc.sbuf_tensor(f"{name}_{l}", shape, dtp))
                qa = [A(f"f3_q{i}", [128, L], BF16) for i in range(2)]
                ka = [A(f"f3_k{i}", [128, L], BF16) for i in range(2)]
                qat = [kk.tok(f"f3_q{i}", dma=True) for i in range(2)]
                kat = [kk.tok(f"f3_k{i}") for i in range(2)]
                va = [A(f"f3_v{i}", [128, 32, 65], BF16) for i in range(2)]
                vat = [kk.tok(f"f3_v{i}") for i in range(2)]
                wh = [A(f"f3_w{i}", [128, 8, 512], BF16) for i in range(2)]
                wht = [kk.tok(f"f3_w{i}", dma=True) for i in range(2)]
                P = [A(f"f3_P{i}", [128, TT], BF16) for i in range(4)]
                Pt = [kk.tok(f"f3_P{i}") for i in range(4)]
                eg = A("f3_eg", [64, TT], F32)
                egt = kk.tok("f3_eg")
                rs = A("f3_rs", [128, TT], F32)
                rst = kk.tok("f3_rs")
                den = A("f3_den", [64, TT], F32)
                dent = kk.tok("f3_den")
                osb = [A(f"f3_o{i}", [64, TT], BF16) for i in range(2)]
                osbt = [kk.tok(f"f3_o{i}", dma=True) for i in range(2)]
                tri = A("f3_tri", [128, 128], BF16)
                trif = A("f3_trif", [128, 128], F32)
                trit = kk.tok("f3_tri", dma=True)
                onesf = A("f3_onesf", [128, 64], F32)
                kk.op("pool", lambda e: e.memset(onesf[:], 1.0), writes=[tc])
                kk.dma("sp", trif[:], self.tri_d, writes=[trit], st=trit)
                kk.op("dve", lambda e: e.tensor_copy(out=tri[:], in_=trif[:]), reads=[trit], writes=[trit])
                for i in range(2):
                    kk.op("pool", lambda e, i=i: e.memset(ka[i][64:96, :], 1.0), writes=[kat[i]])
                    kk.op("pool", lambda e, i=i: e.memset(va[i][:, :, 64:65], 1.0), writes=[vat[i]])
                cw = self.cast_tok[("win", l)]
                pcount = 0
                eg2 = [eg, A("f3_egB", [64, TT], F32)]
                egt2 = [egt, kk.tok("f3_egB")]
                for jp in range(8):
                    w_, wt_ = wh[jp % 2], wht[jp % 2]
                    kk.dma("sp", w_[:], self.win_b[jf, jp].rearrange("(k p) n -> p k n", p=128), reads=[cw],
                           writes=[wt_], st=wt_)
                    for hi in range(2):
                        kk.dma("sp", qa[hi][64:67, :], self.cumq[2 * jp + hi], reads=[cq_t], writes=[qat[hi]], st=qat[hi])
                    for n in range(NT):
                        cs_ = slice(n * TT, (n + 1) * TT)
                        un = [ut[k][n] for k in range(8)]
                        kk.mm_group(ps[3][:, :], [(w_[:, k, 0:128], ufm[:, k, cs_]) for k in range(8)],
                                    reads=[wt_] + un, writes=[pst[3]])
                        for hi in range(2):
                            kk.op("act", lambda e, hi=hi: e.activation(out=qa[hi][0:64, cs_],
                                                                       in_=ps[3][64 * hi:64 * hi + 64, :], func=AF.Copy),
                                  reads=[pst[3]], writes=[qat[hi]])
                        kk.mm_group(ps[7][:, :], [(w_[:, k, 128:256], ufm[:, k, cs_]) for k in range(8)],
                                    reads=[wt_] + un, writes=[pst[7]])
                        for hi in range(2):
                            kk.op("dve", lambda e, hi=hi: e.tensor_copy(out=ka[hi][0:64, cs_],
                                                                        in_=ps[7][64 * hi:64 * hi + 64, :]),
                                  reads=[pst[7]], writes=[kat[hi]])
                    for tg in range(8):
                        for t4 in range(4):
                            tb = tg * 4 + t4
                            e_pe = kk.engs["pe"]
                            rd = [wt_] + [ut[k][tb // 4] for k in range(8)]
                            kk._deps(e_pe, rd, [pst[7]] if t4 == 0 else [])
                            inst = None
                            for k in range(8):
                                inst = e_pe.obj.matmul(ps[7][:, t4 * 128:(t4 + 1) * 128], ufm[:, k, tb * 128:(tb + 1) * 128],
                                                       w_[:, k, 256:384], start=(k == 0), stop=(k == 7))
                            e_pe.cnt += 1
                            inst.then_inc(e_pe.sem, 1)
                            kk._reg((e_pe.sem, e_pe.cnt), rd, [pst[7]])
                        for hi in range(2):
                            kk.op("dve", lambda e, hi=hi: e.tensor_copy(
                                out=va[hi][:, tg * 4:(tg + 1) * 4, 0:64],
                                in_=ps[7][:].rearrange("p (a b) -> p a b", b=128)[:, :, 64 * hi:64 * hi + 64]),
                                reads=[pst[7]], writes=[vat[hi]])
                    for qc in range(NT):
                        qs = slice(qc * TT, (qc + 1) * TT)
                        un = [ut[k][qc] for k in range(8)]
                        kk.mm_group(ps[3][:, :], [(w_[:, k, 384:512], ufm[:, k, qs]) for k in range(8)],
                                    reads=[wt_] + un, writes=[pst[3]])
                        for hi in range(2):
                            kk.op("act", lambda e, hi=hi: e.activation(out=eg2[hi][:], in_=ps[3][64 * hi:64 * hi + 64, :],
                                                                       func=AF.Exp, scale=-1.0),
                                  reads=[pst[3]], writes=[egt2[hi]])
                        for hi in range(2):
                            h = 2 * jp + hi
                            hb_ = hi
                            q_, k_, v_ = qa[hi], ka[hi], va[hi]
                            po = 4 + hi
                            nkt = 4 * qc + 4
                            slots = {}

                            def issue_S(kt):
                                nonlocal pcount
                                j = kt - 4 * qc
                                c0 = max(0, j) * 128
                                pb = pcount % 3
                                pp = pcount % 4
                                pcount += 1
                                slots[kt] = (c0, pp)
                                kk.mm_group(ps[pb][:, c0:TT],
                                            [(k_[0:67, kt * 128:(kt + 1) * 128], q_[0:67, qc * TT + c0:(qc + 1) * TT])],
                                            reads=[kat[hb_], qat[hb_]], writes=[pst[pb]])
                                kk.op("act", lambda e: e.activation(out=P[pp][:, c0:TT], in_=ps[pb][:, c0:TT], func=AF.Exp,
                                                                    scale=0.125, bias=negcT[:, kt, h:h + 1]),
                                      reads=[pst[pb], negct], writes=[Pt[pp]])
                                if j >= 0:
                                    kk.op("pool", lambda e: e.tensor_tensor(out=P[pp][:, c0:c0 + 128],
                                                                            in0=P[pp][:, c0:c0 + 128],
                                                                            in1=tri[:], op=ALU.mult),
                                          reads=[trit], writes=[Pt[pp]])
                            issue_S(0)
                            if nkt > 1:
                                issue_S(1)
                            for kt in range(nkt):
                                if kt + 2 < nkt:
                                    issue_S(kt + 2)
                                c0, pp = slots[kt]
                                e_pe = kk.engs["pe"]
                                kk._deps(e_pe, [vat[hb_], Pt[pp]], [pst[po]] if kt == 0 else [])
                                inst = e_pe.obj.matmul(ps[po][0:65, c0:TT], v_[:, kt, 0:65], P[pp][:, c0:TT],
                                                       start=(kt == 0), stop=(kt == nkt - 1))
                                e_pe.cnt += 1
                                inst.then_inc(e_pe.sem, 1)
                                kk._reg((e_pe.sem, e_pe.cnt), [vat[hb_], Pt[pp]], [pst[po]])
                            ob = hi
                            kk.op("dve", lambda e: e.tensor_copy(out=rs[64:65, :], in_=ps[po][64:65, :]),
                                  reads=[pst[po]], writes=[rst])
                            kk.mm_group(ps[6][0:64, :], [(onesf[64:65, 0:64], rs[64:65, :])], reads=[rst, tc],
                                        writes=[pst[6]])
                            kk.op("dve", lambda e: e.scalar_tensor_tensor(out=den[:], in0=eg2[hi][:], scalar=1.0,
                                                                          in1=ps[6][0:64, :], op0=ALU.add, op1=ALU.mult),
                                  reads=[egt2[hi], pst[6]], writes=[dent])
                            kk.op("dve", lambda e: e.reciprocal(out=den[:], in_=den[:]), reads=[dent], writes=[dent])
                            kk.op("dve", lambda e: e.tensor_tensor(out=osb[ob][:], in0=ps[po][0:64, :], in1=den[:],
                                                                   op=ALU.mult),
                                  reads=[pst[po], dent], writes=[osbt[ob]])
                            kk.dma("sp", o_d[h * 64:(h + 1) * 64, qs], osb[ob][:], reads=[osbt[ob]], writes=[odt],
                                   st=osbt[ob])
        kk.barrier()
        with ExitStack() as es:
            def A(name, shape, dtp):
                return es.enter_context(nc.sbuf_tensor(f"{name}_{l}", shape, dtp))
            wo = A("f4_wo", [128, 8, 1024], BF16)
            wot = kk.tok("f4_wo", dma=True)
            kk.dma("sp", wo[:], self.wout_b[jf].rearrange("(k p) n -> p k n", p=128), reads=[self.cast_tok[("wout", l)]],
                   writes=[wot], st=wot)
            ob = [A(f"f4_o{i}", [128, 8, TT], BF16) for i in range(2)]
            obt = [kk.tok(f"f4_o{i}", dma=True) for i in range(2)]
            hb = [A(f"f4_h{i}", [128, 8, TT], F32) for i in range(2)]
            hbt = [kk.tok(f"f4_h{i}", dma=True) for i in range(2)]
            hst = [kk.tok(f"f4_hs{i}", dma=True) for i in range(2)]
            yb = A("f4_y", [128, 8, TT], F32)
            ybt = [kk.tok(f"f4_y{c}") for c in range(8)]
            ysq = A("f4_ysq", [128, 8, TT], BF16)
            ysqt = [kk.tok(f"f4_ysq{c}") for c in range(8)]
            tmp = A("f4_tmp", [128, TT], F32)
            tmpt = kk.tok("f4_tmp")
            rstd = A("f4_rstd", [128, TT], F32)
            rstdt = kk.tok("f4_rstd")
            for i in range(NT):
                b = i % 2
                cs_ = slice(i * TT, (i + 1) * TT)
                kk.dma("sp", hb[b][:], hsrc[:, cs_].rearrange("(k p) t -> p k t", p=128), writes=[hbt[b]], st=hbt[b])
                kk.dma("sp", ob[b][:], o_d[:, cs_].rearrange("(k p) t -> p k t", p=128), reads=[odt], writes=[obt[b]],
                       st=obt[b])
                for c in range(8):
                    p = c % 2
                    kk.mm_group(ps[p][:], [(wo[:, k, c * 128:(c + 1) * 128], ob[b][:, k, :]) for k in range(8)],
                                reads=[wot, obt[b]], writes=[pst[p]])
                    kk.op("dve", lambda e: e.tensor_copy(out=yb[:, c, :], in_=ps[p][:]), reads=[pst[p]], writes=[ybt[c]])
                    kk.op("act", lambda e: e.activation(out=ysq[:, c, :], in_=yb[:, c, :], func=AF.Square),
                          reads=[ybt[c]], writes=[ysqt[c]])
                kk.mm_group(ps[6][:], [(self.ones[:], ysq[:, c, :]) for c in range(8)], reads=ysqt + [tc], writes=[pst[6]])
                self.rstd_from_ss(ps[6][:], pst[6], tmp[:], tmpt, rstd[:], rstdt)
                for c in range(8):
                    kk.op("dve", lambda e, c=c: e.scalar_tensor_tensor(
                        out=yb[:, c, :], in0=yb[:, c, :], scalar=self.gcol[:, g1 + c:g1 + c + 1],
                        in1=rstd[:], op0=ALU.mult, op1=ALU.mult), reads=[rstdt, tc], writes=[ybt[c]])
                kk.op("pool", lambda e: e.tensor_tensor(out=hb[b][:], in0=hb[b][:], in1=yb[:], op=ALU.add),
                      reads=ybt, writes=[hbt[b]])
                kk.dma("sp", hdst[:, cs_].rearrange("(k p) t -> p k t", p=128), hb[b][:],
                       reads=[hbt[b]], writes=[hst[b]], st=hst[b])

    def mlp_phase(self, l, hsrc, hdst):
        kk, nc = self.kk, self.nc
        TT = 512
        NT = L // TT
        g2 = (l * 4 + 2) * 8
        g3 = (l * 4 + 3) * 8
        with ExitStack() as es:
            def A(name, shape, dtp):
                return es.enter_context(nc.sbuf_tensor(f"{name}_{l}", shape, dtp))
            hb = [A(f"m_h{i}", [128, 8, TT], F32) for i in range(2)]
            hbt = [kk.tok(f"m_h{i}", dma=True) for i in range(2)]
            hst = [kk.tok(f"m_hs{i}", dma=True) for i in range(2)]
            sq = A("m_sq", [128, 8, TT], BF16)
            sqt = kk.tok("m_sq")
            ub = [A(f"m_u{i}", [128, 8, TT], BF16) for i in range(2)]
            ubt = [kk.tok(f"m_u{i}") for i in range(2)]
            hid = A("m_hid", [128, 32, TT], BF16)
            hidt = [kk.tok(f"m_hid{j}") for j in range(32)]
            rr = [A(f"m_r{i}", [128, TT], F32) for i in range(2)]
            rrt = [kk.tok(f"m_r{i}") for i in range(2)]
            w1b = [A(f"m_w1_{i}", [128, 8, 512], BF16) for i in range(4)]
            w1t = [kk.tok(f"m_w1_{i}", dma=True) for i in range(4)]
            w2b = [A(f"m_w2_{i}", [128, 32, 128], BF16) for i in range(4)]
            w2t = [kk.tok(f"m_w2_{i}", dma=True) for i in range(4)]
            yb = A("m_y", [128, 8, TT], F32)
            ybt = [kk.tok(f"m_y{c}") for c in range(8)]
            ysq = A("m_ysq", [128, 8, TT], BF16)
            ysqt = [kk.tok(f"m_ysq{c}") for c in range(8)]
            tmp = A("m_tmp", [128, TT], F32)
            tmpt = kk.tok("m_tmp")
            rstd = [A(f"m_rstd{i}", [128, TT], F32) for i in range(2)]
            rstdt = [kk.tok(f"m_rstd{i}") for i in range(2)]

            w1v = self.w1_b[l].rearrange("(r a) c -> r (a c)", a=2)
            w2v = self.w2_b[l]
            c1, c2 = self.cast_tok[("w1", l)], self.cast_tok[("w2", l)]

            def mk1(q):
                def ld(buf, tk):
                    kk.dma("sp", buf[:], w1v[:, q * 512:(q + 1) * 512].rearrange("(k p) n -> p k n", p=128),
                           reads=[c1], writes=[tk], st=tk)
                return ld

            def mk2(c):
                def ld(buf, tk):
                    kk.dma("sp", buf[:].rearrange("p j n -> p (j n)"), w2v[c], reads=[c2], writes=[tk], st=tk)
                return ld
            s1 = Stream(kk, w1b, w1t, [mk1(q) for _ in range(NT) for q in range(8)])
            s2 = Stream(kk, w2b, w2t, [mk2(c) for _ in range(NT) for c in range(8)])
            PS_S, PS_U, PS_D = 0, (1, 2, 3), (4, 5)
            ps, pst = self.ps, self.pst

            def load_h(i):
                b = i % 2
                kk.dma("sp", hb[b][:], hsrc[:, i * TT:(i + 1) * TT].rearrange("(k p) t -> p k t", p=128),
                       writes=[hbt[b]], st=hbt[b])

            def norm_in(i):
                b = i % 2
                kk.op("act", lambda e: e.activation(out=sq[:], in_=hb[b][:], func=AF.Square),
                      reads=[hbt[b]], writes=[sqt])
                kk.mm_group(ps[PS_S][:], [(self.ones[:], sq[:, k, :]) for k in range(8)],
                            reads=[sqt, self.t_const], writes=[pst[PS_S]])
                self.rstd_from_ss(ps[PS_S][:], pst[PS_S], tmp[:], tmpt, rstd[0][:], rstdt[0])
                for k in range(8):
                    kk.op("dve", lambda e, k=k: e.scalar_tensor_tensor(
                        out=ub[b][:, k, :], in0=hb[b][:, k, :], scalar=self.gcol[:, g2 + k:g2 + k + 1],
                        in1=rstd[0][:], op0=ALU.mult, op1=ALU.mult),
                        reads=[hbt[b], rstdt[0], self.t_const], writes=[ubt[b]])

            def up(i):
                b = i % 2
                if i + 1 < NT:
                    load_h(i + 1)
                s2.get(i * 8)
                for j in range(32):
                    q = i * 8 + j // 4
                    wb, wt = s1.get(q)
                    jj = j % 4
                    p = PS_U[j % 3]
                    kk.mm_group(ps[p][:], [(wb[:, k, jj * 128:(jj + 1) * 128], ub[b][:, k, :]) for k in range(8)],
                                reads=[wt, ubt[b]], writes=[pst[p]])
                    r = j % 2
                    kk.op("act", lambda e, p=p, r=r: e.activation(out=rr[r][:], in_=ps[p][:], func=AF.Relu),
                          reads=[pst[p]], writes=[rrt[r]])
                    kk.op("pool", lambda e, r=r, j=j: e.tensor_tensor(out=hid[:, j, :], in0=rr[r][:], in1=rr[r][:],
                                                                     op=ALU.mult),
                          reads=[rrt[r]], writes=[hidt[j]])

            def down(i):
                b = i % 2
                for c in range(8):
                    wb, wt = s2.get(i * 8 + c)
                    p = PS_D[c % 2]
                    kk.mm_group(ps[p][:], [(wb[:, j, :], hid[:, j, :]) for j in range(32)],
                                reads=[wt] + hidt, writes=[pst[p]])
                    kk.op("dve", lambda e, p=p, c=c: e.tensor_copy(out=yb[:, c, :], in_=ps[p][:]),
                          reads=[pst[p]], writes=[ybt[c]])
                    kk.op("act", lambda e, c=c: e.activation(out=ysq[:, c, :], in_=yb[:, c, :], func=AF.Square),
                          reads=[ybt[c]], writes=[ysqt[c]])
                import os
                stp = os.environ.get("KSTOP", "")
                if stp == "down0a":
                    return
                kk.mm_group(ps[PS_S][:], [(self.ones[:], ysq[:, c, :]) for c in range(8)],
                            reads=ysqt + [self.t_const], writes=[pst[PS_S]])
                self.rstd_from_ss(ps[PS_S][:], pst[PS_S], tmp[:], tmpt, rstd[1][:], rstdt[1])
                for c in range(8):
                    kk.op("dve", lambda e, c=c: e.scalar_tensor_tensor(
                        out=yb[:, c, :], in0=yb[:, c, :], scalar=self.gcol[:, g3 + c:g3 + c + 1],
                        in1=rstd[1][:], op0=ALU.mult, op1=ALU.mult),
                        reads=[rstdt[1], self.t_const], writes=[ybt[c]])
                if stp == "down0b":
                    return
                kk.op("pool", lambda e: e.tensor_tensor(out=hb[b][:], in0=hb[b][:], in1=yb[:], op=ALU.add),
                      reads=ybt, writes=[hbt[b]])
                if stp == "down0c":
                    return
                kk.dma("sp", hdst[:, i * TT:(i + 1) * TT].rearrange("(k p) t -> p k t", p=128), hb[b][:],
                       reads=[hbt[b]], writes=[hst[b]], st=hst[b])

            import os
            stop = os.environ.get("KSTOP", "")
            if stop == "cast":
                return
            load_h(0)
            norm_in(0)
            if stop == "norm0":
                return
            for i in range(NT):
                up(i)
                if stop == "up0":
                    return
                if i + 1 < NT and stop != "down0x":
                    norm_in(i + 1)
                if stop == "norm1":
                    return
                down(i)
                if stop.startswith("down0"):
                    return


FULL_PHASES = [(("s5" if l % 2 == 0 else "fox") if w == 0 else "mlp", l) for l in range(DEPTH) for w in range(2)]


def prep_inputs(inputs, b):
    m = {}
    m["x"] = np.ascontiguousarray(inputs["x"][b].T)
    g = np.asarray(inputs["norm_gains"], np.float32)
    m["gcol"] = np.ascontiguousarray(g.reshape(DEPTH * 4, 8, 128).transpose(2, 0, 1).reshape(128, DEPTH * 4 * 8))
    m["w1"] = np.ascontiguousarray(np.asarray(inputs["mlp_w1"], np.float32).reshape(DEPTH, -1, 2048))
    w2 = np.asarray(inputs["mlp_w2"], np.float32).reshape(DEPTH, 32, 128, 8, 128)
    m["w2"] = np.ascontiguousarray(w2.transpose(0, 3, 2, 1, 4).reshape(DEPTH, -1, 2048))
    NS = 2
    are = np.asarray(inputs["s5_a_re"], np.float32); aim = np.asarray(inputs["s5_a_im"], np.float32)
    ldt = np.asarray(inputs["s5_log_dt"], np.float32)
    def pairT(a):
        return np.ascontiguousarray(a.reshape(NS, 32, 2, 64).transpose(0, 2, 3, 1).reshape(NS, 128, 32))
    m["s5_at_re"] = pairT(are)
    m["s5_at_im"] = pairT(aim)
    m["s5_ldt"] = pairT(np.broadcast_to(ldt[:, :, None], (NS, 64, 64)))
    def pairB(b_):
        return np.ascontiguousarray(b_.reshape(NS, 32, 2, 64, 16).transpose(0, 2, 3, 1, 4).reshape(NS, 128, 512))
    m["s5_bre"] = pairB(np.asarray(inputs["s5_b_re"], np.float32))
    m["s5_bim"] = pairB(np.asarray(inputs["s5_b_im"], np.float32))
    m["s5_cre"] = pairB(np.asarray(inputs["s5_c_re"], np.float32).transpose(0, 1, 3, 2))
    m["s5_cim"] = pairB(np.asarray(inputs["s5_c_im"], np.float32).transpose(0, 1, 3, 2))
    m["s5_dcol"] = np.ascontiguousarray(np.asarray(inputs["s5_d"], np.float32).reshape(NS, 8, 128).transpose(0, 2, 1))
    m["wglu"] = np.ascontiguousarray(np.asarray(inputs["s5_w_glu"], np.float32))
    m["ident"] = np.eye(128, dtype=np.float32)
    NF = 2
    win = np.asarray(inputs["fox_w_in"], np.float32)
    m["win"] = np.ascontiguousarray(win[:, :, :4096].reshape(NF, 1024, 4, 8, 2, 64).transpose(0, 3, 1, 2, 4, 5).reshape(NF, 8, 1024, 512))
    m["wf"] = np.ascontiguousarray(win[:, :, 4096:4112])
    m["bf"] = np.ascontiguousarray(np.asarray(inputs["fox_b_f"], np.float32).reshape(NF, 16, 1))
    m["wout"] = np.ascontiguousarray(np.asarray(inputs["fox_w_out"], np.float32))
    m["tri"] = np.triu(np.ones((128, 128), np.float32))
    return m


def run(inputs, phases=None, cores=NCORES, trace=False):
    prog = Prog(phases or FULL_PHASES)
    nc = prog.build()
    shared = None
    in_maps = []
    for b in range(cores):
        m = prep_inputs(inputs, b)
        if shared is None:
            shared = m
        else:
            for k_ in m:
                if k_ != "x":
                    m[k_] = shared[k_]
        in_maps.append(m)
    res = run_bass_kernel_spmd(nc, in_maps, core_ids=list(range(cores)), trace=trace)
    outs = [np.ascontiguousarray(r["out"].T) for r in res.results]
    return np.stack(outs, 0), res


def kernel(**inputs):
    out, _ = run(inputs)
    return out.astype(np.float32)
```
